# Optimizing a Trainium2 kernel written in Bass

```python
import math
import jax, jax.numpy as jnp
from jax import lax
import numpy as np

D_MODEL = 4096
BATCH = 1
SEQ = 8192
DEPTH = 1
DEC_BATCH = 128
DEC_SEQ = 4
PAST_LEN = 8192
PAGE_SIZE = 128

MIX_WIDTH = D_MODEL
GLA_WIDTH = MIX_WIDTH // 2
SWA_WIDTH = MIX_WIDTH - GLA_WIDTH
GLA_HEADS = 4
GLA_DV = GLA_WIDTH // GLA_HEADS
GLA_DK = GLA_DV // 2
GLA_QK = GLA_HEADS * GLA_DK
GLA_LR = 16
GLA_TAU = 16.0
GLA_CHUNK = 64
HEAD_DIM = 128
SWA_HEADS = SWA_WIDTH // HEAD_DIM
SWA_KV_HEADS = 4
SWA_GROUP = SWA_HEADS // SWA_KV_HEADS
SWA_Q = SWA_HEADS * HEAD_DIM
SWA_KV = SWA_KV_HEADS * HEAD_DIM
WINDOW = 128
D_FF = 256 * ((8 * D_MODEL // 3 + 255) // 256)
CONV_W = 3
NORM_EPS = 1e-6
NEG_INF = -1e30
IN_SIZES = (GLA_QK, GLA_QK, GLA_WIDTH, GLA_WIDTH, GLA_LR, SWA_Q, SWA_KV, SWA_KV)
IN_WIDTH = sum(IN_SIZES)

kernel_name = "hymba_gla_swa_sink_convffn_adaln_step"


def rmsnorm(x, g):
    xf = x.astype(jnp.float32)
    y = xf * lax.rsqrt(jnp.mean(xf * xf, axis=-1, keepdims=True) + NORM_EPS)
    return (y * g.astype(jnp.float32)).astype(x.dtype)


def alibi_slopes():
    return jnp.exp2(-8.0 * jnp.arange(1, SWA_HEADS + 1, dtype=jnp.float32) / SWA_HEADS)


def gla_chunked(q, k, v, log_a, s0):
    B, L, H, dk = q.shape
    dv = v.shape[-1]
    C = math.gcd(L, GLA_CHUNK)
    NC = L // C
    f32 = jnp.float32
    q = q.astype(f32).reshape(B, NC, C, H, dk)
    k = k.astype(f32).reshape(B, NC, C, H, dk)
    v = v.astype(f32).reshape(B, NC, C, H, dv)
    b = jnp.cumsum(log_a.astype(f32).reshape(B, NC, C, H, dk), axis=2)
    qg = q * jnp.exp(b)
    kg = k * jnp.exp(-b)
    causal = jnp.tril(jnp.ones((C, C), dtype=bool))
    A = jnp.einsum('bnthk,bnshk->bnhts', qg, kg)
    A = jnp.where(causal, A, 0.0)
    o_intra = jnp.einsum('bnhts,bnshv->bnthv', A, v)
    b_last = b[:, :, -1]
    kd = k * jnp.exp(b_last[:, :, None] - b)
    dS = jnp.einsum('bnshk,bnshv->bnhkv', kd, v)
    decay = jnp.exp(b_last)

    def step(S, xs):
        qg_n, dS_n, dec_n = xs
        o_n = jnp.einsum('bthk,bhkv->bthv', qg_n, S)
        S = dec_n[..., None] * S + dS_n
        return S, o_n

    xs = (jnp.moveaxis(qg, 1, 0), jnp.moveaxis(dS, 1, 0), jnp.moveaxis(decay, 1, 0))
    s_final, o_inter = lax.scan(step, s0.astype(f32), xs)
    o = o_intra + jnp.moveaxis(o_inter, 0, 1)
    return o.reshape(B, L, H, dv), s_final.astype(s0.dtype)


def sink_attention(q, k, v, dist, valid, sinks):
    slopes = alibi_slopes().reshape(SWA_KV_HEADS, SWA_GROUP)
    s = jnp.einsum('bnqhgd,bnkhd->bnhgqk', q, k,
                   preferred_element_type=jnp.float32) * (HEAD_DIM ** -0.5)
    s = s - slopes[None, None, :, :, None, None] * dist[None, :, None, None]
    s = jnp.where(valid[None, :, None, None], s, NEG_INF)
    sink = jnp.broadcast_to(
        sinks.astype(jnp.float32).reshape(SWA_KV_HEADS, SWA_GROUP)[None, None, :, :, None, None],
        s.shape[:-1] + (1,))
    p = jax.nn.softmax(jnp.concatenate([s, sink], axis=-1), axis=-1)[..., :-1]
    return jnp.einsum('bnhgqk,bnkhd->bnqhgd', p.astype(v.dtype), v)


def swa_prompt(q, k, v, sinks):
    B, L = q.shape[:2]
    W = WINDOW
    NB = L // W
    qb = q.reshape(B, NB, W, SWA_KV_HEADS, SWA_GROUP, HEAD_DIM)
    pad = jnp.zeros((B, W, SWA_KV_HEADS, HEAD_DIM), k.dtype)
    def band(t):
        prev = jnp.concatenate([pad, t], axis=1)[:, :L].reshape(B, NB, W, SWA_KV_HEADS, HEAD_DIM)
        cur = t.reshape(B, NB, W, SWA_KV_HEADS, HEAD_DIM)
        return jnp.concatenate([prev, cur], axis=2)
    kb, vb = band(k), band(v)
    blk = jnp.arange(NB)[:, None, None]
    q_pos = blk * W + jnp.arange(W)[None, :, None]
    k_pos = (blk - 1) * W + jnp.arange(2 * W)[None, None, :]
    d = q_pos - k_pos
    valid = (d >= 0) & (d <= WINDOW) & (k_pos >= 0)
    o = sink_attention(qb, kb, vb, d.astype(jnp.float32), valid, sinks)
    return o.reshape(B, L, SWA_KV_HEADS, SWA_GROUP, HEAD_DIM)


def swa_sample(q, k, v, k_buf, v_buf, sinks):
    L = q.shape[1]
    wb = k_buf.shape[1]
    kc = jnp.concatenate([k_buf.astype(k.dtype), k], axis=1)
    vc = jnp.concatenate([v_buf.astype(v.dtype), v], axis=1)
    k_pos = PAST_LEN - wb + jnp.arange(wb + L)
    q_pos = PAST_LEN + jnp.arange(L)
    d = (q_pos[:, None] - k_pos[None, :])[None]
    valid = (d >= 0) & (d <= WINDOW) & (k_pos[None, None, :] >= 0)
    o = sink_attention(q[:, None], kc[:, None], vc[:, None], d.astype(jnp.float32), valid, sinks)
    return o[:, 0], kc[:, -wb:], vc[:, -wb:]


def trunk_layer(x, c, lp, s0, kv_buf, conv_buf):
    (w_ada, b_ada, g_norm, w_in, w_a_up, b_a, g_gla, swa_sinks,
     w_o, w_up, w_conv, b_conv, w_down) = lp
    B, L, _ = x.shape
    mod = jax.nn.silu(c) @ w_ada + b_ada
    sh1, sc1, gt1, sh2, sc2, gt2 = jnp.split(mod, 6, axis=-1)

    h = rmsnorm(x, g_norm[0]) * (1.0 + sc1[:, None]) + sh1[:, None]
    proj = h @ w_in
    offs = []
    acc = 0
    for sz in IN_SIZES[:-1]:
        acc += sz
        offs.append(acc)
    gq, gk, gv, gr, ga, sq, sk, sv = jnp.split(proj, offs, axis=-1)

    log_a = jax.nn.log_sigmoid((ga @ w_a_up + b_a).astype(jnp.float32)) / GLA_TAU
    o_gla, s_new = gla_chunked(
        gq.reshape(B, L, GLA_HEADS, GLA_DK) * (GLA_DK ** -0.5),
        gk.reshape(B, L, GLA_HEADS, GLA_DK),
        gv.reshape(B, L, GLA_HEADS, GLA_DV),
        log_a.reshape(B, L, GLA_HEADS, GLA_DK), s0)
    o_gla = rmsnorm(o_gla, g_gla) * jax.nn.silu(gr.reshape(B, L, GLA_HEADS, GLA_DV))

    q = sq.reshape(B, L, SWA_KV_HEADS, SWA_GROUP, HEAD_DIM)
    k = sk.reshape(B, L, SWA_KV_HEADS, HEAD_DIM)
    v = sv.reshape(B, L, SWA_KV_HEADS, HEAD_DIM)
    if kv_buf is None:
        o_swa = swa_prompt(q, k, v, swa_sinks)
        wp = min(WINDOW, L)
        kb_new, vb_new = k[:, L - wp:], v[:, L - wp:]
    else:
        o_swa, kb_new, vb_new = swa_sample(q, k, v, kv_buf[0], kv_buf[1], swa_sinks)

    mix = jnp.concatenate([o_gla.reshape(B, L, GLA_WIDTH).astype(x.dtype),
                           o_swa.reshape(B, L, SWA_WIDTH).astype(x.dtype)], axis=-1)
    x = x + gt1[:, None] * (mix @ w_o)

    h2 = rmsnorm(x, g_norm[1]) * (1.0 + sc2[:, None]) + sh2[:, None]
    u = h2 @ w_up
    ue = jnp.concatenate([conv_buf.astype(u.dtype), u], axis=1)
    uc = b_conv
    for j in range(CONV_W):
        uc = uc + w_conv[j] * ue[:, j:j + L]
    gate, val = jnp.split(uc, 2, axis=-1)
    x = x + gt2[:, None] * ((jax.nn.silu(gate) * val) @ w_down)
    conv_new = ue[:, -(CONV_W - 1):]
    return x, s_new, kb_new, vb_new, conv_new


def setup_inputs(seed: int = 0) -> dict:
    key = jax.random.key(seed)
    ks = jax.random.split(key, 32)
    nrm = lambda k, shape, s: jax.random.normal(k, shape, jnp.float32) * s
    wb = min(WINDOW, PAST_LEN)
    F2 = 2 * D_FF
    return {
        "x_prompt": nrm(ks[0], (BATCH, SEQ, D_MODEL), 1.0),
        "x_sample": nrm(ks[1], (DEC_BATCH, DEC_SEQ, D_MODEL), 1.0),
        "c_prompt": nrm(ks[2], (BATCH, D_MODEL), 1.0),
        "c_sample": nrm(ks[3], (DEC_BATCH, D_MODEL), 1.0),
        "state_gla": nrm(ks[4], (DEPTH, DEC_BATCH, GLA_HEADS, GLA_DK, GLA_DV), 0.5),
        "state_swa_k": nrm(ks[5], (DEPTH, DEC_BATCH, wb, SWA_KV_HEADS, HEAD_DIM), 1.0),
        "state_swa_v": nrm(ks[6], (DEPTH, DEC_BATCH, wb, SWA_KV_HEADS, HEAD_DIM), 1.0),
        "state_ffn_conv": nrm(ks[7], (DEPTH, DEC_BATCH, CONV_W - 1, F2), 1.0),
        "w_ada": nrm(ks[8], (DEPTH, D_MODEL, 6 * D_MODEL), D_MODEL ** -0.5),
        "b_ada": nrm(ks[9], (DEPTH, 6 * D_MODEL), 0.02),
        "g_norm": 1.0 + nrm(ks[10], (DEPTH, 2, D_MODEL), 0.02),
        "w_in": nrm(ks[11], (DEPTH, D_MODEL, IN_WIDTH), D_MODEL ** -0.5),
        "w_a_up": nrm(ks[12], (DEPTH, GLA_LR, GLA_QK), GLA_LR ** -0.5),
        "b_a": nrm(ks[13], (DEPTH, GLA_QK), 0.1),
        "g_gla": 1.0 + nrm(ks[14], (DEPTH, GLA_DV), 0.02),
        "swa_sinks": nrm(ks[15], (DEPTH, SWA_HEADS), 0.5),
        "w_o": nrm(ks[16], (DEPTH, MIX_WIDTH, D_MODEL), MIX_WIDTH ** -0.5),
        "w_up": nrm(ks[17], (DEPTH, D_MODEL, F2), D_MODEL ** -0.5),
        "w_conv": nrm(ks[18], (DEPTH, CONV_W, F2), CONV_W ** -0.5),
        "b_conv": nrm(ks[19], (DEPTH, F2), 0.02),
        "w_down": nrm(ks[20], (DEPTH, D_FF, D_MODEL), D_FF ** -0.5),
        "g_final": 1.0 + nrm(ks[21], (D_MODEL,), 0.02),
    }


def reference(x_prompt, x_sample, c_prompt, c_sample, state_gla, state_swa_k, state_swa_v,
              state_ffn_conv, w_ada, b_ada, g_norm, w_in, w_a_up, b_a, g_gla, swa_sinks,
              w_o, w_up, w_conv, b_conv, w_down, g_final):
    yp, ys = x_prompt, x_sample
    Bp = x_prompt.shape[0]
    gla_p, kp, vp, cp = [], [], [], []
    gla_s, kss, vss, cs = [], [], [], []
    for l in range(DEPTH):
        lp = (w_ada[l], b_ada[l], g_norm[l], w_in[l], w_a_up[l], b_a[l], g_gla[l],
              swa_sinks[l], w_o[l], w_up[l], w_conv[l], b_conv[l], w_down[l])
        s0_p = jnp.zeros((Bp, GLA_HEADS, GLA_DK, GLA_DV), state_gla.dtype)
        conv0_p = jnp.zeros((Bp, CONV_W - 1, 2 * D_FF), state_ffn_conv.dtype)
        yp, s_p, k_p, v_p, c_p = trunk_layer(yp, c_prompt, lp, s0_p, None, conv0_p)
        ys, s_s, k_s, v_s, c_s = trunk_layer(ys, c_sample, lp, state_gla[l],
                                             (state_swa_k[l], state_swa_v[l]), state_ffn_conv[l])
        gla_p.append(s_p); kp.append(k_p); vp.append(v_p); cp.append(c_p)
        gla_s.append(s_s); kss.append(k_s); vss.append(v_s); cs.append(c_s)
    y_prompt = rmsnorm(yp, g_final)
    y_sample = rmsnorm(ys, g_final)
    return (y_prompt, y_sample,
            jnp.stack(gla_p), jnp.stack(kp), jnp.stack(vp), jnp.stack(cp),
            jnp.stack(gla_s), jnp.stack(kss), jnp.stack(vss), jnp.stack(cs))
```

```python
import numpy as np
from contextlib import ExitStack
import concourse.bass as bass
import concourse.mybir as mybir
from concourse.bass_utils import run_bass_kernel_spmd

F32 = mybir.dt.float32
BF16 = mybir.dt.bfloat16
AF = mybir.ActivationFunctionType
ALU = mybir.AluOpType
AX = mybir.AxisListType

NCORES = 8
D = 4096
KC = 32
NPRE = 55
NFULL = 10
NT = 65
INW = 9232
F2 = 22016
DFF = 11008
NBLK = 172
OFF_GQ, OFF_GK, OFF_GV, OFF_GR, OFF_GA, OFF_SQ, OFF_SK, OFF_SV = 0, 1024, 2048, 4096, 6144, 6160, 8208, 8720
PW = 3088
NEG = -30000.0
EPS = 1e-6

COMPUTE = ("pe", "act", "dve")
QUEUES = ("sp", "pool")
ALLENG = COMPUTE + QUEUES


class Sched:
    def __init__(self, nc, stack, ndma=8):
        self.nc = nc
        self.sems = []
        self.eng_sem = {e: self._mk(stack, "s_" + e) for e in COMPUTE}
        self.eng_cnt = {e: 0 for e in COMPUTE}
        self.dma_pool = {q: [self._mk(stack, "d_%s%d" % (q, i)) for i in range(ndma)] for q in QUEUES}
        self.dma_n = {q: 0 for q in QUEUES}
        self.ops = {e: [] for e in ALLENG}
        self.known = {e: {} for e in ALLENG}
        self.lastw = {}
        self.reads = {}
        self.flip = 0

    def _mk(self, stack, name):
        s = stack.enter_context(self.nc.semaphore(name))
        self.sems.append(s)
        return len(self.sems) - 1

    def add(self, eng, fn, reads=(), writes=()):
        need = {}

        def dep(tok, kind):
            teng, si, val = tok
            if teng == eng and eng in COMPUTE:
                if eng == "pe" or kind != "raw":
                    return
            need[si] = max(need.get(si, 0), val)

        for b in reads:
            w = self.lastw.get(b)
            if w is not None:
                dep(w, "raw")
        for b in writes:
            w = self.lastw.get(b)
            if w is not None:
                dep(w, "waw")
            for r in self.reads.get(b, ()):
                dep(r, "war")
        if eng in COMPUTE:
            self.eng_cnt[eng] += 1
            si = self.eng_sem[eng]
            tok = (eng, si, self.eng_cnt[eng])
            inc = (si, 1)
        else:
            n = self.dma_n[eng]
            pool = self.dma_pool[eng]
            P = len(pool)
            si = pool[n % P]
            val = 16 * (n // P + 1)
            if n >= P:
                need[si] = max(need.get(si, 0), 16 * (n // P))
            tok = (eng, si, val)
            inc = (si, 16)
            self.dma_n[eng] += 1
        kn = self.known[eng]
        waits = []
        for si, val in need.items():
            if kn.get(si, 0) >= val:
                continue
            kn[si] = val
            waits.append((si, val))
        self.ops[eng].append((waits, fn, inc))
        for b in writes:
            self.lastw[b] = tok
            self.reads[b] = []
        for b in reads:
            if b in writes:
                continue
            lst = self.reads.setdefault(b, [])
            if eng in COMPUTE:
                lst[:] = [t for t in lst if t[0] != eng]
            lst.append(tok)
        return tok

    def pe(self, fn, reads=(), writes=()):
        return self.add("pe", fn, reads, writes)

    def act(self, fn, reads=(), writes=()):
        return self.add("act", fn, reads, writes)

    def dve(self, fn, reads=(), writes=()):
        return self.add("dve", fn, reads, writes)

    def any(self, fn, reads=(), writes=()):
        self.flip ^= 1
        if self.flip:
            return self.add("act", lambda e: fn(e, True), reads, writes)
        return self.add("dve", lambda e: fn(e, False), reads, writes)

    def dma(self, q, out, in_, reads=(), writes=()):
        return self.add(q, lambda e: e.dma_start(out=out, in_=in_), reads, writes)

    def barrier(self):
        targets = []
        for e in COMPUTE:
            if self.eng_cnt[e] > 0:
                targets.append((self.eng_sem[e], self.eng_cnt[e]))
        for q in QUEUES:
            pool = self.dma_pool[q]
            P = len(pool)
            n = self.dma_n[q]
            for i, si in enumerate(pool):
                cnt = (n - i + P - 1) // P if n > i else 0
                if cnt > 0:
                    targets.append((si, 16 * cnt))
        for eng in ALLENG:
            kn = self.known[eng]
            waits = []
            for si, val in targets:
                if kn.get(si, 0) >= val:
                    continue
                kn[si] = val
                waits.append((si, val))
            if waits:
                self.ops[eng].append((waits, None, None))
        self.lastw = {}
        self.reads = {}

    def emit(self):
        nc = self.nc
        sems = self.sems
        ops = self.ops

        def replay(name, eng):
            for waits, fn, inc in ops[name]:
                for si, val in waits:
                    eng.wait_ge(sems[si], val)
                if fn is not None:
                    ins = fn(eng)
                    ins.then_inc(sems[inc[0]], inc[1])

        with nc.Block() as block:
            @block.tensor
            def _(e):
                replay("pe", e)

            @block.scalar
            def _(e):
                replay("act", e)

            @block.vector
            def _(e):
                replay("dve", e)

            @block.sync
            def _(e):
                replay("sp", e)

            @block.gpsimd
            def _(e):
                replay("pool", e)
        self.ops = {e: [] for e in ALLENG}

    def end_phase(self):
        self.barrier()
        self.emit()


def acopy(e, is_act, out, in_):
    if is_act:
        return e.activation(out=out, in_=in_, func=AF.Copy)
    return e.tensor_copy(out=out, in_=in_)


def build_program():
    nc = bass.Bass("TRN2", target_bir_lowering=False)

    def din(name, shape):
        return nc.dram_tensor(name, list(shape), F32, kind="ExternalInput").ap()

    def dout(name, shape):
        return nc.dram_tensor(name, list(shape), F32, kind="ExternalOutput").ap()

    def dscr(name, shape, dt=F32):
        return nc.dram_tensor(name, list(shape), dt).ap()

    xall = din("xall", [NT, 128, D])
    cT_in = din("cT", [128, KC, 256])
    st_gla = din("st_gla", [16, 4, 256, 512])
    st_k = din("st_k", [16, 128, 512])
    st_v = din("st_v", [16, 128, 512])
    st_kT = din("st_kT", [16, 4, 128, 128])
    st_convT = din("st_convT", [F2, 32])
    w_ada = din("w_ada", [D, 6 * D])
    bada_bc = din("bada_bc", [128, 6 * D])
    gn_bc = din("gn_bc", [128, 2, D])
    w_in = din("w_in", [D, INW])
    w17 = din("w17", [17, 1024])
    ggla_bc = din("ggla_bc", [128, 512])
    sink_bc = din("sink_bc", [128, 16])
    sink_s = din("sink_s", [16, 4])
    w_o = din("w_o", [D, D])
    w_up = din("w_up", [D, F2])
    w_convT = din("w_convT", [F2, 4])
    w_down = din("w_down", [DFF, D])
    gf_bc = din("gf_bc", [128, D])
    ident_in = din("ident", [128, 128])
    cmat = din("cmat", [128, 5, 128])
    ti4 = din("ti4", [128, 2, 4, 128])
    negcol = din("negcol", [128, 1])
    bsel16 = din("bsel16", [128, 16])
    bmask = din("bmask", [128, 16])
    pmask = din("pmask", [128, NT])
    hmask = din("hmask", [128, 1])
    bias_p = din("bias_p", [128, 2, 16, 256])
    bias_sb = din("bias_sb", [16, 4, 128])
    bias_sn = din("bias_sn", [16, 4, 16, 64])

    y_p = dout("y_p", [8, 128, D])
    y_s = dout("y_s", [64, D])
    o_gla_p = dout("o_gla_p", [4, 256, 512])
    o_k_p = dout("o_k_p", [128, 512])
    o_v_p = dout("o_v_p", [128, 512])
    o_convT = dout("o_convT", [F2, 34])
    o_gla_s = dout("o_gla_s", [16, 4, 256, 512])
    o_k_s = dout("o_k_s", [16, 128, 512])
    o_v_s = dout("o_v_s", [16, 128, 512])

    MOD = dscr("MOD", [2, 128, 6 * D])
    HT = dscr("HT", [NT, 128, KC * 128], BF16)
    PP = dscr("PP", [NPRE, 128, PW])
    PH2 = dscr("PH2", [128, 1024])
    PF = dscr("PF", [NFULL, 128, INW])
    MIX = dscr("MIX", [NFULL, 128, D], BF16)
    X1 = dscr("X1", [NFULL, 128, D])
    H2T = dscr("H2T", [NFULL, 128, KC * 128], BF16)
    ACTT = dscr("ACTT", [9, 128, 86, 128], BF16)
    X2 = dscr("X2", [9, 128, D])

    with ExitStack() as top:
        S = Sched(nc, top)

        with ExitStack() as ph:
            sb = lambda n, s, dt=F32: ph.enter_context(nc.sbuf_tensor(n, list(s), dt))
            ps = lambda n, s, dt=F32: ph.enter_context(nc.psum_tensor(n, list(s), dt))
            cTs = sb("cTs", [128, KC, 256])
            scb = sb("scb", [128, KC, 256], BF16)
            wb = [sb("wada%d" % i, [128, KC, 512], BF16) for i in range(2)]
            bb = [sb("bada%d" % i, [128, 512]) for i in range(2)]
            mo = [sb("mo%d" % i, [128, 2, 512]) for i in range(2)]
            pm = [ps("pm%d" % i, [128, 2, 512]) for i in range(2)]
            S.dma("sp", cTs[:], cT_in, writes=["cTs"])
            S.act(lambda e: e.activation(out=scb[:], in_=cTs[:], func=AF.Silu), reads=["cTs"], writes=["scb"])
            def ldw0(n):
                S.dma("pool", wb[n % 2][:], w_ada[:, n * 512:(n + 1) * 512].rearrange("(k p) n -> p k n", p=128), writes=[("w", n % 2)])
            ldw0(0)
            for n in range(48):
                i = n % 2
                cs = slice(n * 512, (n + 1) * 512)
                if n + 1 < 48:
                    ldw0(n + 1)
                S.dma("sp", bb[i][:], bada_bc[:, cs], writes=[("b", i)])

                def mm(e, i=i):
                    for g in range(2):
                        for k in range(KC):
                            ins = e.matmul(pm[i][:, g, :], lhsT=scb[:, k, g * 128:(g + 1) * 128], rhs=wb[i][:, k, :],
                                           start=(k == 0), stop=(k == KC - 1))
                    return ins
                S.pe(mm, reads=["scb", ("w", i)], writes=[("pm", i)])
                for g in range(2):
                    S.dve(lambda e, i=i, g=g: e.tensor_tensor(out=mo[i][:, g, :], in0=pm[i][:, g, :], in1=bb[i][:], op=ALU.add),
                          reads=[("pm", i), ("b", i)], writes=[("mo", i, g)])
                S.dma("pool", MOD[:, :, cs].rearrange("g p n -> p g n"), mo[i][:],
                      reads=[("mo", i, 0), ("mo", i, 1)], writes=[("MOD", n)])
            S.end_phase()

        def norm_phase(tag, src_tiles, mod_sel, off_sh, off_sc, gidx, dst):
            with ExitStack() as ph:
                sb = lambda n, s, dt=F32: ph.enter_context(nc.sbuf_tensor(tag + n, list(s), dt))
                ps = lambda n, s, dt=F32: ph.enter_context(nc.psum_tensor(tag + "ps" + n, list(s), dt))
                G = [sb("G%d" % g, [128, D]) for g in range(2)]
                SH = [sb("SH%d" % g, [128, D]) for g in range(2)]
                gn = sb("gn", [128, D])
                idf = sb("idf", [128, 128])
                idb = sb("idb", [128, 128], BF16)
                epsb = sb("eps", [128, 1])
                ssq = sb("ssq", [128, 2 * len(src_tiles)])
                xt = [sb("xt%d" % i, [128, D]) for i in range(3)]
                junk = sb("junk", [128, D], BF16)
                hb = [sb("hb%d" % i, [128, D], BF16) for i in range(2)]
                hT = [sb("hT%d" % i, [128, KC, 128], BF16) for i in range(2)]
                ptr = [ps("ptr%d" % i, [128, 16, 128], BF16) for i in range(3)]
                S.dma("sp", gn[:], gn_bc[:, gidx, :], writes=["gn"])
                S.dma("sp", idf[:], ident_in, writes=["idf"])
                S.dve(lambda e: e.tensor_copy(out=idb[:], in_=idf[:]), reads=["idf"], writes=["idb"])
                S.dve(lambda e: e.memset(epsb[:], EPS), writes=["eps"])
                S.dve(lambda e: e.memset(ssq[:], 0.0), writes=["ssq"])
                for g in range(2):
                    S.dma("sp", SH[g][:], MOD[g, :, off_sh:off_sh + D], writes=[("SH", g)])
                    S.dma("sp", G[g][:], MOD[g, :, off_sc:off_sc + D], writes=[("G", g)])
                    S.dve(lambda e, g=g: e.scalar_tensor_tensor(out=G[g][:], in0=G[g][:], scalar=1.0, in1=gn[:],
                                                                op0=ALU.add, op1=ALU.mult),
                          reads=[("G", g), "gn"], writes=[("G", g)])
                nptr = 0
                for i, src in enumerate(src_tiles):
                    g = mod_sel[i]
                    a = i % 3
                    b2 = i % 2
                    S.dma("sp", xt[a][:], src, writes=[("xt", a)])
                    S.act(lambda e, a=a, i=i: e.activation(out=junk[:], in_=xt[a][:], func=AF.Square,
                                                           accum_out=ssq[:, 2 * i:2 * i + 1]),
                          reads=[("xt", a), "ssq"], writes=["junk", ("ssq", i)])
                    S.act(lambda e, i=i: e.activation(out=ssq[:, 2 * i + 1:2 * i + 2], in_=ssq[:, 2 * i:2 * i + 1],
                                                      func=AF.Sqrt, bias=epsb[:, 0:1], scale=1.0 / D),
                          reads=[("ssq", i), "eps"], writes=[("ssq2", i)])
                    S.dve(lambda e, i=i: e.reciprocal(out=ssq[:, 2 * i + 1:2 * i + 2], in_=ssq[:, 2 * i + 1:2 * i + 2]),
                          reads=[("ssq2", i)], writes=[("ssq2", i)])
                    S.dve(lambda e, a=a, b2=b2, i=i, g=g: e.scalar_tensor_tensor(
                        out=xt[a][:], in0=xt[a][:], scalar=ssq[:, 2 * i + 1:2 * i + 2], in1=G[g][:],
                        op0=ALU.mult, op1=ALU.mult),
                        reads=[("xt", a), ("ssq2", i), ("G", g)], writes=[("xt", a)])
                    S.dve(lambda e, a=a, b2=b2, g=g: e.tensor_tensor(out=hb[b2][:], in0=xt[a][:], in1=SH[g][:], op=ALU.add),
                          reads=[("xt", a), ("SH", g)], writes=[("hb", b2)])
                    for half in range(2):
                        pj = nptr % 3
                        nptr += 1

                        def tr(e, b2=b2, half=half, pj=pj):
                            for k in range(16):
                                kk = half * 16 + k
                                ins = e.transpose(out=ptr[pj][:, k, :], in_=hb[b2][:, kk * 128:(kk + 1) * 128], identity=idb[:])
                            return ins
                        S.pe(tr, reads=[("hb", b2), "idb"], writes=[("ptr", pj)])
                        S.any(lambda e, ia, b2=b2, half=half, pj=pj: acopy(e, ia, hT[b2][:, half * 16:(half + 1) * 16, :], ptr[pj][:]),
                              reads=[("ptr", pj)], writes=[("hT", b2, half)])
                    S.dma("pool", dst[i], hT[b2][:].rearrange("p k t -> p (k t)"),
                          reads=[("hT", b2, 0), ("hT", b2, 1)], writes=[("dst", i)])
                S.end_phase()

        norm_phase("nA", [xall[i] for i in range(NT)], [1 if i == NT - 1 else 0 for i in range(NT)],
                   0, D, 0, [HT[i] for i in range(NT)])

        ALLT = list(range(NT))
        FULLT = list(range(NPRE, NT))

        def pdest(t, c0, n):
            if t >= NPRE:
                return PF[t - NPRE][:, c0:c0 + n]
            if c0 >= OFF_SK:
                return PH2[:, c0 - OFF_SK:c0 - OFF_SK + n]
            if c0 == OFF_GA:
                return PP[t][:, 3072:3088]
            return PP[t][:, c0 - OFF_GK:c0 - OFF_GK + n]

        blocks = []
        for j in range(2):
            blocks.append((OFF_GQ + j * 512, 512, FULLT))
        for j in range(2):
            blocks.append((OFF_GK + j * 512, 512, ALLT))
        for j in range(4):
            blocks.append((OFF_GV + j * 512, 512, ALLT))
        for j in range(4):
            blocks.append((OFF_GR + j * 512, 512, FULLT))
        blocks.append((OFF_GA, 16, ALLT))
        for j in range(4):
            blocks.append((OFF_SQ + j * 512, 512, FULLT))
        blocks.append((OFF_SK, 512, [NPRE - 1] + FULLT))
        blocks.append((OFF_SV, 512, [NPRE - 1] + FULLT))

        with ExitStack() as ph:
            sb = lambda n, s, dt=F32: ph.enter_context(nc.sbuf_tensor("pB" + n, list(s), dt))
            ps = lambda n, s, dt=F32: ph.enter_context(nc.psum_tensor("psB" + n, list(s), dt))
            wb = [sb("w%d" % i, [128, KC, 512], BF16) for i in range(2)]
            hts = [sb("h%d" % i, [128, KC, 128], BF16) for i in range(4)]
            ob = [sb("o%d" % i, [128, 512]) for i in range(4)]
            pp = [ps("p%d" % i, [128, 512]) for i in range(4)]
            cnt = 0
            def ldwB(bi):
                c0, n, tl = blocks[bi]
                S.dma("pool", wb[bi % 2][:, :, 0:n], w_in[:, c0:c0 + n].rearrange("(k p) n -> p k n", p=128), writes=[("w", bi % 2)])
            ldwB(0)
            for bi, (c0, n, tl) in enumerate(blocks):
                wi = bi % 2
                if bi + 1 < len(blocks):
                    ldwB(bi + 1)
                for t in tl:
                    a = cnt % 4
                    cnt += 1
                    S.dma("sp", hts[a][:].rearrange("p k t -> p (k t)"), HT[t], writes=[("h", a)])

                    def mm(e, a=a, wi=wi, n=n):
                        for k in range(KC):
                            ins = e.matmul(pp[a][:, 0:n], lhsT=hts[a][:, k, :], rhs=wb[wi][:, k, 0:n],
                                           start=(k == 0), stop=(k == KC - 1))
                        return ins
                    S.pe(mm, reads=[("h", a), ("w", wi)], writes=[("p", a)])
                    S.any(lambda e, ia, a=a, n=n: acopy(e, ia, ob[a][:, 0:n], pp[a][:, 0:n]), reads=[("p", a)], writes=[("o", a)])
                    S.dma("pool", pdest(t, c0, n), ob[a][:, 0:n], reads=[("o", a)], writes=[("pd", t, c0)])
            S.end_phase()

        with ExitStack() as ph:
            sb = lambda n, s, dt=F32: ph.enter_context(nc.sbuf_tensor("pC" + n, list(s), dt))
            ps = lambda n, s, dt=F32: ph.enter_context(nc.psum_tensor("psC" + n, list(s), dt))
            idf = sb("idf", [128, 128])
            cm = sb("cm", [128, 5, 128])
            t4 = sb("t4", [128, 2, 4, 128])
            ncol = sb("ncol", [128, 1])
            bs16 = sb("bs16", [128, 16])
            bmk = sb("bmk", [128, 16])
            pmk = sb("pmk", [128, NT])
            w17s = sb("w17", [32, 1024])
            gg = sb("gg", [128, 512])
            one1 = sb("one1", [128, 1])
            epsb = sb("eps", [128, 1])
            St = sb("S", [128, 8, 512])
            Sb = sb("Sb", [128, 8, 512], BF16)
            kt = [sb("k%d" % i, [128, 1024]) for i in range(2)]
            vt = [sb("v%d" % i, [128, 2048]) for i in range(2)]
            gat = [sb("ga%d" % i, [128, 16]) for i in range(2)]
            qt = [sb("q%d" % i, [128, 1024]) for i in range(2)]
            rt = [sb("r%d" % i, [128, 2048]) for i in range(2)]
            ga17 = [sb("g17%d" % i, [32, 128]) for i in range(2)]
            e1 = sb("e1", [128, 1024])
            la2 = [sb("la%d" % i, [128, 1024]) for i in range(2)]
            er = sb("er", [128, 1024])
            kd2 = [sb("kd%d" % i, [128, 1024], BF16) for i in range(2)]
            vb2 = [sb("vb%d" % i, [128, 2048], BF16) for i in range(2)]
            dec2 = [sb("dec%d" % i, [128, 16]) for i in range(2)]
            eb = sb("eb", [128, 8, 128])
            enb = sb("enb", [128, 8, 128])
            qg = sb("qg", [128, 8, 128], BF16)
            kg = sb("kg", [128, 8, 128], BF16)
            qg0 = sb("qg0", [128, 8, 128], BF16)
            qg1 = sb("qg1", [128, 8, 128], BF16)
            AT = sb("AT", [128, 4, 128], BF16)
            sg = sb("sg", [128, 2048])
            ss = sb("ss", [128, 8])
            junk = sb("junk", [128, 512], BF16)
            tn = sb("tn", [128, 512])
            mixo = [sb("mx%d" % i, [128, 2048], BF16) for i in range(2)]
            qgm = sb("qgm", [128, 16, 8, 4], BF16)
            qgmf = sb("qgmf", [128, 8, 128], BF16)
            decs = sb("decs", [128, 8, 16])
            s0 = [sb("s0%d" % i, [128, 4, 512]) for i in range(2)]
            s0b = [sb("s0b%d" % i, [128, 4, 512], BF16) for i in range(2)]
            vm = [sb("vm%d" % i, [128, 1024], BF16) for i in range(2)]
            PA = ps("A", [128, 2, 512])
            PB = ps("B", [128, 8, 128])
            PAT = ps("AT", [128, 4, 128])
            PO = ps("O", [128, 2, 512])
            PDS = ps("DS", [128, 512])

            S.dma("sp", idf[:], ident_in, writes=["idf"])
            S.dma("sp", cm[:], cmat, writes=["cm"])
            S.dma("sp", t4[:], ti4, writes=["t4"])
            S.dma("sp", ncol[:], negcol, writes=["ncol"])
            S.dma("sp", bs16[:], bsel16, writes=["bs16"])
            S.dma("sp", bmk[:], bmask, writes=["bmk"])
            S.dma("sp", pmk[:], pmask, writes=["pmk"])
            S.dma("sp", w17s[0:17, :], w17, writes=["w17"])
            S.dma("sp", gg[:], ggla_bc, writes=["gg"])
            S.dve(lambda e: e.memset(one1[:], 1.0), writes=["one1"])
            S.dve(lambda e: e.memset(epsb[:], EPS), writes=["eps"])
            S.dve(lambda e: e.memset(St[:], 0.0), writes=["S"])
            S.dve(lambda e: e.memset(Sb[:], 0.0), writes=["Sb"])
            S.dve(lambda e: e.memset(qg0[:], 0.0), writes=["qg0"])
            S.dve(lambda e: e.memset(qg1[:], 0.0), writes=["qg1"])
            S.dve(lambda e: e.memset(qgmf[:], 0.0), writes=["qgmf"])
            for i in range(2):
                S.dve(lambda e, i=i: e.memset(ga17[i][:], 1.0), writes=[("g17", i)])

            def gla_tile(t):
                full = t >= NPRE
                samp = t == NT - 1
                f = t - NPRE
                a = t % 2
                la = la2[a]
                kd = kd2[a]
                vb = vb2[a]
                dec = dec2[a]
                ci = 2 if samp else 0
                if full:
                    src = PF[f]
                    S.dma("sp", kt[a][:], src[:, OFF_GK:OFF_GK + 1024], writes=[("k", a)])
                    S.dma("sp", vt[a][:], src[:, OFF_GV:OFF_GV + 2048], writes=[("v", a)])
                    S.dma("sp", gat[a][:], src[:, OFF_GA:OFF_GA + 16], writes=[("ga", a)])
                    S.dma("sp", qt[a][:], src[:, OFF_GQ:OFF_GQ + 1024], writes=[("q", a)])
                    S.dma("sp", rt[a][:], src[:, OFF_GR:OFF_GR + 2048], writes=[("r", a)])
                else:
                    S.dma("sp", kt[a][:], PP[t][:, 0:1024], writes=[("k", a)])
                    S.dma("sp", vt[a][:], PP[t][:, 1024:3072], writes=[("v", a)])
                    S.dma("sp", gat[a][:], PP[t][:, 3072:3088], writes=[("ga", a)])
                S.pe(lambda e, a=a: e.transpose(out=PA[0:16, 0, 0:128], in_=gat[a][:], identity=idf[:]),
                     reads=[("ga", a), "idf"], writes=["PA"])
                S.act(lambda e, a=a: e.activation(out=ga17[a][0:16, :], in_=PA[0:16, 0, 0:128], func=AF.Copy),
                      reads=["PA"], writes=[("g17", a)])

                def mmz(e, a=a):
                    for n in range(2):
                        ins = e.matmul(PA[:, n, :], lhsT=ga17[a][0:17, :], rhs=w17s[0:17, n * 512:(n + 1) * 512],
                                       start=True, stop=True)
                    return ins
                S.pe(mmz, reads=[("g17", a), "w17"], writes=["PA"])
                S.act(lambda e: e.activation(out=e1[:], in_=PA[:].rearrange("p a b -> p (a b)"), func=AF.Exp, scale=-1.0),
                      reads=["PA"], writes=["e1"])
                S.act(lambda e: e.activation(out=la[:], in_=e1[:], func=AF.Ln, bias=one1[:, 0:1], scale=1.0),
                      reads=["e1", "one1"], writes=[("la", a)])

                ui = (ci + 1) if full else 4

                def mmr(e, ui=ui):
                    for n in range(2):
                        ins = e.matmul(PA[:, n, :], lhsT=cm[:, ui, :], rhs=la[:, n * 512:(n + 1) * 512], start=True, stop=True)
                    return ins
                S.pe(mmr, reads=[("la", a), "cm"], writes=["PA"])
                S.act(lambda e: e.activation(out=er[:], in_=PA[:].rearrange("p a b -> p (a b)"), func=AF.Exp),
                      reads=["PA"], writes=["er"])
                S.dve(lambda e, a=a, t=t: e.scalar_tensor_tensor(out=kd[:], in0=kt[a][:], scalar=pmk[:, t:t + 1], in1=er[:],
                                                                 op0=ALU.mult, op1=ALU.mult),
                      reads=[("k", a), "pmk", "er"], writes=[("kd", a)])
                S.act(lambda e, a=a: e.activation(out=vb[:], in_=vt[a][:], func=AF.Copy), reads=[("v", a)], writes=[("vb", a)])
                if not full:
                    def mmd(e):
                        for c in range(8):
                            ins = e.matmul(PAT[:, 0, c:c + 1], lhsT=la[:, c * 128:(c + 1) * 128], rhs=ncol[:, 0:1], start=True, stop=True)
                        return ins
                    S.pe(mmd, reads=[("la", a), "ncol"], writes=["PAT"])
                    S.act(lambda e: e.activation(out=dec[:, 0:8], in_=PAT[:, 0, 0:8], func=AF.Exp), reads=["PAT"], writes=[("dec", a)])
                elif not samp:
                    def mmd(e):
                        for j in range(2):
                            for c in range(8):
                                ins = e.matmul(PAT[:, 0, j * 8 + c:j * 8 + c + 1],
                                               lhsT=la[64 * j:64 * j + 64, c * 128:(c + 1) * 128],
                                               rhs=ncol[64 * j:64 * j + 64, 0:1], start=True, stop=True)
                        return ins
                    S.pe(mmd, reads=[("la", a), "ncol"], writes=["PAT"])
                    S.act(lambda e: e.activation(out=dec[:], in_=PAT[:, 0, 0:16], func=AF.Exp), reads=["PAT"], writes=[("dec", a)])
                else:
                    def mmd(e):
                        for c in range(8):
                            ins = e.matmul(PAT[:, 0, c * 16:(c + 1) * 16], lhsT=la[0:64, c * 128:(c + 1) * 128],
                                           rhs=bs16[0:64, :], start=True, stop=True)
                        return ins
                    S.pe(mmd, reads=[("la", a), "bs16"], writes=["PAT"])
                    S.act(lambda e: e.activation(out=decs[:].rearrange("p c b -> p (c b)"), in_=PAT[:, 0, :], func=AF.Exp),
                          reads=["PAT"], writes=["decs"])

                if full:
                    def mmb(e, ci=ci):
                        for c in range(8):
                            ins = e.matmul(PA[:].rearrange("p a (c t) -> p (a c) t", t=128)[:, c, :],
                                           lhsT=la[:, c * 128:(c + 1) * 128], rhs=cm[:, ci, :], start=True, stop=True)
                        return ins
                    S.pe(mmb, reads=[("la", a), "cm"], writes=["PA"])
                    pa8 = PA[:].rearrange("p a (c t) -> p (a c) t", t=128)
                    S.act(lambda e, pa8=pa8: e.activation(out=eb[:], in_=pa8, func=AF.Exp), reads=["PA"], writes=["eb"])
                    S.act(lambda e, pa8=pa8: e.activation(out=enb[:], in_=pa8, func=AF.Exp, scale=-1.0), reads=["PA"], writes=["enb"])

                    def trq(e, a=a):
                        for c in range(8):
                            ins = e.transpose(out=PB[:, c, :], in_=qt[a][:, c * 128:(c + 1) * 128], identity=idf[:])
                        return ins
                    S.pe(trq, reads=[("q", a), "idf"], writes=["PB"])
                    S.dve(lambda e: e.scalar_tensor_tensor(out=qg[:], in0=PB[:], scalar=0.0625, in1=eb[:],
                                                           op0=ALU.mult, op1=ALU.mult),
                          reads=["PB", "eb"], writes=["qg"])

                    def trk(e, a=a):
                        for c in range(8):
                            ins = e.transpose(out=PB[:, c, :], in_=kt[a][:, c * 128:(c + 1) * 128], identity=idf[:])
                        return ins
                    S.pe(trk, reads=[("k", a), "idf"], writes=["PB"])
                    S.dve(lambda e: e.tensor_tensor(out=kg[:], in0=PB[:], in1=enb[:], op=ALU.mult),
                          reads=["PB", "enb"], writes=["kg"])
                    if not samp:
                        S.act(lambda e: e.activation(out=qg0[:, :, 0:64], in_=qg[:, :, 0:64], func=AF.Copy),
                              reads=["qg"], writes=["qg0"])
                        S.act(lambda e: e.activation(out=qg1[:, :, 64:128], in_=qg[:, :, 64:128], func=AF.Copy),
                              reads=["qg"], writes=["qg1"])

                    def mma(e):
                        for h in range(4):
                            for kc in range(2):
                                ins = e.matmul(PAT[:, h, :], lhsT=kg[:, h * 2 + kc, :], rhs=qg[:, h * 2 + kc, :],
                                               start=(kc == 0), stop=(kc == 1))
                        return ins
                    S.pe(mma, reads=["kg", "qg"], writes=["PAT"])
                    S.dve(lambda e, samp=samp: e.tensor_tensor(out=AT[:], in0=PAT[:], in1=t4[:, 1 if samp else 0, :, :], op=ALU.mult),
                          reads=["PAT", "t4"], writes=["AT"])
                    S.act(lambda e, a=a: e.activation(out=sg[:], in_=rt[a][:], func=AF.Silu), reads=[("r", a)], writes=["sg"])

                dsb = [(PDS[:], "PDS"), (PO[:, 0, :], ("PO", 0)), (PO[:, 1, :], ("PO", 1))]
                dsn = [0]

                def s_update(j, c, h, rot=False, cast=True):
                    if rot:
                        dap, dkey = dsb[dsn[0] % 3]
                        dsn[0] += 1
                    else:
                        dap, dkey = dsb[0]
                    if j is None:
                        p0, p1, dcol = 0, 128, c
                    else:
                        p0, p1, dcol = 64 * j, 64 * j + 64, j * 8 + c
                    S.pe(lambda e, c=c, h=h, dap=dap, p0=p0, p1=p1: e.matmul(dap, lhsT=kd[p0:p1, c * 128:(c + 1) * 128],
                                                                             rhs=vb[p0:p1, h * 512:(h + 1) * 512], start=True, stop=True),
                         reads=[("kd", a), ("vb", a)], writes=[dkey])
                    S.dve(lambda e, c=c, dap=dap, dcol=dcol: e.scalar_tensor_tensor(out=St[:, c, :], in0=St[:, c, :],
                                                                                    scalar=dec[:, dcol:dcol + 1], in1=dap,
                                                                                    op0=ALU.mult, op1=ALU.add),
                          reads=[dkey, ("dec", a), ("S", c)], writes=[("S", c)])
                    if cast:
                        S.act(lambda e, c=c: e.activation(out=Sb[:, c, :], in_=St[:, c, :], func=AF.Copy),
                              reads=[("S", c)], writes=[("Sb", c)])

                def finish_heads(hp, a=a, f=f):
                    m = f % 2
                    for hh in range(2):
                        h = hp * 2 + hh
                        S.act(lambda e, hh=hh, h=h: e.activation(out=junk[:], in_=PO[:, hh, :], func=AF.Square,
                                                                 accum_out=ss[:, h:h + 1]),
                              reads=[("PO", hh), "ss0"], writes=["junk", ("ss", h)])
                        S.act(lambda e, h=h: e.activation(out=ss[:, 4 + h:5 + h], in_=ss[:, h:h + 1], func=AF.Sqrt,
                                                          bias=epsb[:, 0:1], scale=1.0 / 512),
                              reads=[("ss", h), "eps"], writes=[("rs", h)])
                        S.dve(lambda e, h=h: e.reciprocal(out=ss[:, 4 + h:5 + h], in_=ss[:, 4 + h:5 + h]),
                              reads=[("rs", h)], writes=[("rs", h)])
                        S.dve(lambda e, hh=hh, h=h: e.scalar_tensor_tensor(out=tn[:], in0=PO[:, hh, :], scalar=ss[:, 4 + h:5 + h],
                                                                           in1=gg[:], op0=ALU.mult, op1=ALU.mult),
                              reads=[("PO", hh), ("rs", h), "gg"], writes=["tn"])
                        S.dve(lambda e, h=h, m=m: e.tensor_tensor(out=mixo[m][:, h * 512:(h + 1) * 512], in0=tn[:],
                                                                  in1=sg[:, h * 512:(h + 1) * 512], op=ALU.mult),
                              reads=["tn", "sg"], writes=[("mx", m, h)])

                if not samp:
                    if full:
                        S.dve(lambda e: e.memset(ss[:, 0:4], 0.0), writes=["ss0"] + [("ss", h) for h in range(4)])
                        for hp in range(2):
                            for hh in range(2):
                                h = hp * 2 + hh

                                def mmo(e, hh=hh, h=h):
                                    e.matmul(PO[:, hh, :], lhsT=AT[:, h, :], rhs=vb[:, h * 512:(h + 1) * 512], start=True, stop=False)
                                    for kc in range(2):
                                        ins = e.matmul(PO[:, hh, :], lhsT=qg0[:, h * 2 + kc, :], rhs=Sb[:, h * 2 + kc, :],
                                                       start=False, stop=False)
                                    return ins
                                S.pe(mmo, reads=["AT", ("vb", a), "qg0", ("Sb", h * 2), ("Sb", h * 2 + 1)], writes=[("PO", hh)])
                            for hh in range(2):
                                h = hp * 2 + hh
                                for kc in range(2):
                                    s_update(0, h * 2 + kc, h)
                            for hh in range(2):
                                h = hp * 2 + hh

                                def mmo2(e, hh=hh, h=h):
                                    for kc in range(2):
                                        ins = e.matmul(PO[:, hh, :], lhsT=qg1[:, h * 2 + kc, :], rhs=Sb[:, h * 2 + kc, :],
                                                       start=False, stop=(kc == 1))
                                    return ins
                                S.pe(mmo2, reads=["qg1", ("Sb", h * 2), ("Sb", h * 2 + 1)], writes=[("PO", hh)])
                            for hh in range(2):
                                h = hp * 2 + hh
                                for kc in range(2):
                                    s_update(1, h * 2 + kc, h)
                            finish_heads(hp)
                        S.dma("pool", MIX[f][:, 0:2048], mixo[f % 2][:], reads=[("mx", f % 2, h) for h in range(4)],
                              writes=[("MIXg", f)])
                    else:
                        for h in range(4):
                            for kc in range(2):
                                s_update(None, h * 2 + kc, h, rot=True, cast=(t == NPRE - 1))
                    if t == NT - 2:
                        S.dma("sp", o_gla_p.rearrange("h (kc p) v -> p h kc v", p=128),
                              St[:].rearrange("p (h kc) v -> p h kc v", kc=2),
                              reads=[("S", c) for c in range(8)], writes=["o_gla_p"])
                else:
                    S.dve(lambda e: e.tensor_copy(out=qgm[:], in_=qg[:, :, 0:64].rearrange("p c (b t) -> p b c t", t=4)),
                          reads=["qg"], writes=["qgm"])
                    S.dve(lambda e: e.memset(ss[:, 0:4], 0.0), writes=["ss0"] + [("ss", h) for h in range(4)])
                    it = 0
                    for hp in range(2):
                        for hh in range(2):
                            h = hp * 2 + hh
                            S.pe(lambda e, hh=hh, h=h: e.matmul(PO[:, hh, :], lhsT=AT[:, h, :], rhs=vb[:, h * 512:(h + 1) * 512],
                                                                start=True, stop=False),
                                 reads=["AT", ("vb", a)], writes=[("PO", hh)])
                        for b in range(16):
                            u = it % 2
                            it += 1
                            S.dma("sp", s0[u][:].rearrange("p (h kc) v -> p h kc v", kc=2),
                                  st_gla[b, hp * 2:hp * 2 + 2].rearrange("h (kc p) v -> p h kc v", p=128), writes=[("s0", u)])
                            S.act(lambda e, u=u: e.activation(out=s0b[u][:], in_=s0[u][:], func=AF.Copy),
                                  reads=[("s0", u)], writes=[("s0b", u)])
                            S.dve(lambda e, b=b: e.tensor_copy(out=qgmf[:, :, 4 * b:4 * b + 4], in_=qgm[:, b, :, :]),
                                  reads=["qgm"], writes=["qgmf"])
                            S.act(lambda e, u=u, b=b, hp=hp: e.mul(out=vm[u][:], in_=vb[:, hp * 1024:(hp + 1) * 1024], mul=bmk[:, b:b + 1]),
                                  reads=[("vb", a), "bmk"], writes=[("vm", u)])

                            def mmi(e, u=u, hp=hp, b=b):
                                for hh in range(2):
                                    for kc in range(2):
                                        last = (b == 15 and kc == 1)
                                        ins = e.matmul(PO[:, hh, :], lhsT=qgmf[:, (hp * 2 + hh) * 2 + kc, :], rhs=s0b[u][:, hh * 2 + kc, :],
                                                       start=False, stop=last)
                                return ins
                            S.pe(mmi, reads=["qgmf", ("s0b", u)], writes=[("PO", 0), ("PO", 1)])
                            if b > 0 or True:
                                S.dve(lambda e, b=b: e.memset(qgmf[:, :, 4 * b:4 * b + 4], 0.0), reads=[], writes=["qgmf"])
                            for hh in range(2):
                                for kc in range(2):
                                    c = (hp * 2 + hh) * 2 + kc
                                    S.pe(lambda e, u=u, hh=hh, c=c: e.matmul(PDS[:], lhsT=kd[0:64, c * 128:(c + 1) * 128],
                                                                            rhs=vm[u][0:64, hh * 512:(hh + 1) * 512], start=True, stop=True),
                                         reads=[("kd", a), ("vm", u)], writes=["PDS"])
                                    S.dve(lambda e, u=u, hh=hh, kc=kc, c=c, b=b: e.scalar_tensor_tensor(
                                        out=s0[u][:, hh * 2 + kc, :], in0=s0[u][:, hh * 2 + kc, :], scalar=decs[:, c, b:b + 1],
                                        in1=PDS[:], op0=ALU.mult, op1=ALU.add),
                                        reads=["PDS", "decs", ("s0", u)], writes=[("s0", u)])
                            S.dma("pool", o_gla_s[b, hp * 2:hp * 2 + 2].rearrange("h (kc p) v -> p h kc v", p=128),
                                  s0[u][:].rearrange("p (h kc) v -> p h kc v", kc=2),
                                  reads=[("s0", u)], writes=[("ogs", b, hp)])
                        finish_heads(hp)
                    S.dma("sp", MIX[f][:, 0:2048], mixo[f % 2][:], reads=[("mx", f % 2, h) for h in range(4)],
                          writes=[("MIXg", f)])
            for t in range(NT):
                gla_tile(t)
            S.end_phase()

        SCALE = 128.0 ** -0.5
        with ExitStack() as ph:
            sb = lambda n, s, dt=F32: ph.enter_context(nc.sbuf_tensor("pD" + n, list(s), dt))
            ps = lambda n, s, dt=F32: ph.enter_context(nc.psum_tensor("psD" + n, list(s), dt))
            idf = sb("idf", [128, 128])
            idb = sb("idb", [128, 128], BF16)
            bias = sb("bias", [128, 2, 16, 256])
            snk = sb("snk", [128, 16])
            snks = sb("snks", [16, 4])
            bsb = sb("bsb", [16, 4, 128])
            bsn = sb("bsn", [16, 4, 16, 64])
            qin = [sb("qin%d" % i, [128, 2048]) for i in range(2)]
            kin = [sb("kin%d" % i, [128, 512]) for i in range(2)]
            vin = [sb("vin%d" % i, [128, 512]) for i in range(2)]
            qT = sb("qT", [128, 16, 128], BF16)
            kT = [sb("kT%d" % i, [128, 4, 128], BF16) for i in range(2)]
            vv = [sb("vv%d" % i, [128, 512], BF16) for i in range(2)]
            ssb = [sb("ssb%d" % i, [128, 256]) for i in range(4)]
            pex = [sb("pex%d" % i, [128, 256], BF16) for i in range(4)]
            qb = [sb("qb%d" % i, [128, 2048], BF16) for i in range(2)]
            kbf = [sb("kbf%d" % i, [128, 512], BF16) for i in range(2)]
            st = sb("st", [128, 16, 8])
            pT = [sb("pT%d" % i, [128, 2, 128], BF16) for i in range(4)]
            osw = [sb("osw%d" % i, [128, 2048], BF16) for i in range(2)]
            kbT = [sb("kbT%d" % i, [128, 4, 128], BF16) for i in range(2)]
            vbf = [sb("vbf%d" % i, [128, 512], BF16) for i in range(2)]
            OS = sb("OS", [16, 16, 4, 128], BF16)
            qS = sb("qS", [128, 16, 4, 16], BF16)
            PQ = ps("Q", [128, 16, 128], BF16)
            PK = ps("K", [128, 4, 128], BF16)
            PS_ = ps("S", [128, 4, 256])
            PT = ps("T", [128, 4, 2, 128], BF16)
            POo = ps("O", [128, 4, 128])
            S.dma("sp", idf[:], ident_in, writes=["idf"])
            S.dve(lambda e: e.tensor_copy(out=idb[:], in_=idf[:]), reads=["idf"], writes=["idb"])
            S.dma("sp", bias[:], bias_p, writes=["bias"])
            S.dma("sp", snk[:], sink_bc, writes=["snk"])
            S.dma("sp", snks[:], sink_s, writes=["snks"])
            S.dma("sp", bsb[:], bias_sb, writes=["bsb"])
            S.dma("sp", bsn[:], bias_sn, writes=["bsn"])
            S.dve(lambda e: e.memset(st[:], 0.0), writes=["st"])

            def load_kv(slot, ksrc, vsrc):
                S.dma("sp", kin[slot][:], ksrc, writes=[("kin", slot)])
                S.dma("sp", vin[slot][:], vsrc, writes=[("vin", slot)])

                S.act(lambda e, slot=slot: e.activation(out=kbf[slot][:], in_=kin[slot][:], func=AF.Copy),
                      reads=[("kin", slot)], writes=[("kbf", slot)])

                def trk(e, slot=slot):
                    for c in range(4):
                        ins = e.transpose(out=PK[:, c, :], in_=kbf[slot][:, c * 128:(c + 1) * 128], identity=idb[:])
                    return ins
                S.pe(trk, reads=[("kbf", slot), "idb"], writes=["PK"])
                S.act(lambda e, slot=slot: e.activation(out=kT[slot][:], in_=PK[:], func=AF.Copy), reads=["PK"], writes=[("kT", slot)])
                S.act(lambda e, slot=slot: e.activation(out=vv[slot][:], in_=vin[slot][:], func=AF.Copy),
                      reads=[("vin", slot)], writes=[("vv", slot)])

            def load_q(slot, qsrc):
                S.dma("sp", qin[slot][:], qsrc, writes=[("qin", slot)])

                S.act(lambda e, slot=slot: e.activation(out=qb[slot][:], in_=qin[slot][:], func=AF.Copy),
                      reads=[("qin", slot)], writes=[("qb", slot)])

                def trq(e, slot=slot):
                    for c in range(16):
                        ins = e.transpose(out=PQ[:, c, :], in_=qb[slot][:, c * 128:(c + 1) * 128], identity=idb[:])
                    return ins
                S.pe(trq, reads=[("qb", slot), "idb"], writes=["PQ"])
                S.dve(lambda e: e.tensor_copy(out=qT[:], in_=PQ[:]), reads=["PQ"], writes=["qT"])

            def softmax_rows(np_, z, sinkcol, width, hkey):
                S.dve(lambda e: e.tensor_reduce(out=st[0:np_, hkey, 0:1], in_=ssb[z][0:np_, 0:width], axis=AX.X, op=ALU.max),
                      reads=[("ssb", z)], writes=[("st", hkey)])
                S.dve(lambda e: e.tensor_scalar(out=st[0:np_, hkey, 1:2], in0=st[0:np_, hkey, 0:1], scalar1=sinkcol, scalar2=-1.0,
                                                op0=ALU.max, op1=ALU.mult),
                      reads=[("st", hkey), "snk"], writes=[("st", hkey)])
                S.act(lambda e: e.activation(out=pex[z][0:np_, 0:width], in_=ssb[z][0:np_, 0:width], func=AF.Exp,
                                             bias=st[0:np_, hkey, 1:2], scale=1.0, accum_out=st[0:np_, hkey, 2:3]),
                      reads=[("ssb", z), ("st", hkey)], writes=[("pex", z), ("st", hkey)])
                S.act(lambda e: e.activation(out=st[0:np_, hkey, 3:4], in_=sinkcol, func=AF.Exp, bias=st[0:np_, hkey, 1:2], scale=1.0),
                      reads=[("st", hkey), "snk"], writes=[("st", hkey)])
                S.dve(lambda e: e.tensor_tensor(out=st[0:np_, hkey, 4:5], in0=st[0:np_, hkey, 2:3], in1=st[0:np_, hkey, 3:4], op=ALU.add),
                      reads=[("st", hkey)], writes=[("st", hkey)])
                S.dve(lambda e: e.reciprocal(out=st[0:np_, hkey, 5:6], in_=st[0:np_, hkey, 4:5]),
                      reads=[("st", hkey)], writes=[("st", hkey)])

            load_kv(1, PH2[:, 0:512], PH2[:, 512:1024])
            hc = 0
            for f in range(NFULL):
                cur = f % 2
                prv = 1 - cur
                src = PF[f]
                load_kv(cur, src[:, OFF_SK:OFF_SK + 512], src[:, OFF_SV:OFF_SV + 512])
                load_q(cur, src[:, OFF_SQ:OFF_SQ + 2048])
                if f < NFULL - 1:
                    bsel = 0 if f == 1 else 1
                    for h in range(16):
                        kvh = h // 4
                        z = hc % 4
                        o4 = hc % 4
                        hc += 1

                        def mms(e, h=h, kvh=kvh, z=z, prv=prv, cur=cur):
                            e.matmul(PS_[:, z, 0:128], lhsT=qT[:, h, :], rhs=kT[prv][:, kvh, :], start=True, stop=True)
                            return e.matmul(PS_[:, z, 128:256], lhsT=qT[:, h, :], rhs=kT[cur][:, kvh, :], start=True, stop=True)
                        S.pe(mms, reads=["qT", ("kT", prv), ("kT", cur)], writes=[("PS", z)])
                        S.dve(lambda e, z=z, h=h, bsel=bsel: e.scalar_tensor_tensor(out=ssb[z][:], in0=PS_[:, z, :], scalar=SCALE,
                                                                                    in1=bias[:, bsel, h, :], op0=ALU.mult, op1=ALU.add),
                              reads=[("PS", z), "bias"], writes=[("ssb", z)])
                        S.dve(lambda e, h=h: e.memset(st[:, h, 2:3], 0.0), writes=[("st", h)])
                        softmax_rows(128, z, snk[:, h:h + 1], 256, h)

                        def trp(e, z=z):
                            e.transpose(out=PT[:, z, 0, :], in_=pex[z][:, 0:128], identity=idb[:])
                            return e.transpose(out=PT[:, z, 1, :], in_=pex[z][:, 128:256], identity=idb[:])
                        S.pe(trp, reads=[("pex", z), "idb"], writes=[("PT", z)])
                        S.any(lambda e, ia, z=z: acopy(e, ia, pT[z][:], PT[:, z, :, :]), reads=[("PT", z)], writes=[("pT", z)])

                        def mmv(e, z=z, kvh=kvh, o4=o4, prv=prv, cur=cur):
                            e.matmul(POo[:, o4, :], lhsT=pT[z][:, 0, :], rhs=vv[prv][:, kvh * 128:(kvh + 1) * 128], start=True, stop=False)
                            return e.matmul(POo[:, o4, :], lhsT=pT[z][:, 1, :], rhs=vv[cur][:, kvh * 128:(kvh + 1) * 128], start=False, stop=True)
                        S.pe(mmv, reads=[("pT", z), ("vv", prv), ("vv", cur)], writes=[("PO", o4)])
                        S.act(lambda e, o4=o4, h=h, cur=cur: e.mul(out=osw[cur][:, h * 128:(h + 1) * 128], in_=POo[:, o4, :], mul=st[:, h, 5:6]),
                              reads=[("PO", o4), ("st", h)], writes=[("osw", cur, h)])
                    S.dma("pool", MIX[f][:, 2048:4096], osw[cur][:], reads=[("osw", cur, h) for h in range(16)], writes=[("MIXs", f)])
                    if f == NFULL - 2:
                        S.dma("sp", o_k_p, src[:, OFF_SK:OFF_SK + 512], writes=["okp"])
                        S.dma("sp", o_v_p, src[:, OFF_SV:OFF_SV + 512], writes=["ovp"])
                else:
                    S.dma("sp", o_k_s[:, 0:124, :], st_k[:, 4:128, :], writes=["oks0"])
                    S.dma("sp", o_v_s[:, 0:124, :], st_v[:, 4:128, :], writes=["ovs0"])
                    S.dma("sp", o_k_s[:, 124:128, :], src[0:64, OFF_SK:OFF_SK + 512].rearrange("(b t) c -> b t c", t=4), writes=["oks1"])
                    S.dma("sp", o_v_s[:, 124:128, :], src[0:64, OFF_SV:OFF_SV + 512].rearrange("(b t) c -> b t c", t=4), writes=["ovs1"])
                    for kvh in range(4):
                        S.dve(lambda e, kvh=kvh: e.tensor_copy(out=qS[:, :, kvh, :].rearrange("p b (g t) -> p b g t", t=4),
                                                               in_=qT[:, kvh * 4:(kvh + 1) * 4, 0:64].rearrange("p g (b t) -> p b g t", t=4)),
                              reads=["qT"], writes=["qS"])
                    for b in range(16):
                        u = b % 2
                        S.dma("pool", kbT[u][:], st_kT[b].rearrange("h d k -> d h k"), writes=[("kbT", u)])
                        S.dma("pool", vbf[u][:], st_v[b], writes=[("vbf", u)])
                        for kvh in range(4):
                            z = hc % 4
                            o4 = hc % 4
                            hc += 1
                            hk = kvh * 4
                            qsl = qS[:, b, kvh, :]

                            def mms(e, z=z, kvh=kvh, u=u, qsl=qsl, cur=cur):
                                e.matmul(PS_[0:16, z, 0:128], lhsT=qsl, rhs=kbT[u][:, kvh, :], start=True, stop=True)
                                return e.matmul(PS_[0:16, z, 128:192], lhsT=qsl, rhs=kT[cur][:, kvh, 0:64], start=True, stop=True)
                            S.pe(mms, reads=["qS", ("kbT", u), ("kT", cur)], writes=[("PS", z)])
                            S.dve(lambda e, z=z, kvh=kvh: e.scalar_tensor_tensor(out=ssb[z][0:16, 0:128], in0=PS_[0:16, z, 0:128], scalar=SCALE,
                                                                                 in1=bsb[:, kvh, :], op0=ALU.mult, op1=ALU.add),
                                  reads=[("PS", z), "bsb"], writes=[("ssb", z)])
                            S.dve(lambda e, z=z, kvh=kvh, b=b: e.scalar_tensor_tensor(out=ssb[z][0:16, 128:192], in0=PS_[0:16, z, 128:192],
                                                                                      scalar=SCALE, in1=bsn[:, kvh, b, :],
                                                                                      op0=ALU.mult, op1=ALU.add),
                                  reads=[("PS", z), "bsn", ("ssb", z)], writes=[("ssb", z)])
                            S.dve(lambda e, hk=hk: e.memset(st[0:16, hk, 2:3], 0.0), writes=[("st", hk)])
                            softmax_rows(16, z, snks[:, kvh:kvh + 1], 192, hk)

                            def trp(e, z=z):
                                e.transpose(out=PT[:, z, 0, 0:16], in_=pex[z][0:16, 0:128], identity=idb[0:16, 0:16])
                                return e.transpose(out=PT[0:64, z, 1, 0:16], in_=pex[z][0:16, 128:192], identity=idb[0:16, 0:16])
                            S.pe(trp, reads=[("pex", z), "idb"], writes=[("PT", z)])
                            S.any(lambda e, ia, z=z: acopy(e, ia, pT[z][:, 0, 0:16], PT[:, z, 0, 0:16]), reads=[("PT", z)], writes=[("pT", z, 0)])
                            S.any(lambda e, ia, z=z: acopy(e, ia, pT[z][0:64, 1, 0:16], PT[0:64, z, 1, 0:16]), reads=[("PT", z)], writes=[("pT", z, 1)])

                            def mmv(e, z=z, kvh=kvh, o4=o4, u=u, cur=cur):
                                e.matmul(POo[0:16, o4, :], lhsT=pT[z][:, 0, 0:16], rhs=vbf[u][:, kvh * 128:(kvh + 1) * 128], start=True, stop=False)
                                return e.matmul(POo[0:16, o4, :], lhsT=pT[z][0:64, 1, 0:16], rhs=vv[cur][0:64, kvh * 128:(kvh + 1) * 128],
                                                start=False, stop=True)
                            S.pe(mmv, reads=[("pT", z, 0), ("pT", z, 1), ("vbf", u), ("vv", cur)], writes=[("PO", o4)])
                            S.act(lambda e, o4=o4, hk=hk, b=b, kvh=kvh: e.mul(out=OS[:, b, kvh, :], in_=POo[0:16, o4, :], mul=st[0:16, hk, 5:6]),
                                  reads=[("PO", o4), ("st", hk)], writes=["OS"])
                    for g in range(4):
                        for kvh in range(4):
                            S.dma("sp", MIX[f][0:64, 2048:4096].rearrange("(b t) (kvh g d) -> g kvh t b d", t=4, g=4, d=128)[g, kvh],
                                  OS[4 * g:4 * g + 4, :, kvh, :], reads=["OS"], writes=[("MIXs", f, g, kvh)])
            S.end_phase()

        with ExitStack() as ph:
            sb = lambda n, s, dt=F32: ph.enter_context(nc.sbuf_tensor("pE" + n, list(s), dt))
            ps = lambda n, s, dt=F32: ph.enter_context(nc.psum_tensor("psE" + n, list(s), dt))
            idf = sb("idf", [128, 128])
            idb = sb("idb", [128, 128], BF16)
            mT = sb("mT", [128, NFULL, KC, 128], BF16)
            mx = [sb("mx%d" % i, [128, D], BF16) for i in range(2)]
            GT = [sb("GT%d" % g, [128, D]) for g in range(2)]
            wb = [sb("w%d" % i, [128, KC, 256], BF16) for i in range(2)]
            xb = [sb("xb%d" % i, [128, 256]) for i in range(3)]
            tb = [sb("tb%d" % i, [128, 256]) for i in range(3)]
            ptr = [ps("ptr%d" % i, [128, 16, 128], BF16) for i in range(2)]
            pp = [ps("p%d" % i, [128, 512]) for i in range(3)]
            S.dma("sp", idf[:], ident_in, writes=["idf"])
            S.dve(lambda e: e.tensor_copy(out=idb[:], in_=idf[:]), reads=["idf"], writes=["idb"])
            for g in range(2):
                S.dma("sp", GT[g][:], MOD[g, :, 2 * D:3 * D], writes=[("GT", g)])
            npt = 0
            for f in range(NFULL):
                a = f % 2
                S.dma("sp", mx[a][:], MIX[f], writes=[("mx", a)])
                for half in range(2):
                    pj = npt % 2
                    npt += 1

                    def tr(e, a=a, half=half, pj=pj):
                        for k in range(16):
                            kk = half * 16 + k
                            ins = e.transpose(out=ptr[pj][:, k, :], in_=mx[a][:, kk * 128:(kk + 1) * 128], identity=idb[:])
                        return ins
                    S.pe(tr, reads=[("mx", a), "idb"], writes=[("ptr", pj)])
                    S.any(lambda e, ia, f=f, half=half, pj=pj: acopy(e, ia, mT[:, f, half * 16:(half + 1) * 16, :], ptr[pj][:]),
                          reads=[("ptr", pj)], writes=[("mT", f, half)])
            cnt = 0
            def ldwE(n):
                S.dma("pool", wb[n % 2][:], w_o[:, n * 256:(n + 1) * 256].rearrange("(k p) n -> p k n", p=128), writes=[("w", n % 2)])
            ldwE(0)
            for n in range(16):
                wi = n % 2
                cs = slice(n * 256, (n + 1) * 256)
                if n + 1 < 16:
                    ldwE(n + 1)
                for f in range(NFULL):
                    a = cnt % 3
                    cnt += 1
                    g = 1 if f == NFULL - 1 else 0
                    S.dma("sp", xb[a][:], xall[NPRE + f][:, cs], writes=[("xb", a)])

                    def mm(e, a=a, wi=wi, f=f):
                        for k in range(KC):
                            ins = e.matmul(pp[a][:, 0:256], lhsT=mT[:, f, k, :], rhs=wb[wi][:, k, :], start=(k == 0), stop=(k == KC - 1))
                        return ins
                    S.pe(mm, reads=[("mT", f, 0), ("mT", f, 1), ("w", wi)], writes=[("p", a)])
                    S.dve(lambda e, a=a, g=g, cs=cs: e.tensor_tensor(out=tb[a][:], in0=pp[a][:, 0:256], in1=GT[g][:, cs], op=ALU.mult),
                          reads=[("p", a), ("GT", g)], writes=[("tb", a)])
                    S.dve(lambda e, a=a: e.tensor_tensor(out=tb[a][:], in0=tb[a][:], in1=xb[a][:], op=ALU.add),
                          reads=[("tb", a), ("xb", a)], writes=[("tb", a)])
                    S.dma("pool", X1[f][:, cs], tb[a][:], reads=[("tb", a)], writes=[("X1", f, n)])
            S.end_phase()

        norm_phase("nE", [X1[f] for f in range(NFULL)], [1 if f == NFULL - 1 else 0 for f in range(NFULL)],
                   3 * D, 4 * D, 1, [H2T[f] for f in range(NFULL)])

        NTF = 1090
        TG = [(0, 512), (512, 512), (1024, 66)]
        with ExitStack() as ph:
            sb = lambda n, s, dt=F32: ph.enter_context(nc.sbuf_tensor("pF" + n, list(s), dt))
            ps = lambda n, s, dt=F32: ph.enter_context(nc.psum_tensor("psF" + n, list(s), dt))
            h2 = sb("h2", [128, KC, NTF], BF16)
            hm = sb("hm", [128, 1])
            wc = sb("wc", [128, NBLK, 4])
            cst = sb("cst", [128, NBLK, 32])
            wg = [sb("wg%d" % i, [128, KC, 128], BF16) for i in range(2)]
            wv = [sb("wv%d" % i, [128, KC, 128], BF16) for i in range(2)]
            U = [sb("U%d" % i, [128, 3, 512]) for i in range(2)]
            US = [sb("US%d" % i, [128, 16, 6]) for i in range(2)]
            acc = [sb("acc%d" % i, [128, 1088]) for i in range(2)]
            sgt = sb("sgt", [128, 1088])
            ao = [sb("ao%d" % i, [128, 1088], BF16) for i in range(2)]
            cn = [sb("cn%d" % i, [128, 34]) for i in range(2)]
            PU = [ps("U%d" % i, [128, 3, 512]) for i in range(2)]
            S.dma("sp", hm[:], hmask, writes=["hm"])
            S.dma("sp", wc[:], w_convT.rearrange("(b p) j -> p b j", p=128), writes=["wc"])
            S.dma("sp", cst[:], st_convT.rearrange("(b p) j -> p b j", p=128), writes=["cst"])
            for f in range(1, NFULL):
                ntok = 64 if f == NFULL - 1 else 128
                S.dma("sp", h2[:, :, 2 + (f - 1) * 128:2 + (f - 1) * 128 + ntok],
                      H2T[f].rearrange("p (k t) -> p k t", t=128)[:, :, 0:ntok], writes=[("h2", f)])
            S.dma("sp", h2[:, :, 0:2], H2T[0].rearrange("p (k t) -> p k t", t=128)[:, :, 126:128], writes=[("h2", 0)])
            S.dve(lambda e: e.tensor_scalar(out=h2[:, :, 0:2], in0=h2[:, :, 0:2], scalar1=hm[:, 0:1], scalar2=None, op0=ALU.mult),
                  reads=[("h2", 0), "hm"], writes=[("h2", 0)])
            h2keys = [("h2", f) for f in range(NFULL)]
            def ldwF(gi):
                S.dma("pool", wg[gi % 2][:], w_up[:, gi * 128:(gi + 1) * 128].rearrange("(k p) n -> p k n", p=128), writes=[("wg", gi % 2)])
                S.dma("pool", wv[gi % 2][:], w_up[:, DFF + gi * 128:DFF + (gi + 1) * 128].rearrange("(k p) n -> p k n", p=128),
                      writes=[("wv", gi % 2)])
            ldwF(0)
            for gi in range(86):
                wi = gi % 2
                if gi + 1 < 86:
                    ldwF(gi + 1)
                for sub in range(1):
                    blk = gi
                    ai = blk % 2
                    for which in range(2):
                        wt = wg if which == 0 else wv
                        bidx = blk if which == 0 else 86 + blk

                        def mm(e, wt=wt, wi=wi, sub=sub, which=which):
                            for tg, (t0, n) in enumerate(TG):
                                for k in range(KC):
                                    ins = e.matmul(PU[which][:, tg, 0:n], lhsT=wt[wi][:, k, sub * 128:(sub + 1) * 128],
                                                   rhs=h2[:, k, t0:t0 + n], start=(k == 0), stop=(k == KC - 1))
                            return ins
                        S.pe(mm, reads=h2keys + [("wg" if which == 0 else "wv", wi)], writes=[("PU", which)])
                        S.act(lambda e, which=which: e.activation(out=U[which][:], in_=PU[which][:], func=AF.Copy),
                              reads=[("PU", which)], writes=[("U", which)])
                        Uf = U[which][:].rearrange("p a b -> p (a b)")
                        S.dve(lambda e, which=which, Uf=Uf, bidx=bidx: e.tensor_scalar(out=acc[which][:, 0:1024], in0=Uf[:, 2:1026],
                                                                                      scalar1=wc[:, bidx, 2:3], scalar2=wc[:, bidx, 3:4],
                                                                                      op0=ALU.mult, op1=ALU.add),
                              reads=[("U", which), "wc"], writes=[("acc", which)])
                        S.dve(lambda e, which=which, Uf=Uf, bidx=bidx: e.scalar_tensor_tensor(out=acc[which][:, 0:1024], in0=Uf[:, 1:1025],
                                                                                             scalar=wc[:, bidx, 1:2], in1=acc[which][:, 0:1024],
                                                                                             op0=ALU.mult, op1=ALU.add),
                              reads=[("U", which), "wc", ("acc", which)], writes=[("acc", which)])
                        S.dve(lambda e, which=which, Uf=Uf, bidx=bidx: e.scalar_tensor_tensor(out=acc[which][:, 0:1024], in0=Uf[:, 0:1024],
                                                                                             scalar=wc[:, bidx, 0:1], in1=acc[which][:, 0:1024],
                                                                                             op0=ALU.mult, op1=ALU.add),
                              reads=[("U", which), "wc", ("acc", which)], writes=[("acc", which)])
                        S.act(lambda e, which=which, bidx=bidx: e.activation(out=US[which][:, :, 0:2],
                                                                            in_=cst[:, bidx, :].rearrange("p (b j) -> p b j", j=2), func=AF.Copy),
                              reads=["cst"], writes=[("US", which, 0)])
                        S.act(lambda e, which=which, Uf=Uf: e.activation(out=US[which][:, :, 2:6],
                                                                        in_=Uf[:, 1026:1090].rearrange("p (b t) -> p b t", t=4), func=AF.Copy),
                              reads=[("U", which)], writes=[("US", which, 1)])
                        accs = acc[which][:, 1024:1088].rearrange("p (b t) -> p b t", t=4)
                        S.dve(lambda e, which=which, bidx=bidx, accs=accs: e.tensor_scalar(out=accs, in0=US[which][:, :, 2:6],
                                                                                          scalar1=wc[:, bidx, 2:3], scalar2=wc[:, bidx, 3:4],
                                                                                          op0=ALU.mult, op1=ALU.add),
                              reads=[("US", which, 0), ("US", which, 1), "wc", ("acc", which)], writes=[("acc", which)])
                        S.dve(lambda e, which=which, bidx=bidx, accs=accs: e.scalar_tensor_tensor(out=accs, in0=US[which][:, :, 1:5],
                                                                                                 scalar=wc[:, bidx, 1:2], in1=accs,
                                                                                                 op0=ALU.mult, op1=ALU.add),
                              reads=[("US", which, 0), ("US", which, 1), "wc", ("acc", which)], writes=[("acc", which)])
                        S.dve(lambda e, which=which, bidx=bidx, accs=accs: e.scalar_tensor_tensor(out=accs, in0=US[which][:, :, 0:4],
                                                                                                 scalar=wc[:, bidx, 0:1], in1=accs,
                                                                                                 op0=ALU.mult, op1=ALU.add),
                              reads=[("US", which, 0), ("US", which, 1), "wc", ("acc", which)], writes=[("acc", which)])
                        ci = (blk * 2 + which) % 2
                        S.act(lambda e, ci=ci, Uf=Uf: e.activation(out=cn[ci][:, 0:2], in_=Uf[:, 1024:1026], func=AF.Copy),
                              reads=[("U", which)], writes=[("cn", ci, 0)])
                        S.act(lambda e, ci=ci, which=which: e.activation(out=cn[ci][:, 2:34].rearrange("p (b j) -> p b j", j=2),
                                                                        in_=US[which][:, :, 4:6], func=AF.Copy),
                              reads=[("US", which, 1)], writes=[("cn", ci, 1)])
                        S.dma("sp", o_convT[bidx * 128:(bidx + 1) * 128, :], cn[ci][:], reads=[("cn", ci, 0), ("cn", ci, 1)],
                              writes=[("oc", bidx)])
                    S.act(lambda e: e.activation(out=sgt[:], in_=acc[0][:], func=AF.Silu), reads=[("acc", 0)], writes=["sgt"])
                    S.dve(lambda e, ai=ai: e.tensor_tensor(out=ao[ai][:], in0=sgt[:], in1=acc[1][:], op=ALU.mult),
                          reads=["sgt", ("acc", 1)], writes=[("ao", ai)])
                    S.dma("sp", ACTT[0:8, :, blk, :].rearrange("t p c -> p t c"), ao[ai][:, 0:1024].rearrange("p (t c) -> p t c", c=128),
                          reads=[("ao", ai)], writes=[("ACTT", blk, 0)])
                    S.dma("sp", ACTT[8, :, blk, 0:64], ao[ai][:, 1024:1088], reads=[("ao", ai)], writes=[("ACTT", blk, 1)])
            S.end_phase()

        with ExitStack() as ph:
            sb = lambda n, s, dt=F32: ph.enter_context(nc.sbuf_tensor("pG" + n, list(s), dt))
            ps = lambda n, s, dt=F32: ph.enter_context(nc.psum_tensor("psG" + n, list(s), dt))
            wd = [sb("wd%d" % i, [128, 86, 256], BF16) for i in range(2)]
            at = [sb("at%d" % i, [128, 86, 128], BF16) for i in range(2)]
            GT = [sb("GT%d" % g, [128, D]) for g in range(2)]
            xb = [sb("xb%d" % i, [128, 256]) for i in range(3)]
            tb = [sb("tb%d" % i, [128, 256]) for i in range(3)]
            pp = [ps("p%d" % i, [128, 512]) for i in range(3)]
            for g in range(2):
                S.dma("sp", GT[g][:], MOD[g, :, 5 * D:6 * D], writes=[("GT", g)])
            cnt = 0
            def ldwG(n):
                S.dma("pool", wd[n % 2][:], w_down[:, n * 256:(n + 1) * 256].rearrange("(k p) n -> p k n", p=128), writes=[("wd", n % 2)])
            def g_loads(idx):
                n_, tt_ = idx // 9, idx % 9
                S.dma("sp", at[idx % 2][:, 0:43, :], ACTT[tt_][:, 0:43, :], writes=[("at", idx % 2, 0)])
                S.dma("pool", at[idx % 2][:, 43:86, :], ACTT[tt_][:, 43:86, :], writes=[("at", idx % 2, 1)])
                S.dma("sp", xb[idx % 3][:], X1[tt_ + 1][:, n_ * 256:(n_ + 1) * 256], writes=[("xb", idx % 3)])
            ldwG(0)
            g_loads(0)
            for n in range(16):
                wi = n % 2
                cs = slice(n * 256, (n + 1) * 256)
                if n + 1 < 16:
                    ldwG(n + 1)
                for tt in range(9):
                    a = cnt % 3
                    a2 = cnt % 2
                    cnt += 1
                    g = 1 if tt == 8 else 0
                    m = 64 if tt == 8 else 128
                    if cnt < 16 * 9:
                        g_loads(cnt)

                    def mm(e, a=a, a2=a2, wi=wi, m=m):
                        for k in range(86):
                            ins = e.matmul(pp[a][0:m, 0:256], lhsT=at[a2][:, k, 0:m], rhs=wd[wi][:, k, :], start=(k == 0), stop=(k == 85))
                        return ins
                    S.pe(mm, reads=[("at", a2, 0), ("at", a2, 1), ("wd", wi)], writes=[("p", a)])
                    S.dve(lambda e, a=a, g=g, cs=cs, m=m: e.tensor_tensor(out=tb[a][0:m, :], in0=pp[a][0:m, 0:256], in1=GT[g][0:m, cs], op=ALU.mult),
                          reads=[("p", a), ("GT", g)], writes=[("tb", a)])
                    S.dve(lambda e, a=a, m=m: e.tensor_tensor(out=tb[a][0:m, :], in0=tb[a][0:m, :], in1=xb[a][0:m, :], op=ALU.add),
                          reads=[("tb", a), ("xb", a)], writes=[("tb", a)])
                    S.dma("pool", X2[tt][0:m, cs], tb[a][0:m, :], reads=[("tb", a)], writes=[("X2", tt, n)])
            S.end_phase()

        with ExitStack() as ph:
            sb = lambda n, s, dt=F32: ph.enter_context(nc.sbuf_tensor("pH" + n, list(s), dt))
            gf = sb("gf", [128, D])
            epsb = sb("eps", [128, 1])
            ssq = sb("ssq", [128, 18])
            xt = [sb("xt%d" % i, [128, D]) for i in range(3)]
            yo = [sb("yo%d" % i, [128, D]) for i in range(2)]
            junk = sb("junk", [128, D], BF16)
            S.dma("sp", gf[:], gf_bc, writes=["gf"])
            S.dve(lambda e: e.memset(epsb[:], EPS), writes=["eps"])
            S.dve(lambda e: e.memset(ssq[:], 0.0), writes=["ssq"])
            for tt in range(9):
                a = tt % 3
                b2 = tt % 2
                m = 64 if tt == 8 else 128
                S.dma("sp", xt[a][0:m, :], X2[tt][0:m, :], writes=[("xt", a)])
                S.act(lambda e, a=a, tt=tt, m=m: e.activation(out=junk[0:m, :], in_=xt[a][0:m, :], func=AF.Square,
                                                             accum_out=ssq[0:m, 2 * tt:2 * tt + 1]),
                      reads=[("xt", a), "ssq"], writes=["junk", ("ssq", tt)])
                S.act(lambda e, tt=tt, m=m: e.activation(out=ssq[0:m, 2 * tt + 1:2 * tt + 2], in_=ssq[0:m, 2 * tt:2 * tt + 1],
                                                        func=AF.Sqrt, bias=epsb[0:m, 0:1], scale=1.0 / D),
                      reads=[("ssq", tt), "eps"], writes=[("ssq2", tt)])
                S.dve(lambda e, tt=tt, m=m: e.reciprocal(out=ssq[0:m, 2 * tt + 1:2 * tt + 2], in_=ssq[0:m, 2 * tt + 1:2 * tt + 2]),
                      reads=[("ssq2", tt)], writes=[("ssq2", tt)])
                S.dve(lambda e, a=a, b2=b2, tt=tt, m=m: e.scalar_tensor_tensor(out=yo[b2][0:m, :], in0=xt[a][0:m, :],
                                                                              scalar=ssq[0:m, 2 * tt + 1:2 * tt + 2], in1=gf[0:m, :],
                                                                              op0=ALU.mult, op1=ALU.mult),
                      reads=[("xt", a), ("ssq2", tt), "gf"], writes=[("yo", b2)])
                dst = y_s if tt == 8 else y_p[tt]
                S.dma("pool", dst, yo[b2][0:m, :], reads=[("yo", b2)], writes=[("y", tt)])
            S.end_phase()
    return nc


def _constants():
    p = np.arange(128)
    same64 = (p[:, None] // 64) == (p[None, :] // 64)
    TIp = (same64 & (p[:, None] <= p[None, :])).astype(np.float32)
    Up = (same64 & (p[:, None] > p[None, :])).astype(np.float32)
    val = (p[:, None] < 64) & (p[None, :] < 64)
    same4 = ((p[:, None] // 4) == (p[None, :] // 4)) & val
    TIs = (same4 & (p[:, None] <= p[None, :])).astype(np.float32)
    Us = (same4 & (p[:, None] > p[None, :])).astype(np.float32)
    U128 = (p[:, None] > p[None, :]).astype(np.float32)
    cmat = np.stack([-TIp / 16.0, -Up / 16.0, -TIs / 16.0, -Us / 16.0, -U128 / 16.0], axis=1).astype(np.float32)
    ti4 = np.stack([np.repeat(TIp[:, None, :], 4, axis=1), np.repeat(TIs[:, None, :], 4, axis=1)], axis=1).astype(np.float32)
    negcol = np.full((128, 1), -1.0 / 16.0, np.float32)
    bm = ((p[:, None] // 4) == np.arange(16)[None, :]) & (p[:, None] < 64)
    bmask = bm.astype(np.float32)
    bsel16 = (-bmask / 16.0).astype(np.float32)
    slopes = np.exp2(-8.0 * np.arange(1, 17, dtype=np.float32) / 16.0).astype(np.float32)
    i = np.arange(128)[:, None]
    j = np.arange(256)[None, :]
    d = (i + 128 - j).astype(np.float32)
    valid = (d >= 0) & (d <= 128)
    bias_gen = np.where(valid[:, None, :], -slopes[None, :, None] * d[:, None, :], NEG).astype(np.float32)
    bias_first = bias_gen.copy()
    bias_first[:, :, 0:128] = NEG
    r = np.arange(16)
    g_ = r // 4
    t_ = r % 4
    jb = np.arange(128)
    bias_sb = np.zeros((16, 4, 128), np.float32)
    bias_sn = np.full((16, 4, 16, 64), NEG, np.float32)
    for kvh in range(4):
        sl = slopes[kvh * 4 + g_]
        dd = (t_[:, None] + 128 - jb[None, :]).astype(np.float32)
        ok = (dd >= 0) & (dd <= 128)
        bias_sb[:, kvh, :] = np.where(ok, -sl[:, None] * dd, NEG)
        for b in range(16):
            for tp in range(4):
                dn = (t_ - tp).astype(np.float32)
                okn = dn >= 0
                bias_sn[:, kvh, b, 4 * b + tp] = np.where(okn, -sl * dn, NEG)
    return dict(ident=np.eye(128, dtype=np.float32), cmat=cmat, ti4=ti4, negcol=negcol, bsel16=bsel16, bmask=bmask,
                bias_gen=bias_gen, bias_first=bias_first, bias_sb=bias_sb, bias_sn=bias_sn)


_NC_CACHE = {}


def kernel(x_prompt, x_sample, c_prompt, c_sample, state_gla, state_swa_k, state_swa_v,
           state_ffn_conv, w_ada, b_ada, g_norm, w_in, w_a_up, b_a, g_gla, swa_sinks,
           w_o, w_up, w_conv, b_conv, w_down, g_final):
    f32 = np.float32
    A = lambda a: np.ascontiguousarray(np.asarray(a, dtype=f32))
    x_prompt, x_sample, c_prompt, c_sample = A(x_prompt), A(x_sample), A(c_prompt), A(c_sample)
    state_gla, state_swa_k, state_swa_v, state_ffn_conv = A(state_gla), A(state_swa_k), A(state_swa_v), A(state_ffn_conv)
    w_ada, b_ada, g_norm, w_in, w_a_up, b_a = A(w_ada)[0], A(b_ada)[0], A(g_norm)[0], A(w_in)[0], A(w_a_up)[0], A(b_a)[0]
    g_gla, swa_sinks, w_o, w_up, w_conv, b_conv, w_down, g_final = (A(g_gla)[0], A(swa_sinks)[0], A(w_o)[0], A(w_up)[0],
                                                                    A(w_conv)[0], A(b_conv)[0], A(w_down)[0], A(g_final))
    C = _constants()
    rep = lambda v, n=128: np.ascontiguousarray(np.broadcast_to(v[None], (n,) + v.shape))
    shared = dict(
        w_ada=w_ada, bada_bc=rep(b_ada), gn_bc=rep(g_norm), w_in=w_in,
        w17=np.ascontiguousarray(np.concatenate([w_a_up, b_a[None, :]], axis=0)),
        ggla_bc=rep(g_gla), sink_bc=rep(swa_sinks),
        sink_s=np.ascontiguousarray(swa_sinks.reshape(4, 4)[:, np.arange(16) // 4].T),
        w_o=w_o, w_up=w_up,
        w_convT=np.ascontiguousarray(np.concatenate([w_conv, b_conv[None, :]], axis=0).T),
        w_down=w_down, gf_bc=rep(g_final),
        ident=C["ident"], cmat=C["cmat"], ti4=C["ti4"], negcol=C["negcol"], bsel16=C["bsel16"], bmask=C["bmask"],
        bias_sb=C["bias_sb"], bias_sn=C["bias_sn"],
    )
    xp = x_prompt[0]
    xpad = np.concatenate([np.zeros((7168, D), f32), xp], axis=0)
    in_maps = []
    for c in range(NCORES):
        start = 1024 * c
        xa = np.zeros((NT, 128, D), f32)
        xa[0:64] = xpad[start:start + 8192].reshape(64, 128, D)
        xa[64, 0:64] = x_sample[16 * c:16 * c + 16].reshape(64, D)
        pm = np.zeros((128, NT), f32)
        for t in range(64):
            g0 = start - 7168 + t * 128
            pm[:, t] = 1.0 if g0 >= 0 else 0.0
        pm[0:64, 64] = 1.0
        cT = np.zeros((128, KC, 256), f32)
        cT[:, :, 0:128] = c_prompt[0].reshape(KC, 128).T[:, :, None]
        cs = np.repeat(c_sample[16 * c:16 * c + 16], 4, axis=0)
        cT[:, :, 128:192] = cs.reshape(64, KC, 128).transpose(2, 1, 0)
        bias_p = np.stack([C["bias_first"] if c == 0 else C["bias_gen"], C["bias_gen"]], axis=1)
        sk = state_swa_k[0, 16 * c:16 * c + 16]
        sv = state_swa_v[0, 16 * c:16 * c + 16]
        m = dict(shared)
        m.update(
            xall=xa, cT=cT, st_gla=np.ascontiguousarray(state_gla[0, 16 * c:16 * c + 16]),
            st_k=np.ascontiguousarray(sk.reshape(16, 128, 512)), st_v=np.ascontiguousarray(sv.reshape(16, 128, 512)),
            st_kT=np.ascontiguousarray(sk.transpose(0, 2, 3, 1)),
            st_convT=np.ascontiguousarray(state_ffn_conv[0, 16 * c:16 * c + 16].reshape(32, F2).T),
            pmask=pm, hmask=np.full((128, 1), 0.0 if c == 0 else 1.0, f32), bias_p=np.ascontiguousarray(bias_p),
        )
        in_maps.append(m)
    if "nc" not in _NC_CACHE:
        _NC_CACHE["nc"] = build_program()
    nc = _NC_CACHE["nc"]
    res = run_bass_kernel_spmd(nc, in_maps, core_ids=list(range(NCORES)))
    R = res.results
    y_prompt = np.concatenate([R[c]["y_p"].reshape(1024, D) for c in range(NCORES)], axis=0)[None]
    y_sample = np.concatenate([R[c]["y_s"] for c in range(NCORES)], axis=0).reshape(128, 4, D)
    gla_p = R[7]["o_gla_p"].reshape(1, 1, 4, 256, 512)
    k_p = R[7]["o_k_p"].reshape(1, 1, 128, 4, 128)
    v_p = R[7]["o_v_p"].reshape(1, 1, 128, 4, 128)
    conv_p = np.ascontiguousarray(R[7]["o_convT"][:, 0:2].T).reshape(1, 1, 2, F2)
    gla_s = np.concatenate([R[c]["o_gla_s"] for c in range(NCORES)], axis=0)[None]
    k_s = np.concatenate([R[c]["o_k_s"] for c in range(NCORES)], axis=0).reshape(1, 128, 128, 4, 128)
    v_s = np.concatenate([R[c]["o_v_s"] for c in range(NCORES)], axis=0).reshape(1, 128, 128, 4, 128)
    conv_s = np.concatenate([R[c]["o_convT"][:, 2:34].T.reshape(16, 2, F2) for c in range(NCORES)], axis=0)[None]
    outs = (y_prompt, y_sample, gla_p, k_p, v_p, conv_p, gla_s, k_s, v_s, conv_s)
    return tuple(np.ascontiguousarray(o, dtype=f32) for o in outs)
```

```python
import numpy as np
from contextlib import ExitStack
import concourse.bass as bass
import concourse.mybir as mybir
from concourse.bass_utils import run_bass_kernel_spmd

F32 = mybir.dt.float32
BF16 = mybir.dt.bfloat16
AF = mybir.ActivationFunctionType
ALU = mybir.AluOpType
AX = mybir.AxisListType

NCORES = 8
D = 4096
KC = 32
NPRE = 55
NFULL = 10
NT = 65
INW = 9232
F2 = 22016
DFF = 11008
NBLK = 172
OFF_GQ, OFF_GK, OFF_GV, OFF_GR, OFF_GA, OFF_SQ, OFF_SK, OFF_SV = 0, 1024, 2048, 4096, 6144, 6160, 8208, 8720
PW = 3088
NEG = -30000.0
EPS = 1e-6

COMPUTE = ("pe", "act", "dve")
QUEUES = ("sp", "pool", "aq")
ALLENG = COMPUTE + QUEUES


class Sched:
    def __init__(self, nc, stack, ndma=8):
        self.nc = nc
        self.sems = []
        self.eng_sem = {e: self._mk(stack, "s_" + e) for e in COMPUTE}
        self.eng_cnt = {e: 0 for e in COMPUTE}
        self.dma_pool = {q: [self._mk(stack, "d_%s%d" % (q, i)) for i in range(ndma)] for q in QUEUES}
        self.dma_n = {q: 0 for q in QUEUES}
        self.ops = {e: [] for e in ALLENG}
        self.known = {e: {} for e in ALLENG}
        self.lastw = {}
        self.reads = {}
        self.flip = 0
        self.seq = 0

    def _mk(self, stack, name):
        s = stack.enter_context(self.nc.semaphore(name))
        self.sems.append(s)
        return len(self.sems) - 1

    def add(self, eng, fn, reads=(), writes=()):
        need = {}

        def dep(tok, kind):
            teng, si, val = tok
            if teng == eng and eng in COMPUTE:
                if eng == "pe" or kind != "raw":
                    return
            need[si] = max(need.get(si, 0), val)

        for b in reads:
            w = self.lastw.get(b)
            if w is not None:
                dep(w, "raw")
        for b in writes:
            w = self.lastw.get(b)
            if w is not None:
                dep(w, "waw")
            for r in self.reads.get(b, ()):
                dep(r, "war")
        if eng in COMPUTE:
            self.eng_cnt[eng] += 1
            si = self.eng_sem[eng]
            tok = (eng, si, self.eng_cnt[eng])
            inc = (si, 1)
        else:
            n = self.dma_n[eng]
            pool = self.dma_pool[eng]
            P = len(pool)
            si = pool[n % P]
            val = 16 * (n // P + 1)
            if n >= P:
                need[si] = max(need.get(si, 0), 16 * (n // P))
            tok = (eng, si, val)
            inc = (si, 16)
            self.dma_n[eng] += 1
        kn = self.known[eng]
        waits = []
        for si, val in need.items():
            if kn.get(si, 0) >= val:
                continue
            kn[si] = val
            waits.append((si, val))
        self.seq += 1
        self.ops[eng].append((waits, fn, inc, self.seq))
        for b in writes:
            self.lastw[b] = tok
            self.reads[b] = []
        for b in reads:
            if b in writes:
                continue
            lst = self.reads.setdefault(b, [])
            if eng in COMPUTE:
                lst[:] = [t for t in lst if t[0] != eng]
            lst.append(tok)
        return tok

    def pe(self, fn, reads=(), writes=()):
        return self.add("pe", fn, reads, writes)

    def act(self, fn, reads=(), writes=()):
        return self.add("act", fn, reads, writes)

    def dve(self, fn, reads=(), writes=()):
        return self.add("dve", fn, reads, writes)

    def any(self, fn, reads=(), writes=()):
        self.flip ^= 1
        if self.flip:
            return self.add("act", lambda e: fn(e, True), reads, writes)
        return self.add("dve", lambda e: fn(e, False), reads, writes)

    def dma(self, q, out, in_, reads=(), writes=()):
        return self.add(q, lambda e: e.dma_start(out=out, in_=in_), reads, writes)

    def barrier(self):
        targets = []
        for e in COMPUTE:
            if self.eng_cnt[e] > 0:
                targets.append((self.eng_sem[e], self.eng_cnt[e]))
        for q in QUEUES:
            pool = self.dma_pool[q]
            P = len(pool)
            n = self.dma_n[q]
            for i, si in enumerate(pool):
                cnt = (n - i + P - 1) // P if n > i else 0
                if cnt > 0:
                    targets.append((si, 16 * cnt))
        for eng in ALLENG:
            kn = self.known[eng]
            waits = []
            for si, val in targets:
                if kn.get(si, 0) >= val:
                    continue
                kn[si] = val
                waits.append((si, val))
            if waits:
                self.seq += 1
                self.ops[eng].append((waits, None, None, self.seq))
        self.lastw = {}
        self.reads = {}

    def emit(self):
        nc = self.nc
        sems = self.sems
        ops = self.ops

        def replay(name, eng):
            lst = ops[name]
            if name == "act":
                lst = sorted(ops["act"] + ops["aq"], key=lambda o: o[3])
            for waits, fn, inc, _ in lst:
                for si, val in waits:
                    eng.wait_ge(sems[si], val)
                if fn is not None:
                    ins = fn(eng)
                    ins.then_inc(sems[inc[0]], inc[1])

        with nc.Block() as block:
            @block.tensor
            def _(e):
                replay("pe", e)

            @block.scalar
            def _(e):
                replay("act", e)

            @block.vector
            def _(e):
                replay("dve", e)

            @block.sync
            def _(e):
                replay("sp", e)

            @block.gpsimd
            def _(e):
                replay("pool", e)
        self.ops = {e: [] for e in ALLENG}

    def end_phase(self):
        self.barrier()
        self.emit()


def acopy(e, is_act, out, in_):
    if is_act:
        return e.activation(out=out, in_=in_, func=AF.Copy)
    return e.tensor_copy(out=out, in_=in_)


def build_program():
    nc = bass.Bass("TRN2", target_bir_lowering=False)

    def din(name, shape):
        return nc.dram_tensor(name, list(shape), F32, kind="ExternalInput").ap()

    def dout(name, shape):
        return nc.dram_tensor(name, list(shape), F32, kind="ExternalOutput").ap()

    def dscr(name, shape, dt=F32):
        return nc.dram_tensor(name, list(shape), dt).ap()

    xall = din("xall", [NT, 128, D])
    cT_in = din("cT", [128, KC, 256])
    st_gla = din("st_gla", [16, 4, 256, 512])
    st_k = din("st_k", [16, 128, 512])
    st_v = din("st_v", [16, 128, 512])
    st_kT = din("st_kT", [16, 4, 128, 128])
    st_convT = din("st_convT", [F2, 32])
    w_ada = din("w_ada", [D, 6 * D])
    bada_bc = din("bada_bc", [128, 6 * D])
    gn_bc = din("gn_bc", [128, 2, D])
    w_in = din("w_in", [D, INW])
    w17 = din("w17", [17, 1024])
    ggla_bc = din("ggla_bc", [128, 512])
    sink_bc = din("sink_bc", [128, 16])
    sink_s = din("sink_s", [16, 4])
    w_o = din("w_o", [D, D])
    w_up = din("w_up", [D, F2])
    w_convT = din("w_convT", [F2, 4])
    w_down = din("w_down", [DFF, D])
    gf_bc = din("gf_bc", [128, D])
    ident_in = din("ident", [128, 128])
    cmat = din("cmat", [128, 5, 128])
    ti4 = din("ti4", [128, 2, 4, 128])
    negcol = din("negcol", [128, 1])
    bsel16 = din("bsel16", [128, 16])
    bmask = din("bmask", [128, 16])
    pmask = din("pmask", [128, NT])
    hmask = din("hmask", [128, 1])
    bias_p = din("bias_p", [128, 2, 16, 256])
    bias_sb = din("bias_sb", [16, 4, 128])
    bias_sn = din("bias_sn", [16, 4, 16, 64])

    y_p = dout("y_p", [8, 128, D])
    y_s = dout("y_s", [64, D])
    o_gla_p = dout("o_gla_p", [4, 256, 512])
    o_k_p = dout("o_k_p", [128, 512])
    o_v_p = dout("o_v_p", [128, 512])
    o_convT = dout("o_convT", [F2, 34])
    o_gla_s = dout("o_gla_s", [16, 4, 256, 512])
    o_k_s = dout("o_k_s", [16, 128, 512])
    o_v_s = dout("o_v_s", [16, 128, 512])

    MOD = dscr("MOD", [2, 128, 6 * D])
    HT = dscr("HT", [NT, 128, KC * 128], BF16)
    PP = dscr("PP", [NPRE, 128, PW])
    PH2 = dscr("PH2", [128, 1024])
    PF = dscr("PF", [NFULL, 128, INW])
    MIX = dscr("MIX", [NFULL, 128, D], BF16)
    X1 = dscr("X1", [NFULL, 128, D])
    H2T = dscr("H2T", [NFULL, 128, KC * 128], BF16)
    ACTT = dscr("ACTT", [9, 128, 86, 128], BF16)
    X2 = dscr("X2", [9, 128, D])
    SCB = dscr("SCB", [128, KC * 256], BF16)

    with ExitStack() as top:
        S = Sched(nc, top)

        with ExitStack() as ph:
            sb = lambda n, s, dt=F32: ph.enter_context(nc.sbuf_tensor(n, list(s), dt))
            ps = lambda n, s, dt=F32: ph.enter_context(nc.psum_tensor(n, list(s), dt))
            cTs = sb("cTs", [128, KC, 256])
            scb = sb("scb", [128, KC, 256], BF16)
            wb = [sb("wada%d" % i, [128, KC, 512], BF16) for i in range(2)]
            bb = [sb("bada%d" % i, [128, 512]) for i in range(2)]
            mo = [sb("mo%d" % i, [128, 2, 512]) for i in range(2)]
            pm = [ps("pm%d" % i, [128, 2, 512]) for i in range(2)]
            S.dma("sp", cTs[:], cT_in, writes=["cTs"])
            S.act(lambda e: e.activation(out=scb[:], in_=cTs[:], func=AF.Silu), reads=["cTs"], writes=["scb"])
            def ldw0(n):
                S.dma("pool", wb[n % 2][:], w_ada[:, n * 512:(n + 1) * 512].rearrange("(k p) n -> p k n", p=128), writes=[("w", n % 2)])
            S.dma("pool", SCB, scb[:].rearrange("p k m -> p (k m)"), reads=["scb"], writes=["SCB"])
            ldw0(0)
            for n in range(16):
                i = n % 2
                cs = slice(n * 512, (n + 1) * 512)
                if n + 1 < 16:
                    ldw0(n + 1)
                S.dma("sp", bb[i][:], bada_bc[:, cs], writes=[("b", i)])

                def mm(e, i=i):
                    for g in range(2):
                        for k in range(KC):
                            ins = e.matmul(pm[i][:, g, :], lhsT=scb[:, k, g * 128:(g + 1) * 128], rhs=wb[i][:, k, :],
                                           start=(k == 0), stop=(k == KC - 1))
                    return ins
                S.pe(mm, reads=["scb", ("w", i)], writes=[("pm", i)])
                for g in range(2):
                    S.dve(lambda e, i=i, g=g: e.tensor_tensor(out=mo[i][:, g, :], in0=pm[i][:, g, :], in1=bb[i][:], op=ALU.add),
                          reads=[("pm", i), ("b", i)], writes=[("mo", i, g)])
                S.dma("pool", MOD[:, :, cs].rearrange("g p n -> p g n"), mo[i][:],
                      reads=[("mo", i, 0), ("mo", i, 1)], writes=[("MOD", n)])
            S.end_phase()

        def norm_phase(tag, src_tiles, mod_sel, off_sh, off_sc, gidx, dst):
            with ExitStack() as ph:
                sb = lambda n, s, dt=F32: ph.enter_context(nc.sbuf_tensor(tag + n, list(s), dt))
                ps = lambda n, s, dt=F32: ph.enter_context(nc.psum_tensor(tag + "ps" + n, list(s), dt))
                G = [sb("G%d" % g, [128, D]) for g in range(2)]
                SH = [sb("SH%d" % g, [128, D]) for g in range(2)]
                gn = sb("gn", [128, D])
                idf = sb("idf", [128, 128])
                idb = sb("idb", [128, 128], BF16)
                epsb = sb("eps", [128, 1])
                ssq = sb("ssq", [128, 2 * len(src_tiles)])
                xt = [sb("xt%d" % i, [128, D]) for i in range(3)]
                junk = sb("junk", [128, D], BF16)
                hb = [sb("hb%d" % i, [128, D], BF16) for i in range(2)]
                hT = [sb("hT%d" % i, [128, KC, 128], BF16) for i in range(2)]
                ptr = [ps("ptr%d" % i, [128, 16, 128], BF16) for i in range(3)]
                S.dma("sp", gn[:], gn_bc[:, gidx, :], writes=["gn"])
                S.dma("sp", idf[:], ident_in, writes=["idf"])
                S.dve(lambda e: e.tensor_copy(out=idb[:], in_=idf[:]), reads=["idf"], writes=["idb"])
                S.dve(lambda e: e.memset(epsb[:], EPS), writes=["eps"])
                S.dve(lambda e: e.memset(ssq[:], 0.0), writes=["ssq"])
                for g in range(2):
                    S.dma("sp", SH[g][:], MOD[g, :, off_sh:off_sh + D], writes=[("SH", g)])
                    S.dma("sp", G[g][:], MOD[g, :, off_sc:off_sc + D], writes=[("G", g)])
                    S.dve(lambda e, g=g: e.scalar_tensor_tensor(out=G[g][:], in0=G[g][:], scalar=1.0, in1=gn[:],
                                                                op0=ALU.add, op1=ALU.mult),
                          reads=[("G", g), "gn"], writes=[("G", g)])
                nptr = 0
                for i, src in enumerate(src_tiles):
                    g = mod_sel[i]
                    a = i % 3
                    b2 = i % 2
                    S.dma("sp", xt[a][:], src, writes=[("xt", a)])
                    S.act(lambda e, a=a, i=i: e.activation(out=junk[:], in_=xt[a][:], func=AF.Square,
                                                           accum_out=ssq[:, 2 * i:2 * i + 1]),
                          reads=[("xt", a), "ssq"], writes=["junk", ("ssq", i)])
                    S.act(lambda e, i=i: e.activation(out=ssq[:, 2 * i + 1:2 * i + 2], in_=ssq[:, 2 * i:2 * i + 1],
                                                      func=AF.Sqrt, bias=epsb[:, 0:1], scale=1.0 / D),
                          reads=[("ssq", i), "eps"], writes=[("ssq2", i)])
                    S.dve(lambda e, i=i: e.reciprocal(out=ssq[:, 2 * i + 1:2 * i + 2], in_=ssq[:, 2 * i + 1:2 * i + 2]),
                          reads=[("ssq2", i)], writes=[("ssq2", i)])
                    S.dve(lambda e, a=a, b2=b2, i=i, g=g: e.scalar_tensor_tensor(
                        out=xt[a][:], in0=xt[a][:], scalar=ssq[:, 2 * i + 1:2 * i + 2], in1=G[g][:],
                        op0=ALU.mult, op1=ALU.mult),
                        reads=[("xt", a), ("ssq2", i), ("G", g)], writes=[("xt", a)])
                    S.dve(lambda e, a=a, b2=b2, g=g: e.tensor_tensor(out=hb[b2][:], in0=xt[a][:], in1=SH[g][:], op=ALU.add),
                          reads=[("xt", a), ("SH", g)], writes=[("hb", b2)])
                    for half in range(2):
                        pj = nptr % 3
                        nptr += 1

                        def tr(e, b2=b2, half=half, pj=pj):
                            for k in range(16):
                                kk = half * 16 + k
                                ins = e.transpose(out=ptr[pj][:, k, :], in_=hb[b2][:, kk * 128:(kk + 1) * 128], identity=idb[:])
                            return ins
                        S.pe(tr, reads=[("hb", b2), "idb"], writes=[("ptr", pj)])
                        S.any(lambda e, ia, b2=b2, half=half, pj=pj: acopy(e, ia, hT[b2][:, half * 16:(half + 1) * 16, :], ptr[pj][:]),
                              reads=[("ptr", pj)], writes=[("hT", b2, half)])
                    S.dma("pool", dst[i], hT[b2][:].rearrange("p k t -> p (k t)"),
                          reads=[("hT", b2, 0), ("hT", b2, 1)], writes=[("dst", i)])
                S.end_phase()

        norm_phase("nA", [xall[i] for i in range(NT)], [1 if i == NT - 1 else 0 for i in range(NT)],
                   0, D, 0, [HT[i] for i in range(NT)])

        ALLT = list(range(NT))
        FULLT = list(range(NPRE, NT))

        def pdest(t, c0, n):
            if t >= NPRE:
                return PF[t - NPRE][:, c0:c0 + n]
            if c0 >= OFF_SK:
                return PH2[:, c0 - OFF_SK:c0 - OFF_SK + n]
            if c0 == OFF_GA:
                return PP[t][:, 3072:3088]
            return PP[t][:, c0 - OFF_GK:c0 - OFF_GK + n]

        blocks = []
        for j in range(2):
            blocks.append((OFF_GQ + j * 512, 512, FULLT))
        for j in range(2):
            blocks.append((OFF_GK + j * 512, 512, ALLT))
        for j in range(4):
            blocks.append((OFF_GV + j * 512, 512, ALLT))
        for j in range(4):
            blocks.append((OFF_GR + j * 512, 512, FULLT))
        blocks.append((OFF_GA, 16, ALLT))
        for j in range(4):
            blocks.append((OFF_SQ + j * 512, 512, FULLT))
        blocks.append((OFF_SK, 512, [NPRE - 1] + FULLT))
        blocks.append((OFF_SV, 512, [NPRE - 1] + FULLT))

        with ExitStack() as ph:
            sb = lambda n, s, dt=F32: ph.enter_context(nc.sbuf_tensor("pB" + n, list(s), dt))
            ps = lambda n, s, dt=F32: ph.enter_context(nc.psum_tensor("psB" + n, list(s), dt))
            wb = [sb("w%d" % i, [128, KC, 512], BF16) for i in range(2)]
            hts = [sb("h%d" % i, [128, KC, 128], BF16) for i in range(4)]
            ob = [sb("o%d" % i, [128, 512]) for i in range(4)]
            pp = [ps("p%d" % i, [128, 512]) for i in range(4)]
            cnt = 0
            def ldwB(bi):
                c0, n, tl = blocks[bi]
                S.dma("pool", wb[bi % 2][:, :, 0:n], w_in[:, c0:c0 + n].rearrange("(k p) n -> p k n", p=128), writes=[("w", bi % 2)])
            ldwB(0)
            for bi, (c0, n, tl) in enumerate(blocks):
                wi = bi % 2
                if bi + 1 < len(blocks):
                    ldwB(bi + 1)
                for t in tl:
                    a = cnt % 4
                    cnt += 1
                    S.dma("sp", hts[a][:].rearrange("p k t -> p (k t)"), HT[t], writes=[("h", a)])

                    def mm(e, a=a, wi=wi, n=n):
                        for k in range(KC):
                            ins = e.matmul(pp[a][:, 0:n], lhsT=hts[a][:, k, :], rhs=wb[wi][:, k, 0:n],
                                           start=(k == 0), stop=(k == KC - 1))
                        return ins
                    S.pe(mm, reads=[("h", a), ("w", wi)], writes=[("p", a)])
                    S.any(lambda e, ia, a=a, n=n: acopy(e, ia, ob[a][:, 0:n], pp[a][:, 0:n]), reads=[("p", a)], writes=[("o", a)])
                    S.dma("pool", pdest(t, c0, n), ob[a][:, 0:n], reads=[("o", a)], writes=[("pd", t, c0)])
            S.end_phase()

        with ExitStack() as ph:
            sb = lambda n, s, dt=F32: ph.enter_context(nc.sbuf_tensor("pC" + n, list(s), dt))
            ps = lambda n, s, dt=F32: ph.enter_context(nc.psum_tensor("psC" + n, list(s), dt))
            idf = sb("idf", [128, 128])
            cm = sb("cm", [128, 5, 128])
            t4 = sb("t4", [128, 2, 4, 128])
            ncol = sb("ncol", [128, 1])
            bs16 = sb("bs16", [128, 16])
            bmk = sb("bmk", [128, 16])
            pmk = sb("pmk", [128, NT])
            w17s = sb("w17", [32, 1024])
            gg = sb("gg", [128, 512])
            one1 = sb("one1", [128, 1])
            epsb = sb("eps", [128, 1])
            St = sb("S", [128, 8, 512])
            Sb = sb("Sb", [128, 8, 512], BF16)
            kt = [sb("k%d" % i, [128, 1024]) for i in range(2)]
            vt = [sb("v%d" % i, [128, 2048]) for i in range(2)]
            gat = [sb("ga%d" % i, [128, 16]) for i in range(2)]
            qt = [sb("q%d" % i, [128, 1024]) for i in range(2)]
            rt = [sb("r%d" % i, [128, 2048]) for i in range(2)]
            ga17 = [sb("g17%d" % i, [32, 128]) for i in range(2)]
            e1 = sb("e1", [128, 1024])
            la2 = [sb("la%d" % i, [128, 1024]) for i in range(2)]
            er = sb("er", [128, 1024])
            kd2 = [sb("kd%d" % i, [128, 1024], BF16) for i in range(2)]
            vb2 = [sb("vb%d" % i, [128, 2048], BF16) for i in range(2)]
            dec2 = [sb("dec%d" % i, [128, 16]) for i in range(2)]
            eb = sb("eb", [128, 8, 128])
            enb = sb("enb", [128, 8, 128])
            qg = sb("qg", [128, 8, 128], BF16)
            kg = sb("kg", [128, 8, 128], BF16)
            qg0 = sb("qg0", [128, 8, 128], BF16)
            qg1 = sb("qg1", [128, 8, 128], BF16)
            AT = sb("AT", [128, 4, 128], BF16)
            sg = sb("sg", [128, 2048])
            ss = sb("ss", [128, 8])
            junk = sb("junk", [128, 512], BF16)
            tn = sb("tn", [128, 512])
            mixo = [sb("mx%d" % i, [128, 2048], BF16) for i in range(2)]
            qgm = sb("qgm", [128, 16, 8, 4], BF16)
            qgmf = sb("qgmf", [128, 8, 128], BF16)
            decs = sb("decs", [128, 8, 16])
            s0 = [sb("s0%d" % i, [128, 4, 512]) for i in range(2)]
            s0b = [sb("s0b%d" % i, [128, 4, 512], BF16) for i in range(2)]
            vm = [sb("vm%d" % i, [128, 1024], BF16) for i in range(2)]
            PA = ps("A", [128, 2, 512])
            PB = ps("B", [128, 8, 128])
            PAT = ps("AT", [128, 4, 128])
            PO = ps("O", [128, 2, 512])
            PDS = ps("DS", [128, 512])

            S.dma("sp", idf[:], ident_in, writes=["idf"])
            S.dma("sp", cm[:], cmat, writes=["cm"])
            S.dma("sp", t4[:], ti4, writes=["t4"])
            S.dma("sp", ncol[:], negcol, writes=["ncol"])
            S.dma("sp", bs16[:], bsel16, writes=["bs16"])
            S.dma("sp", bmk[:], bmask, writes=["bmk"])
            S.dma("sp", pmk[:], pmask, writes=["pmk"])
            S.dma("sp", w17s[0:17, :], w17, writes=["w17"])
            S.dma("sp", gg[:], ggla_bc, writes=["gg"])
            S.dve(lambda e: e.memset(one1[:], 1.0), writes=["one1"])
            S.dve(lambda e: e.memset(epsb[:], EPS), writes=["eps"])
            S.dve(lambda e: e.memset(St[:], 0.0), writes=["S"])
            S.dve(lambda e: e.memset(Sb[:], 0.0), writes=["Sb"])
            S.dve(lambda e: e.memset(qg0[:], 0.0), writes=["qg0"])
            S.dve(lambda e: e.memset(qg1[:], 0.0), writes=["qg1"])
            S.dve(lambda e: e.memset(qgmf[:], 0.0), writes=["qgmf"])
            for i in range(2):
                S.dve(lambda e, i=i: e.memset(ga17[i][:], 1.0), writes=[("g17", i)])

            def gla_tile(t):
                full = t >= NPRE
                samp = t == NT - 1
                f = t - NPRE
                a = t % 2
                la = la2[a]
                kd = kd2[a]
                vb = vb2[a]
                dec = dec2[a]
                ci = 2 if samp else 0
                if full:
                    src = PF[f]
                    S.dma("sp", kt[a][:], src[:, OFF_GK:OFF_GK + 1024], writes=[("k", a)])
                    S.dma("sp", vt[a][:], src[:, OFF_GV:OFF_GV + 2048], writes=[("v", a)])
                    S.dma("sp", gat[a][:], src[:, OFF_GA:OFF_GA + 16], writes=[("ga", a)])
                    S.dma("sp", qt[a][:], src[:, OFF_GQ:OFF_GQ + 1024], writes=[("q", a)])
                    S.dma("sp", rt[a][:], src[:, OFF_GR:OFF_GR + 2048], writes=[("r", a)])
                else:
                    S.dma("sp", kt[a][:], PP[t][:, 0:1024], writes=[("k", a)])
                    S.dma("sp", vt[a][:], PP[t][:, 1024:3072], writes=[("v", a)])
                    S.dma("sp", gat[a][:], PP[t][:, 3072:3088], writes=[("ga", a)])
                S.pe(lambda e, a=a: e.transpose(out=PA[0:16, 0, 0:128], in_=gat[a][:], identity=idf[:]),
                     reads=[("ga", a), "idf"], writes=["PA"])
                S.act(lambda e, a=a: e.activation(out=ga17[a][0:16, :], in_=PA[0:16, 0, 0:128], func=AF.Copy),
                      reads=["PA"], writes=[("g17", a)])

                def mmz(e, a=a):
                    for n in range(2):
                        ins = e.matmul(PA[:, n, :], lhsT=ga17[a][0:17, :], rhs=w17s[0:17, n * 512:(n + 1) * 512],
                                       start=True, stop=True)
                    return ins
                S.pe(mmz, reads=[("g17", a), "w17"], writes=["PA"])
                S.act(lambda e: e.activation(out=e1[:], in_=PA[:].rearrange("p a b -> p (a b)"), func=AF.Exp, scale=-1.0),
                      reads=["PA"], writes=["e1"])
                S.act(lambda e: e.activation(out=la[:], in_=e1[:], func=AF.Ln, bias=one1[:, 0:1], scale=1.0),
                      reads=["e1", "one1"], writes=[("la", a)])

                ui = (ci + 1) if full else 4

                def mmr(e, ui=ui):
                    for n in range(2):
                        ins = e.matmul(PA[:, n, :], lhsT=cm[:, ui, :], rhs=la[:, n * 512:(n + 1) * 512], start=True, stop=True)
                    return ins
                S.pe(mmr, reads=[("la", a), "cm"], writes=["PA"])
                S.act(lambda e: e.activation(out=er[:], in_=PA[:].rearrange("p a b -> p (a b)"), func=AF.Exp),
                      reads=["PA"], writes=["er"])
                S.dve(lambda e, a=a, t=t: e.scalar_tensor_tensor(out=kd[:], in0=kt[a][:], scalar=pmk[:, t:t + 1], in1=er[:],
                                                                 op0=ALU.mult, op1=ALU.mult),
                      reads=[("k", a), "pmk", "er"], writes=[("kd", a)])
                S.act(lambda e, a=a: e.activation(out=vb[:], in_=vt[a][:], func=AF.Copy), reads=[("v", a)], writes=[("vb", a)])
                if not full:
                    def mmd(e):
                        for c in range(8):
                            ins = e.matmul(PAT[:, 0, c:c + 1], lhsT=la[:, c * 128:(c + 1) * 128], rhs=ncol[:, 0:1], start=True, stop=True)
                        return ins
                    S.pe(mmd, reads=[("la", a), "ncol"], writes=["PAT"])
                    S.act(lambda e: e.activation(out=dec[:, 0:8], in_=PAT[:, 0, 0:8], func=AF.Exp), reads=["PAT"], writes=[("dec", a)])
                elif not samp:
                    def mmd(e):
                        for j in range(2):
                            for c in range(8):
                                ins = e.matmul(PAT[:, 0, j * 8 + c:j * 8 + c + 1],
                                               lhsT=la[64 * j:64 * j + 64, c * 128:(c + 1) * 128],
                                               rhs=ncol[64 * j:64 * j + 64, 0:1], start=True, stop=True)
                        return ins
                    S.pe(mmd, reads=[("la", a), "ncol"], writes=["PAT"])
                    S.act(lambda e: e.activation(out=dec[:], in_=PAT[:, 0, 0:16], func=AF.Exp), reads=["PAT"], writes=[("dec", a)])
                else:
                    def mmd(e):
                        for c in range(8):
                            ins = e.matmul(PAT[:, 0, c * 16:(c + 1) * 16], lhsT=la[0:64, c * 128:(c + 1) * 128],
                                           rhs=bs16[0:64, :], start=True, stop=True)
                        return ins
                    S.pe(mmd, reads=[("la", a), "bs16"], writes=["PAT"])
                    S.act(lambda e: e.activation(out=decs[:].rearrange("p c b -> p (c b)"), in_=PAT[:, 0, :], func=AF.Exp),
                          reads=["PAT"], writes=["decs"])

                if full:
                    def mmb(e, ci=ci):
                        for c in range(8):
                            ins = e.matmul(PA[:].rearrange("p a (c t) -> p (a c) t", t=128)[:, c, :],
                                           lhsT=la[:, c * 128:(c + 1) * 128], rhs=cm[:, ci, :], start=True, stop=True)
                        return ins
                    S.pe(mmb, reads=[("la", a), "cm"], writes=["PA"])
                    pa8 = PA[:].rearrange("p a (c t) -> p (a c) t", t=128)
                    S.act(lambda e, pa8=pa8: e.activation(out=eb[:], in_=pa8, func=AF.Exp), reads=["PA"], writes=["eb"])
                    S.act(lambda e, pa8=pa8: e.activation(out=enb[:], in_=pa8, func=AF.Exp, scale=-1.0), reads=["PA"], writes=["enb"])

                    def trq(e, a=a):
                        for c in range(8):
                            ins = e.transpose(out=PB[:, c, :], in_=qt[a][:, c * 128:(c + 1) * 128], identity=idf[:])
                        return ins
                    S.pe(trq, reads=[("q", a), "idf"], writes=["PB"])
                    S.dve(lambda e: e.scalar_tensor_tensor(out=qg[:], in0=PB[:], scalar=0.0625, in1=eb[:],
                                                           op0=ALU.mult, op1=ALU.mult),
                          reads=["PB", "eb"], writes=["qg"])

                    def trk(e, a=a):
                        for c in range(8):
                            ins = e.transpose(out=PB[:, c, :], in_=kt[a][:, c * 128:(c + 1) * 128], identity=idf[:])
                        return ins
                    S.pe(trk, reads=[("k", a), "idf"], writes=["PB"])
                    S.dve(lambda e: e.tensor_tensor(out=kg[:], in0=PB[:], in1=enb[:], op=ALU.mult),
                          reads=["PB", "enb"], writes=["kg"])
                    if not samp:
                        S.act(lambda e: e.activation(out=qg0[:, :, 0:64], in_=qg[:, :, 0:64], func=AF.Copy),
                              reads=["qg"], writes=["qg0"])
                        S.act(lambda e: e.activation(out=qg1[:, :, 64:128], in_=qg[:, :, 64:128], func=AF.Copy),
                              reads=["qg"], writes=["qg1"])

                    def mma(e):
                        for h in range(4):
                            for kc in range(2):
                                ins = e.matmul(PAT[:, h, :], lhsT=kg[:, h * 2 + kc, :], rhs=qg[:, h * 2 + kc, :],
                                               start=(kc == 0), stop=(kc == 1))
                        return ins
                    S.pe(mma, reads=["kg", "qg"], writes=["PAT"])
                    S.dve(lambda e, samp=samp: e.tensor_tensor(out=AT[:], in0=PAT[:], in1=t4[:, 1 if samp else 0, :, :], op=ALU.mult),
                          reads=["PAT", "t4"], writes=["AT"])
                    S.act(lambda e, a=a: e.activation(out=sg[:], in_=rt[a][:], func=AF.Silu), reads=[("r", a)], writes=["sg"])

                dsb = [(PDS[:], "PDS"), (PO[:, 0, :], ("PO", 0)), (PO[:, 1, :], ("PO", 1))]
                dsn = [0]

                def s_update(j, c, h, rot=False, cast=True):
                    if rot:
                        dap, dkey = dsb[dsn[0] % 3]
                        dsn[0] += 1
                    else:
                        dap, dkey = dsb[0]
                    if j is None:
                        p0, p1, dcol = 0, 128, c
                    else:
                        p0, p1, dcol = 64 * j, 64 * j + 64, j * 8 + c
                    S.pe(lambda e, c=c, h=h, dap=dap, p0=p0, p1=p1: e.matmul(dap, lhsT=kd[p0:p1, c * 128:(c + 1) * 128],
                                                                             rhs=vb[p0:p1, h * 512:(h + 1) * 512], start=True, stop=True),
                         reads=[("kd", a), ("vb", a)], writes=[dkey])
                    S.dve(lambda e, c=c, dap=dap, dcol=dcol: e.scalar_tensor_tensor(out=St[:, c, :], in0=St[:, c, :],
                                                                                    scalar=dec[:, dcol:dcol + 1], in1=dap,
                                                                                    op0=ALU.mult, op1=ALU.add),
                          reads=[dkey, ("dec", a), ("S", c)], writes=[("S", c)])
                    if cast:
                        S.act(lambda e, c=c: e.activation(out=Sb[:, c, :], in_=St[:, c, :], func=AF.Copy),
                              reads=[("S", c)], writes=[("Sb", c)])

                def finish_heads(hp, a=a, f=f):
                    m = f % 2
                    for hh in range(2):
                        h = hp * 2 + hh
                        S.act(lambda e, hh=hh, h=h: e.activation(out=junk[:], in_=PO[:, hh, :], func=AF.Square,
                                                                 accum_out=ss[:, h:h + 1]),
                              reads=[("PO", hh), "ss0"], writes=["junk", ("ss", h)])
                        S.act(lambda e, h=h: e.activation(out=ss[:, 4 + h:5 + h], in_=ss[:, h:h + 1], func=AF.Sqrt,
                                                          bias=epsb[:, 0:1], scale=1.0 / 512),
                              reads=[("ss", h), "eps"], writes=[("rs", h)])
                        S.dve(lambda e, h=h: e.reciprocal(out=ss[:, 4 + h:5 + h], in_=ss[:, 4 + h:5 + h]),
                              reads=[("rs", h)], writes=[("rs", h)])
                        S.dve(lambda e, hh=hh, h=h: e.scalar_tensor_tensor(out=tn[:], in0=PO[:, hh, :], scalar=ss[:, 4 + h:5 + h],
                                                                           in1=gg[:], op0=ALU.mult, op1=ALU.mult),
                              reads=[("PO", hh), ("rs", h), "gg"], writes=["tn"])
                        S.dve(lambda e, h=h, m=m: e.tensor_tensor(out=mixo[m][:, h * 512:(h + 1) * 512], in0=tn[:],
                                                                  in1=sg[:, h * 512:(h + 1) * 512], op=ALU.mult),
                              reads=["tn", "sg"], writes=[("mx", m, h)])

                if not samp:
                    if full:
                        S.dve(lambda e: e.memset(ss[:, 0:4], 0.0), writes=["ss0"] + [("ss", h) for h in range(4)])
                        for hp in range(2):
                            for hh in range(2):
                                h = hp * 2 + hh

                                def mmo(e, hh=hh, h=h):
                                    e.matmul(PO[:, hh, :], lhsT=AT[:, h, :], rhs=vb[:, h * 512:(h + 1) * 512], start=True, stop=False)
                                    for kc in range(2):
                                        ins = e.matmul(PO[:, hh, :], lhsT=qg0[:, h * 2 + kc, :], rhs=Sb[:, h * 2 + kc, :],
                                                       start=False, stop=False)
                                    return ins
                                S.pe(mmo, reads=["AT", ("vb", a), "qg0", ("Sb", h * 2), ("Sb", h * 2 + 1)], writes=[("PO", hh)])
                            for hh in range(2):
                                h = hp * 2 + hh
                                for kc in range(2):
                                    s_update(0, h * 2 + kc, h)
                            for hh in range(2):
                                h = hp * 2 + hh

                                def mmo2(e, hh=hh, h=h):
                                    for kc in range(2):
                                        ins = e.matmul(PO[:, hh, :], lhsT=qg1[:, h * 2 + kc, :], rhs=Sb[:, h * 2 + kc, :],
                                                       start=False, stop=(kc == 1))
                                    return ins
                                S.pe(mmo2, reads=["qg1", ("Sb", h * 2), ("Sb", h * 2 + 1)], writes=[("PO", hh)])
                            for hh in range(2):
                                h = hp * 2 + hh
                                for kc in range(2):
                                    s_update(1, h * 2 + kc, h)
                            finish_heads(hp)
                        S.dma("pool", MIX[f][:, 0:2048], mixo[f % 2][:], reads=[("mx", f % 2, h) for h in range(4)],
                              writes=[("MIXg", f)])
                    else:
                        for h in range(4):
                            for kc in range(2):
                                s_update(None, h * 2 + kc, h, rot=True, cast=(t == NPRE - 1))
                    if t == NT - 2:
                        S.dma("sp", o_gla_p.rearrange("h (kc p) v -> p h kc v", p=128),
                              St[:].rearrange("p (h kc) v -> p h kc v", kc=2),
                              reads=[("S", c) for c in range(8)], writes=["o_gla_p"])
                else:
                    S.dve(lambda e: e.tensor_copy(out=qgm[:], in_=qg[:, :, 0:64].rearrange("p c (b t) -> p b c t", t=4)),
                          reads=["qg"], writes=["qgm"])
                    S.dve(lambda e: e.memset(ss[:, 0:4], 0.0), writes=["ss0"] + [("ss", h) for h in range(4)])
                    it = 0
                    for hp in range(2):
                        for hh in range(2):
                            h = hp * 2 + hh
                            S.pe(lambda e, hh=hh, h=h: e.matmul(PO[:, hh, :], lhsT=AT[:, h, :], rhs=vb[:, h * 512:(h + 1) * 512],
                                                                start=True, stop=False),
                                 reads=["AT", ("vb", a)], writes=[("PO", hh)])
                        for b in range(16):
                            u = it % 2
                            it += 1
                            S.dma("sp", s0[u][:].rearrange("p (h kc) v -> p h kc v", kc=2),
                                  st_gla[b, hp * 2:hp * 2 + 2].rearrange("h (kc p) v -> p h kc v", p=128), writes=[("s0", u)])
                            S.act(lambda e, u=u: e.activation(out=s0b[u][:], in_=s0[u][:], func=AF.Copy),
                                  reads=[("s0", u)], writes=[("s0b", u)])
                            S.dve(lambda e, b=b: e.tensor_copy(out=qgmf[:, :, 4 * b:4 * b + 4], in_=qgm[:, b, :, :]),
                                  reads=["qgm"], writes=["qgmf"])
                            S.act(lambda e, u=u, b=b, hp=hp: e.mul(out=vm[u][:], in_=vb[:, hp * 1024:(hp + 1) * 1024], mul=bmk[:, b:b + 1]),
                                  reads=[("vb", a), "bmk"], writes=[("vm", u)])

                            def mmi(e, u=u, hp=hp, b=b):
                                for hh in range(2):
                                    for kc in range(2):
                                        last = (b == 15 and kc == 1)
                                        ins = e.matmul(PO[:, hh, :], lhsT=qgmf[:, (hp * 2 + hh) * 2 + kc, :], rhs=s0b[u][:, hh * 2 + kc, :],
                                                       start=False, stop=last)
                                return ins
                            S.pe(mmi, reads=["qgmf", ("s0b", u)], writes=[("PO", 0), ("PO", 1)])
                            if b > 0 or True:
                                S.dve(lambda e, b=b: e.memset(qgmf[:, :, 4 * b:4 * b + 4], 0.0), reads=[], writes=["qgmf"])
                            for hh in range(2):
                                for kc in range(2):
                                    c = (hp * 2 + hh) * 2 + kc
                                    S.pe(lambda e, u=u, hh=hh, c=c: e.matmul(PDS[:], lhsT=kd[0:64, c * 128:(c + 1) * 128],
                                                                            rhs=vm[u][0:64, hh * 512:(hh + 1) * 512], start=True, stop=True),
                                         reads=[("kd", a), ("vm", u)], writes=["PDS"])
                                    S.dve(lambda e, u=u, hh=hh, kc=kc, c=c, b=b: e.scalar_tensor_tensor(
                                        out=s0[u][:, hh * 2 + kc, :], in0=s0[u][:, hh * 2 + kc, :], scalar=decs[:, c, b:b + 1],
                                        in1=PDS[:], op0=ALU.mult, op1=ALU.add),
                                        reads=["PDS", "decs", ("s0", u)], writes=[("s0", u)])
                            S.dma("pool", o_gla_s[b, hp * 2:hp * 2 + 2].rearrange("h (kc p) v -> p h kc v", p=128),
                                  s0[u][:].rearrange("p (h kc) v -> p h kc v", kc=2),
                                  reads=[("s0", u)], writes=[("ogs", b, hp)])
                        finish_heads(hp)
                    S.dma("sp", MIX[f][:, 0:2048], mixo[f % 2][:], reads=[("mx", f % 2, h) for h in range(4)],
                          writes=[("MIXg", f)])
            for t in range(NT):
                gla_tile(t)
            S.end_phase()

        SCALE = 128.0 ** -0.5
        with ExitStack() as ph:
            sb = lambda n, s, dt=F32: ph.enter_context(nc.sbuf_tensor("pD" + n, list(s), dt))
            ps = lambda n, s, dt=F32: ph.enter_context(nc.psum_tensor("psD" + n, list(s), dt))
            idf = sb("idf", [128, 128])
            idb = sb("idb", [128, 128], BF16)
            bias = sb("bias", [128, 2, 16, 256])
            snk = sb("snk", [128, 16])
            snks = sb("snks", [16, 4])
            bsb = sb("bsb", [16, 4, 128])
            bsn = sb("bsn", [16, 4, 16, 64])
            qin = [sb("qin%d" % i, [128, 2048]) for i in range(2)]
            kin = [sb("kin%d" % i, [128, 512]) for i in range(2)]
            vin = [sb("vin%d" % i, [128, 512]) for i in range(2)]
            qT = sb("qT", [128, 16, 128], BF16)
            kT = [sb("kT%d" % i, [128, 4, 128], BF16) for i in range(2)]
            vv = [sb("vv%d" % i, [128, 512], BF16) for i in range(2)]
            ssb = [sb("ssb%d" % i, [128, 256]) for i in range(4)]
            pex = [sb("pex%d" % i, [128, 256], BF16) for i in range(4)]
            qb = [sb("qb%d" % i, [128, 2048], BF16) for i in range(2)]
            kbf = [sb("kbf%d" % i, [128, 512], BF16) for i in range(2)]
            st = sb("st", [128, 16, 8])
            pT = [sb("pT%d" % i, [128, 2, 128], BF16) for i in range(4)]
            osw = [sb("osw%d" % i, [128, 2048], BF16) for i in range(2)]
            kbT = [sb("kbT%d" % i, [128, 4, 128], BF16) for i in range(2)]
            vbf = [sb("vbf%d" % i, [128, 512], BF16) for i in range(2)]
            OS = sb("OS", [16, 16, 4, 128], BF16)
            qS = sb("qS", [128, 16, 4, 16], BF16)
            scb2 = sb("scb2", [128, KC, 256], BF16)
            wa = [sb("wa%d" % i, [128, KC, 256], BF16) for i in range(2)]
            ba = [sb("ba%d" % i, [128, 256]) for i in range(2)]
            ma = [sb("ma%d" % i, [128, 2, 256]) for i in range(2)]
            PQ = ps("Q", [128, 16, 128], BF16)
            PK = ps("K", [128, 4, 128], BF16)
            PS_ = ps("S", [128, 4, 256])
            PT = ps("T", [128, 4, 2, 128], BF16)
            POo = ps("O", [128, 4, 128])
            PM = ps("M", [128, 2, 256])
            S.dma("sp", scb2[:].rearrange("p k m -> p (k m)"), SCB, writes=["scb2"])
            NB0 = 32
            NB1 = 96

            def ldwa(n):
                S.dma("pool", wa[n % 2][:], w_ada[:, n * 256:(n + 1) * 256].rearrange("(k p) n -> p k n", p=128), writes=[("wa", n % 2)])
            ldwa(NB0)
            p0state = [NB0, 0]

            def p0b_step():
                n = p0state[0]
                if n >= NB1:
                    return
                p0state[0] += 1
                i = n % 2
                cs = slice(n * 256, (n + 1) * 256)
                if n + 1 < NB1:
                    ldwa(n + 1)
                S.dma("sp", ba[i][:], bada_bc[:, cs], writes=[("ba", i)])

                def mm(e, i=i):
                    for g in range(2):
                        for k in range(KC):
                            ins = e.matmul(PM[:, g, :], lhsT=scb2[:, k, g * 128:(g + 1) * 128], rhs=wa[i][:, k, :],
                                           start=(k == 0), stop=(k == KC - 1))
                    return ins
                S.pe(mm, reads=["scb2", ("wa", i)], writes=["PM"])
                for g in range(2):
                    S.dve(lambda e, i=i, g=g: e.tensor_tensor(out=ma[i][:, g, :], in0=PM[:, g, :], in1=ba[i][:], op=ALU.add),
                          reads=["PM", ("ba", i)], writes=[("ma", i, g)])
                S.dma("pool", MOD[:, :, cs].rearrange("g p n -> p g n"), ma[i][:],
                      reads=[("ma", i, 0), ("ma", i, 1)], writes=[("MOD", n)])

            def p0b_tick(total_units=208):
                p0state[1] += 1
                want = NB0 + (p0state[1] * (NB1 - NB0) + total_units - 1) // total_units
                while p0state[0] < min(want, NB1):
                    p0b_step()
            S.dma("sp", idf[:], ident_in, writes=["idf"])
            S.dve(lambda e: e.tensor_copy(out=idb[:], in_=idf[:]), reads=["idf"], writes=["idb"])
            S.dma("sp", bias[:], bias_p, writes=["bias"])
            S.dma("sp", snk[:], sink_bc, writes=["snk"])
            S.dma("sp", snks[:], sink_s, writes=["snks"])
            S.dma("sp", bsb[:], bias_sb, writes=["bsb"])
            S.dma("sp", bsn[:], bias_sn, writes=["bsn"])
            S.dve(lambda e: e.memset(st[:], 0.0), writes=["st"])

            def load_kv(slot, ksrc, vsrc):
                S.dma("sp", kin[slot][:], ksrc, writes=[("kin", slot)])
                S.dma("sp", vin[slot][:], vsrc, writes=[("vin", slot)])

                S.act(lambda e, slot=slot: e.activation(out=kbf[slot][:], in_=kin[slot][:], func=AF.Copy),
                      reads=[("kin", slot)], writes=[("kbf", slot)])

                def trk(e, slot=slot):
                    for c in range(4):
                        ins = e.transpose(out=PK[:, c, :], in_=kbf[slot][:, c * 128:(c + 1) * 128], identity=idb[:])
                    return ins
                S.pe(trk, reads=[("kbf", slot), "idb"], writes=["PK"])
                S.act(lambda e, slot=slot: e.activation(out=kT[slot][:], in_=PK[:], func=AF.Copy), reads=["PK"], writes=[("kT", slot)])
                S.act(lambda e, slot=slot: e.activation(out=vv[slot][:], in_=vin[slot][:], func=AF.Copy),
                      reads=[("vin", slot)], writes=[("vv", slot)])

            def load_q(slot, qsrc):
                S.dma("sp", qin[slot][:], qsrc, writes=[("qin", slot)])

                S.act(lambda e, slot=slot: e.activation(out=qb[slot][:], in_=qin[slot][:], func=AF.Copy),
                      reads=[("qin", slot)], writes=[("qb", slot)])

                def trq(e, slot=slot):
                    for c in range(16):
                        ins = e.transpose(out=PQ[:, c, :], in_=qb[slot][:, c * 128:(c + 1) * 128], identity=idb[:])
                    return ins
                S.pe(trq, reads=[("qb", slot), "idb"], writes=["PQ"])
                S.dve(lambda e: e.tensor_copy(out=qT[:], in_=PQ[:]), reads=["PQ"], writes=["qT"])

            def softmax_rows(np_, z, sinkcol, width, hkey):
                S.dve(lambda e: e.tensor_reduce(out=st[0:np_, hkey, 0:1], in_=ssb[z][0:np_, 0:width], axis=AX.X, op=ALU.max),
                      reads=[("ssb", z)], writes=[("st", hkey)])
                S.dve(lambda e: e.tensor_scalar(out=st[0:np_, hkey, 1:2], in0=st[0:np_, hkey, 0:1], scalar1=sinkcol, scalar2=-1.0,
                                                op0=ALU.max, op1=ALU.mult),
                      reads=[("st", hkey), "snk"], writes=[("st", hkey)])
                S.act(lambda e: e.activation(out=pex[z][0:np_, 0:width], in_=ssb[z][0:np_, 0:width], func=AF.Exp,
                                             bias=st[0:np_, hkey, 1:2], scale=1.0, accum_out=st[0:np_, hkey, 2:3]),
                      reads=[("ssb", z), ("st", hkey)], writes=[("pex", z), ("st", hkey)])
                S.act(lambda e: e.activation(out=st[0:np_, hkey, 3:4], in_=sinkcol, func=AF.Exp, bias=st[0:np_, hkey, 1:2], scale=1.0),
                      reads=[("st", hkey), "snk"], writes=[("st", hkey)])
                S.dve(lambda e: e.tensor_tensor(out=st[0:np_, hkey, 4:5], in0=st[0:np_, hkey, 2:3], in1=st[0:np_, hkey, 3:4], op=ALU.add),
                      reads=[("st", hkey)], writes=[("st", hkey)])
                S.dve(lambda e: e.reciprocal(out=st[0:np_, hkey, 5:6], in_=st[0:np_, hkey, 4:5]),
                      reads=[("st", hkey)], writes=[("st", hkey)])

            load_kv(1, PH2[:, 0:512], PH2[:, 512:1024])
            hc = 0
            for f in range(NFULL):
                cur = f % 2
                prv = 1 - cur
                src = PF[f]
                load_kv(cur, src[:, OFF_SK:OFF_SK + 512], src[:, OFF_SV:OFF_SV + 512])
                load_q(cur, src[:, OFF_SQ:OFF_SQ + 2048])
                if f < NFULL - 1:
                    bsel = 0 if f == 1 else 1
                    for h in range(16):
                        kvh = h // 4
                        z = hc % 4
                        o4 = hc % 4
                        hc += 1

                        def mms(e, h=h, kvh=kvh, z=z, prv=prv, cur=cur):
                            e.matmul(PS_[:, z, 0:128], lhsT=qT[:, h, :], rhs=kT[prv][:, kvh, :], start=True, stop=True)
                            return e.matmul(PS_[:, z, 128:256], lhsT=qT[:, h, :], rhs=kT[cur][:, kvh, :], start=True, stop=True)
                        S.pe(mms, reads=["qT", ("kT", prv), ("kT", cur)], writes=[("PS", z)])
                        S.dve(lambda e, z=z, h=h, bsel=bsel: e.scalar_tensor_tensor(out=ssb[z][:], in0=PS_[:, z, :], scalar=SCALE,
                                                                                    in1=bias[:, bsel, h, :], op0=ALU.mult, op1=ALU.add),
                              reads=[("PS", z), "bias"], writes=[("ssb", z)])
                        S.dve(lambda e, h=h: e.memset(st[:, h, 2:3], 0.0), writes=[("st", h)])
                        softmax_rows(128, z, snk[:, h:h + 1], 256, h)

                        def trp(e, z=z):
                            e.transpose(out=PT[:, z, 0, :], in_=pex[z][:, 0:128], identity=idb[:])
                            return e.transpose(out=PT[:, z, 1, :], in_=pex[z][:, 128:256], identity=idb[:])
                        S.pe(trp, reads=[("pex", z), "idb"], writes=[("PT", z)])
                        S.any(lambda e, ia, z=z: acopy(e, ia, pT[z][:], PT[:, z, :, :]), reads=[("PT", z)], writes=[("pT", z)])

                        def mmv(e, z=z, kvh=kvh, o4=o4, prv=prv, cur=cur):
                            e.matmul(POo[:, o4, :], lhsT=pT[z][:, 0, :], rhs=vv[prv][:, kvh * 128:(kvh + 1) * 128], start=True, stop=False)
                            return e.matmul(POo[:, o4, :], lhsT=pT[z][:, 1, :], rhs=vv[cur][:, kvh * 128:(kvh + 1) * 128], start=False, stop=True)
                        S.pe(mmv, reads=[("pT", z), ("vv", prv), ("vv", cur)], writes=[("PO", o4)])
                        S.act(lambda e, o4=o4, h=h, cur=cur: e.mul(out=osw[cur][:, h * 128:(h + 1) * 128], in_=POo[:, o4, :], mul=st[:, h, 5:6]),
                              reads=[("PO", o4), ("st", h)], writes=[("osw", cur, h)])
                        p0b_tick()
                    S.dma("pool", MIX[f][:, 2048:4096], osw[cur][:], reads=[("osw", cur, h) for h in range(16)], writes=[("MIXs", f)])
                    if f == NFULL - 2:
                        S.dma("sp", o_k_p, src[:, OFF_SK:OFF_SK + 512], writes=["okp"])
                        S.dma("sp", o_v_p, src[:, OFF_SV:OFF_SV + 512], writes=["ovp"])
                else:
                    S.dma("sp", o_k_s[:, 0:124, :], st_k[:, 4:128, :], writes=["oks0"])
                    S.dma("sp", o_v_s[:, 0:124, :], st_v[:, 4:128, :], writes=["ovs0"])
                    S.dma("sp", o_k_s[:, 124:128, :], src[0:64, OFF_SK:OFF_SK + 512].rearrange("(b t) c -> b t c", t=4), writes=["oks1"])
                    S.dma("sp", o_v_s[:, 124:128, :], src[0:64, OFF_SV:OFF_SV + 512].rearrange("(b t) c -> b t c", t=4), writes=["ovs1"])
                    for kvh in range(4):
                        S.dve(lambda e, kvh=kvh: e.tensor_copy(out=qS[:, :, kvh, :].rearrange("p b (g t) -> p b g t", t=4),
                                                               in_=qT[:, kvh * 4:(kvh + 1) * 4, 0:64].rearrange("p g (b t) -> p b g t", t=4)),
                              reads=["qT"], writes=["qS"])
                    for b in range(16):
                        u = b % 2
                        S.dma("pool", kbT[u][:], st_kT[b].rearrange("h d k -> d h k"), writes=[("kbT", u)])
                        S.dma("pool", vbf[u][:], st_v[b], writes=[("vbf", u)])
                        for kvh in range(4):
                            z = hc % 4
                            o4 = hc % 4
                            hc += 1
                            hk = kvh * 4
                            qsl = qS[:, b, kvh, :]

                            def mms(e, z=z, kvh=kvh, u=u, qsl=qsl, cur=cur):
                                e.matmul(PS_[0:16, z, 0:128], lhsT=qsl, rhs=kbT[u][:, kvh, :], start=True, stop=True)
                                return e.matmul(PS_[0:16, z, 128:192], lhsT=qsl, rhs=kT[cur][:, kvh, 0:64], start=True, stop=True)
                            S.pe(mms, reads=["qS", ("kbT", u), ("kT", cur)], writes=[("PS", z)])
                            S.dve(lambda e, z=z, kvh=kvh: e.scalar_tensor_tensor(out=ssb[z][0:16, 0:128], in0=PS_[0:16, z, 0:128], scalar=SCALE,
                                                                                 in1=bsb[:, kvh, :], op0=ALU.mult, op1=ALU.add),
                                  reads=[("PS", z), "bsb"], writes=[("ssb", z)])
                            S.dve(lambda e, z=z, kvh=kvh, b=b: e.scalar_tensor_tensor(out=ssb[z][0:16, 128:192], in0=PS_[0:16, z, 128:192],
                                                                                      scalar=SCALE, in1=bsn[:, kvh, b, :],
                                                                                      op0=ALU.mult, op1=ALU.add),
                                  reads=[("PS", z), "bsn", ("ssb", z)], writes=[("ssb", z)])
                            S.dve(lambda e, hk=hk: e.memset(st[0:16, hk, 2:3], 0.0), writes=[("st", hk)])
                            softmax_rows(16, z, snks[:, kvh:kvh + 1], 192, hk)

                            def trp(e, z=z):
                                e.transpose(out=PT[:, z, 0, 0:16], in_=pex[z][0:16, 0:128], identity=idb[0:16, 0:16])
                                return e.transpose(out=PT[0:64, z, 1, 0:16], in_=pex[z][0:16, 128:192], identity=idb[0:16, 0:16])
                            S.pe(trp, reads=[("pex", z), "idb"], writes=[("PT", z)])
                            S.any(lambda e, ia, z=z: acopy(e, ia, pT[z][:, 0, 0:16], PT[:, z, 0, 0:16]), reads=[("PT", z)], writes=[("pT", z, 0)])
                            S.any(lambda e, ia, z=z: acopy(e, ia, pT[z][0:64, 1, 0:16], PT[0:64, z, 1, 0:16]), reads=[("PT", z)], writes=[("pT", z, 1)])

                            def mmv(e, z=z, kvh=kvh, o4=o4, u=u, cur=cur):
                                e.matmul(POo[0:16, o4, :], lhsT=pT[z][:, 0, 0:16], rhs=vbf[u][:, kvh * 128:(kvh + 1) * 128], start=True, stop=False)
                                return e.matmul(POo[0:16, o4, :], lhsT=pT[z][0:64, 1, 0:16], rhs=vv[cur][0:64, kvh * 128:(kvh + 1) * 128],
                                                start=False, stop=True)
                            S.pe(mmv, reads=[("pT", z, 0), ("pT", z, 1), ("vbf", u), ("vv", cur)], writes=[("PO", o4)])
                            S.act(lambda e, o4=o4, hk=hk, b=b, kvh=kvh: e.mul(out=OS[:, b, kvh, :], in_=POo[0:16, o4, :], mul=st[0:16, hk, 5:6]),
                                  reads=[("PO", o4), ("st", hk)], writes=["OS"])
                            p0b_tick()
                    for g in range(4):
                        for kvh in range(4):
                            S.dma("sp", MIX[f][0:64, 2048:4096].rearrange("(b t) (kvh g d) -> g kvh t b d", t=4, g=4, d=128)[g, kvh],
                                  OS[4 * g:4 * g + 4, :, kvh, :], reads=["OS"], writes=[("MIXs", f, g, kvh)])
            while p0state[0] < NB1:
                p0b_step()
            S.end_phase()

        with ExitStack() as ph:
            sb = lambda n, s, dt=F32: ph.enter_context(nc.sbuf_tensor("pE" + n, list(s), dt))
            ps = lambda n, s, dt=F32: ph.enter_context(nc.psum_tensor("psE" + n, list(s), dt))
            idf = sb("idf", [128, 128])
            idb = sb("idb", [128, 128], BF16)
            mT = sb("mT", [128, NFULL, KC, 128], BF16)
            mx = [sb("mx%d" % i, [128, D], BF16) for i in range(2)]
            GT = [sb("GT%d" % g, [128, D]) for g in range(2)]
            wb = [sb("w%d" % i, [128, KC, 256], BF16) for i in range(2)]
            xb = [sb("xb%d" % i, [128, 256]) for i in range(3)]
            tb = [sb("tb%d" % i, [128, 256]) for i in range(3)]
            ptr = [ps("ptr%d" % i, [128, 16, 128], BF16) for i in range(2)]
            pp = [ps("p%d" % i, [128, 512]) for i in range(3)]
            S.dma("sp", idf[:], ident_in, writes=["idf"])
            S.dve(lambda e: e.tensor_copy(out=idb[:], in_=idf[:]), reads=["idf"], writes=["idb"])
            for g in range(2):
                S.dma("sp", GT[g][:], MOD[g, :, 2 * D:3 * D], writes=[("GT", g)])
            npt = 0
            for f in range(NFULL):
                a = f % 2
                S.dma("sp", mx[a][:], MIX[f], writes=[("mx", a)])
                for half in range(2):
                    pj = npt % 2
                    npt += 1

                    def tr(e, a=a, half=half, pj=pj):
                        for k in range(16):
                            kk = half * 16 + k
                            ins = e.transpose(out=ptr[pj][:, k, :], in_=mx[a][:, kk * 128:(kk + 1) * 128], identity=idb[:])
                        return ins
                    S.pe(tr, reads=[("mx", a), "idb"], writes=[("ptr", pj)])
                    S.any(lambda e, ia, f=f, half=half, pj=pj: acopy(e, ia, mT[:, f, half * 16:(half + 1) * 16, :], ptr[pj][:]),
                          reads=[("ptr", pj)], writes=[("mT", f, half)])
            cnt = 0
            def ldwE(n):
                S.dma("pool", wb[n % 2][:], w_o[:, n * 256:(n + 1) * 256].rearrange("(k p) n -> p k n", p=128), writes=[("w", n % 2)])
            ldwE(0)
            for n in range(16):
                wi = n % 2
                cs = slice(n * 256, (n + 1) * 256)
                if n + 1 < 16:
                    ldwE(n + 1)
                for f in range(NFULL):
                    a = cnt % 3
                    cnt += 1
                    g = 1 if f == NFULL - 1 else 0
                    S.dma("sp", xb[a][:], xall[NPRE + f][:, cs], writes=[("xb", a)])

                    def mm(e, a=a, wi=wi, f=f):
                        for k in range(KC):
                            ins = e.matmul(pp[a][:, 0:256], lhsT=mT[:, f, k, :], rhs=wb[wi][:, k, :], start=(k == 0), stop=(k == KC - 1))
                        return ins
                    S.pe(mm, reads=[("mT", f, 0), ("mT", f, 1), ("w", wi)], writes=[("p", a)])
                    S.dve(lambda e, a=a, g=g, cs=cs: e.tensor_tensor(out=tb[a][:], in0=pp[a][:, 0:256], in1=GT[g][:, cs], op=ALU.mult),
                          reads=[("p", a), ("GT", g)], writes=[("tb", a)])
                    S.dve(lambda e, a=a: e.tensor_tensor(out=tb[a][:], in0=tb[a][:], in1=xb[a][:], op=ALU.add),
                          reads=[("tb", a), ("xb", a)], writes=[("tb", a)])
                    S.dma("pool", X1[f][:, cs], tb[a][:], reads=[("tb", a)], writes=[("X1", f, n)])
            S.end_phase()

        norm_phase("nE", [X1[f] for f in range(NFULL)], [1 if f == NFULL - 1 else 0 for f in range(NFULL)],
                   3 * D, 4 * D, 1, [H2T[f] for f in range(NFULL)])

        NTF = 1090
        TG = [(0, 512), (512, 512), (1024, 66)]
        with ExitStack() as ph:
            sb = lambda n, s, dt=F32: ph.enter_context(nc.sbuf_tensor("pF" + n, list(s), dt))
            ps = lambda n, s, dt=F32: ph.enter_context(nc.psum_tensor("psF" + n, list(s), dt))
            h2 = sb("h2", [128, KC, NTF], BF16)
            hm = sb("hm", [128, 1])
            wc = sb("wc", [128, NBLK, 4])
            cst = sb("cst", [128, NBLK, 32])
            wg = [sb("wg%d" % i, [128, KC, 128], BF16) for i in range(2)]
            wv = [sb("wv%d" % i, [128, KC, 128], BF16) for i in range(2)]
            U = [sb("U%d" % i, [128, 3, 512]) for i in range(2)]
            US = [sb("US%d" % i, [128, 16, 6]) for i in range(2)]
            acc = [sb("acc%d" % i, [128, 1088]) for i in range(2)]
            sgt = sb("sgt", [128, 1088])
            ao = [sb("ao%d" % i, [128, 1088], BF16) for i in range(2)]
            cn = [sb("cn%d" % i, [128, 34]) for i in range(2)]
            PU = [ps("U%d" % i, [128, 3, 512]) for i in range(2)]
            S.dma("sp", hm[:], hmask, writes=["hm"])
            S.dma("sp", wc[:], w_convT.rearrange("(b p) j -> p b j", p=128), writes=["wc"])
            S.dma("sp", cst[:], st_convT.rearrange("(b p) j -> p b j", p=128), writes=["cst"])
            for f in range(1, NFULL):
                ntok = 64 if f == NFULL - 1 else 128
                S.dma("sp", h2[:, :, 2 + (f - 1) * 128:2 + (f - 1) * 128 + ntok],
                      H2T[f].rearrange("p (k t) -> p k t", t=128)[:, :, 0:ntok], writes=[("h2", f)])
            S.dma("sp", h2[:, :, 0:2], H2T[0].rearrange("p (k t) -> p k t", t=128)[:, :, 126:128], writes=[("h2", 0)])
            S.dve(lambda e: e.tensor_scalar(out=h2[:, :, 0:2], in0=h2[:, :, 0:2], scalar1=hm[:, 0:1], scalar2=None, op0=ALU.mult),
                  reads=[("h2", 0), "hm"], writes=[("h2", 0)])
            h2keys = [("h2", f) for f in range(NFULL)]
            def ldwF(gi):
                S.dma("pool", wg[gi % 2][:], w_up[:, gi * 128:(gi + 1) * 128].rearrange("(k p) n -> p k n", p=128), writes=[("wg", gi % 2)])
                S.dma("pool", wv[gi % 2][:], w_up[:, DFF + gi * 128:DFF + (gi + 1) * 128].rearrange("(k p) n -> p k n", p=128),
                      writes=[("wv", gi % 2)])
            ldwF(0)
            for gi in range(86):
                wi = gi % 2
                if gi + 1 < 86:
                    ldwF(gi + 1)
                for sub in range(1):
                    blk = gi
                    ai = blk % 2
                    for which in range(2):
                        wt = wg if which == 0 else wv
                        bidx = blk if which == 0 else 86 + blk

                        def mm(e, wt=wt, wi=wi, sub=sub, which=which):
                            for tg, (t0, n) in enumerate(TG):
                                for k in range(KC):
                                    ins = e.matmul(PU[which][:, tg, 0:n], lhsT=wt[wi][:, k, sub * 128:(sub + 1) * 128],
                                                   rhs=h2[:, k, t0:t0 + n], start=(k == 0), stop=(k == KC - 1))
                            return ins
                        S.pe(mm, reads=h2keys + [("wg" if which == 0 else "wv", wi)], writes=[("PU", which)])
                        S.act(lambda e, which=which: e.activation(out=U[which][:], in_=PU[which][:], func=AF.Copy),
                              reads=[("PU", which)], writes=[("U", which)])
                        Uf = U[which][:].rearrange("p a b -> p (a b)")
                        S.dve(lambda e, which=which, Uf=Uf, bidx=bidx: e.tensor_scalar(out=acc[which][:, 0:1024], in0=Uf[:, 2:1026],
                                                                                      scalar1=wc[:, bidx, 2:3], scalar2=wc[:, bidx, 3:4],
                                                                                      op0=ALU.mult, op1=ALU.add),
                              reads=[("U", which), "wc"], writes=[("acc", which)])
                        S.dve(lambda e, which=which, Uf=Uf, bidx=bidx: e.scalar_tensor_tensor(out=acc[which][:, 0:1024], in0=Uf[:, 1:1025],
                                                                                             scalar=wc[:, bidx, 1:2], in1=acc[which][:, 0:1024],
                                                                                             op0=ALU.mult, op1=ALU.add),
                              reads=[("U", which), "wc", ("acc", which)], writes=[("acc", which)])
                        S.dve(lambda e, which=which, Uf=Uf, bidx=bidx: e.scalar_tensor_tensor(out=acc[which][:, 0:1024], in0=Uf[:, 0:1024],
                                                                                             scalar=wc[:, bidx, 0:1], in1=acc[which][:, 0:1024],
                                                                                             op0=ALU.mult, op1=ALU.add),
                              reads=[("U", which), "wc", ("acc", which)], writes=[("acc", which)])
                        S.act(lambda e, which=which, bidx=bidx: e.activation(out=US[which][:, :, 0:2],
                                                                            in_=cst[:, bidx, :].rearrange("p (b j) -> p b j", j=2), func=AF.Copy),
                              reads=["cst"], writes=[("US", which, 0)])
                        S.act(lambda e, which=which, Uf=Uf: e.activation(out=US[which][:, :, 2:6],
                                                                        in_=Uf[:, 1026:1090].rearrange("p (b t) -> p b t", t=4), func=AF.Copy),
                              reads=[("U", which)], writes=[("US", which, 1)])
                        accs = acc[which][:, 1024:1088].rearrange("p (b t) -> p b t", t=4)
                        S.dve(lambda e, which=which, bidx=bidx, accs=accs: e.tensor_scalar(out=accs, in0=US[which][:, :, 2:6],
                                                                                          scalar1=wc[:, bidx, 2:3], scalar2=wc[:, bidx, 3:4],
                                                                                          op0=ALU.mult, op1=ALU.add),
                              reads=[("US", which, 0), ("US", which, 1), "wc", ("acc", which)], writes=[("acc", which)])
                        S.dve(lambda e, which=which, bidx=bidx, accs=accs: e.scalar_tensor_tensor(out=accs, in0=US[which][:, :, 1:5],
                                                                                                 scalar=wc[:, bidx, 1:2], in1=accs,
                                                                                                 op0=ALU.mult, op1=ALU.add),
                              reads=[("US", which, 0), ("US", which, 1), "wc", ("acc", which)], writes=[("acc", which)])
                        S.dve(lambda e, which=which, bidx=bidx, accs=accs: e.scalar_tensor_tensor(out=accs, in0=US[which][:, :, 0:4],
                                                                                                 scalar=wc[:, bidx, 0:1], in1=accs,
                                                                                                 op0=ALU.mult, op1=ALU.add),
                              reads=[("US", which, 0), ("US", which, 1), "wc", ("acc", which)], writes=[("acc", which)])
                        ci = (blk * 2 + which) % 2
                        S.act(lambda e, ci=ci, Uf=Uf: e.activation(out=cn[ci][:, 0:2], in_=Uf[:, 1024:1026], func=AF.Copy),
                              reads=[("U", which)], writes=[("cn", ci, 0)])
                        S.act(lambda e, ci=ci, which=which: e.activation(out=cn[ci][:, 2:34].rearrange("p (b j) -> p b j", j=2),
                                                                        in_=US[which][:, :, 4:6], func=AF.Copy),
                              reads=[("US", which, 1)], writes=[("cn", ci, 1)])
                        S.dma("sp", o_convT[bidx * 128:(bidx + 1) * 128, :], cn[ci][:], reads=[("cn", ci, 0), ("cn", ci, 1)],
                              writes=[("oc", bidx)])
                    S.act(lambda e: e.activation(out=sgt[:], in_=acc[0][:], func=AF.Silu), reads=[("acc", 0)], writes=["sgt"])
                    S.dve(lambda e, ai=ai: e.tensor_tensor(out=ao[ai][:], in0=sgt[:], in1=acc[1][:], op=ALU.mult),
                          reads=["sgt", ("acc", 1)], writes=[("ao", ai)])
                    S.dma("sp", ACTT[0:8, :, blk, :].rearrange("t p c -> p t c"), ao[ai][:, 0:1024].rearrange("p (t c) -> p t c", c=128),
                          reads=[("ao", ai)], writes=[("ACTT", blk, 0)])
                    S.dma("sp", ACTT[8, :, blk, 0:64], ao[ai][:, 1024:1088], reads=[("ao", ai)], writes=[("ACTT", blk, 1)])
            S.end_phase()

        with ExitStack() as ph:
            sb = lambda n, s, dt=F32: ph.enter_context(nc.sbuf_tensor("pG" + n, list(s), dt))
            ps = lambda n, s, dt=F32: ph.enter_context(nc.psum_tensor("psG" + n, list(s), dt))
            wd = [sb("wd%d" % i, [128, 86, 256], BF16) for i in range(2)]
            at = [sb("at%d" % i, [128, 86, 128], BF16) for i in range(2)]
            GT = [sb("GT%d" % g, [128, D]) for g in range(2)]
            xb = [sb("xb%d" % i, [128, 256]) for i in range(3)]
            tb = [sb("tb%d" % i, [128, 256]) for i in range(3)]
            pp = [ps("p%d" % i, [128, 512]) for i in range(3)]
            for g in range(2):
                S.dma("sp", GT[g][:], MOD[g, :, 5 * D:6 * D], writes=[("GT", g)])
            cnt = 0
            def ldwG(n):
                S.dma("pool", wd[n % 2][:], w_down[:, n * 256:(n + 1) * 256].rearrange("(k p) n -> p k n", p=128), writes=[("wd", n % 2)])
            def g_loads(idx):
                n_, tt_ = idx // 9, idx % 9
                S.dma("sp", at[idx % 2][:, 0:43, :], ACTT[tt_][:, 0:43, :], writes=[("at", idx % 2, 0)])
                S.dma("aq", at[idx % 2][:, 43:86, :], ACTT[tt_][:, 43:86, :], writes=[("at", idx % 2, 1)])
                S.dma("sp", xb[idx % 3][:], X1[tt_ + 1][:, n_ * 256:(n_ + 1) * 256], writes=[("xb", idx % 3)])
            ldwG(0)
            g_loads(0)
            for n in range(16):
                wi = n % 2
                cs = slice(n * 256, (n + 1) * 256)
                if n + 1 < 16:
                    ldwG(n + 1)
                for tt in range(9):
                    a = cnt % 3
                    a2 = cnt % 2
                    cnt += 1
                    g = 1 if tt == 8 else 0
                    m = 64 if tt == 8 else 128
                    if cnt < 16 * 9:
                        g_loads(cnt)

                    def mm(e, a=a, a2=a2, wi=wi, m=m):
                        for k in range(86):
                            ins = e.matmul(pp[a][0:m, 0:256], lhsT=at[a2][:, k, 0:m], rhs=wd[wi][:, k, :], start=(k == 0), stop=(k == 85))
                        return ins
                    S.pe(mm, reads=[("at", a2, 0), ("at", a2, 1), ("wd", wi)], writes=[("p", a)])
                    S.dve(lambda e, a=a, g=g, cs=cs, m=m: e.tensor_tensor(out=tb[a][0:m, :], in0=pp[a][0:m, 0:256], in1=GT[g][0:m, cs], op=ALU.mult),
                          reads=[("p", a), ("GT", g)], writes=[("tb", a)])
                    S.dve(lambda e, a=a, m=m: e.tensor_tensor(out=tb[a][0:m, :], in0=tb[a][0:m, :], in1=xb[a][0:m, :], op=ALU.add),
                          reads=[("tb", a), ("xb", a)], writes=[("tb", a)])
                    S.dma("pool", X2[tt][0:m, cs], tb[a][0:m, :], reads=[("tb", a)], writes=[("X2", tt, n)])
            S.end_phase()

        with ExitStack() as ph:
            sb = lambda n, s, dt=F32: ph.enter_context(nc.sbuf_tensor("pH" + n, list(s), dt))
            gf = sb("gf", [128, D])
            epsb = sb("eps", [128, 1])
            ssq = sb("ssq", [128, 18])
            xt = [sb("xt%d" % i, [128, D]) for i in range(3)]
            yo = [sb("yo%d" % i, [128, D]) for i in range(2)]
            junk = sb("junk", [128, D], BF16)
            S.dma("sp", gf[:], gf_bc, writes=["gf"])
            S.dve(lambda e: e.memset(epsb[:], EPS), writes=["eps"])
            S.dve(lambda e: e.memset(ssq[:], 0.0), writes=["ssq"])
            for tt in range(9):
                a = tt % 3
                b2 = tt % 2
                m = 64 if tt == 8 else 128
                S.dma("sp", xt[a][0:m, :], X2[tt][0:m, :], writes=[("xt", a)])
                S.act(lambda e, a=a, tt=tt, m=m: e.activation(out=junk[0:m, :], in_=xt[a][0:m, :], func=AF.Square,
                                                             accum_out=ssq[0:m, 2 * tt:2 * tt + 1]),
                      reads=[("xt", a), "ssq"], writes=["junk", ("ssq", tt)])
                S.act(lambda e, tt=tt, m=m: e.activation(out=ssq[0:m, 2 * tt + 1:2 * tt + 2], in_=ssq[0:m, 2 * tt:2 * tt + 1],
                                                        func=AF.Sqrt, bias=epsb[0:m, 0:1], scale=1.0 / D),
                      reads=[("ssq", tt), "eps"], writes=[("ssq2", tt)])
                S.dve(lambda e, tt=tt, m=m: e.reciprocal(out=ssq[0:m, 2 * tt + 1:2 * tt + 2], in_=ssq[0:m, 2 * tt + 1:2 * tt + 2]),
                      reads=[("ssq2", tt)], writes=[("ssq2", tt)])
                S.dve(lambda e, a=a, b2=b2, tt=tt, m=m: e.scalar_tensor_tensor(out=yo[b2][0:m, :], in0=xt[a][0:m, :],
                                                                              scalar=ssq[0:m, 2 * tt + 1:2 * tt + 2], in1=gf[0:m, :],
                                                                              op0=ALU.mult, op1=ALU.mult),
                      reads=[("xt", a), ("ssq2", tt), "gf"], writes=[("yo", b2)])
                dst = y_s if tt == 8 else y_p[tt]
                S.dma("pool", dst, yo[b2][0:m, :], reads=[("yo", b2)], writes=[("y", tt)])
            S.end_phase()
    return nc


def _constants():
    p = np.arange(128)
    same64 = (p[:, None] // 64) == (p[None, :] // 64)
    TIp = (same64 & (p[:, None] <= p[None, :])).astype(np.float32)
    Up = (same64 & (p[:, None] > p[None, :])).astype(np.float32)
    val = (p[:, None] < 64) & (p[None, :] < 64)
    same4 = ((p[:, None] // 4) == (p[None, :] // 4)) & val
    TIs = (same4 & (p[:, None] <= p[None, :])).astype(np.float32)
    Us = (same4 & (p[:, None] > p[None, :])).astype(np.float32)
    U128 = (p[:, None] > p[None, :]).astype(np.float32)
    cmat = np.stack([-TIp / 16.0, -Up / 16.0, -TIs / 16.0, -Us / 16.0, -U128 / 16.0], axis=1).astype(np.float32)
    ti4 = np.stack([np.repeat(TIp[:, None, :], 4, axis=1), np.repeat(TIs[:, None, :], 4, axis=1)], axis=1).astype(np.float32)
    negcol = np.full((128, 1), -1.0 / 16.0, np.float32)
    bm = ((p[:, None] // 4) == np.arange(16)[None, :]) & (p[:, None] < 64)
    bmask = bm.astype(np.float32)
    bsel16 = (-bmask / 16.0).astype(np.float32)
    slopes = np.exp2(-8.0 * np.arange(1, 17, dtype=np.float32) / 16.0).astype(np.float32)
    i = np.arange(128)[:, None]
    j = np.arange(256)[None, :]
    d = (i + 128 - j).astype(np.float32)
    valid = (d >= 0) & (d <= 128)
    bias_gen = np.where(valid[:, None, :], -slopes[None, :, None] * d[:, None, :], NEG).astype(np.float32)
    bias_first = bias_gen.copy()
    bias_first[:, :, 0:128] = NEG
    r = np.arange(16)
    g_ = r // 4
    t_ = r % 4
    jb = np.arange(128)
    bias_sb = np.zeros((16, 4, 128), np.float32)
    bias_sn = np.full((16, 4, 16, 64), NEG, np.float32)
    for kvh in range(4):
        sl = slopes[kvh * 4 + g_]
        dd = (t_[:, None] + 128 - jb[None, :]).astype(np.float32)
        ok = (dd >= 0) & (dd <= 128)
        bias_sb[:, kvh, :] = np.where(ok, -sl[:, None] * dd, NEG)
        for b in range(16):
            for tp in range(4):
                dn = (t_ - tp).astype(np.float32)
                okn = dn >= 0
                bias_sn[:, kvh, b, 4 * b + tp] = np.where(okn, -sl * dn, NEG)
    return dict(ident=np.eye(128, dtype=np.float32), cmat=cmat, ti4=ti4, negcol=negcol, bsel16=bsel16, bmask=bmask,
                bias_gen=bias_gen, bias_first=bias_first, bias_sb=bias_sb, bias_sn=bias_sn)


_NC_CACHE = {}


def kernel(x_prompt, x_sample, c_prompt, c_sample, state_gla, state_swa_k, state_swa_v,
           state_ffn_conv, w_ada, b_ada, g_norm, w_in, w_a_up, b_a, g_gla, swa_sinks,
           w_o, w_up, w_conv, b_conv, w_down, g_final):
    f32 = np.float32
    A = lambda a: np.ascontiguousarray(np.asarray(a, dtype=f32))
    x_prompt, x_sample, c_prompt, c_sample = A(x_prompt), A(x_sample), A(c_prompt), A(c_sample)
    state_gla, state_swa_k, state_swa_v, state_ffn_conv = A(state_gla), A(state_swa_k), A(state_swa_v), A(state_ffn_conv)
    w_ada, b_ada, g_norm, w_in, w_a_up, b_a = A(w_ada)[0], A(b_ada)[0], A(g_norm)[0], A(w_in)[0], A(w_a_up)[0], A(b_a)[0]
    g_gla, swa_sinks, w_o, w_up, w_conv, b_conv, w_down, g_final = (A(g_gla)[0], A(swa_sinks)[0], A(w_o)[0], A(w_up)[0],
                                                                    A(w_conv)[0], A(b_conv)[0], A(w_down)[0], A(g_final))
    C = _constants()
    rep = lambda v, n=128: np.ascontiguousarray(np.broadcast_to(v[None], (n,) + v.shape))
    shared = dict(
        w_ada=w_ada, bada_bc=rep(b_ada), gn_bc=rep(g_norm), w_in=w_in,
        w17=np.ascontiguousarray(np.concatenate([w_a_up, b_a[None, :]], axis=0)),
        ggla_bc=rep(g_gla), sink_bc=rep(swa_sinks),
        sink_s=np.ascontiguousarray(swa_sinks.reshape(4, 4)[:, np.arange(16) // 4].T),
        w_o=w_o, w_up=w_up,
        w_convT=np.ascontiguousarray(np.concatenate([w_conv, b_conv[None, :]], axis=0).T),
        w_down=w_down, gf_bc=rep(g_final),
        ident=C["ident"], cmat=C["cmat"], ti4=C["ti4"], negcol=C["negcol"], bsel16=C["bsel16"], bmask=C["bmask"],
        bias_sb=C["bias_sb"], bias_sn=C["bias_sn"],
    )
    xp = x_prompt[0]
    xpad = np.concatenate([np.zeros((7168, D), f32), xp], axis=0)
    in_maps = []
    for c in range(NCORES):
        start = 1024 * c
        xa = np.zeros((NT, 128, D), f32)
        xa[0:64] = xpad[start:start + 8192].reshape(64, 128, D)
        xa[64, 0:64] = x_sample[16 * c:16 * c + 16].reshape(64, D)
        pm = np.zeros((128, NT), f32)
        for t in range(64):
            g0 = start - 7168 + t * 128
            pm[:, t] = 1.0 if g0 >= 0 else 0.0
        pm[0:64, 64] = 1.0
        cT = np.zeros((128, KC, 256), f32)
        cT[:, :, 0:128] = c_prompt[0].reshape(KC, 128).T[:, :, None]
        cs = np.repeat(c_sample[16 * c:16 * c + 16], 4, axis=0)
        cT[:, :, 128:192] = cs.reshape(64, KC, 128).transpose(2, 1, 0)
        bias_p = np.stack([C["bias_first"] if c == 0 else C["bias_gen"], C["bias_gen"]], axis=1)
        sk = state_swa_k[0, 16 * c:16 * c + 16]
        sv = state_swa_v[0, 16 * c:16 * c + 16]
        m = dict(shared)
        m.update(
            xall=xa, cT=cT, st_gla=np.ascontiguousarray(state_gla[0, 16 * c:16 * c + 16]),
            st_k=np.ascontiguousarray(sk.reshape(16, 128, 512)), st_v=np.ascontiguousarray(sv.reshape(16, 128, 512)),
            st_kT=np.ascontiguousarray(sk.transpose(0, 2, 3, 1)),
            st_convT=np.ascontiguousarray(state_ffn_conv[0, 16 * c:16 * c + 16].reshape(32, F2).T),
            pmask=pm, hmask=np.full((128, 1), 0.0 if c == 0 else 1.0, f32), bias_p=np.ascontiguousarray(bias_p),
        )
        in_maps.append(m)
    if "nc" not in _NC_CACHE:
        _NC_CACHE["nc"] = build_program()
    nc = _NC_CACHE["nc"]
    res = run_bass_kernel_spmd(nc, in_maps, core_ids=list(range(NCORES)))
    R = res.results
    y_prompt = np.concatenate([R[c]["y_p"].reshape(1024, D) for c in range(NCORES)], axis=0)[None]
    y_sample = np.concatenate([R[c]["y_s"] for c in range(NCORES)], axis=0).reshape(128, 4, D)
    gla_p = R[7]["o_gla_p"].reshape(1, 1, 4, 256, 512)
    k_p = R[7]["o_k_p"].reshape(1, 1, 128, 4, 128)
    v_p = R[7]["o_v_p"].reshape(1, 1, 128, 4, 128)
    conv_p = np.ascontiguousarray(R[7]["o_convT"][:, 0:2].T).reshape(1, 1, 2, F2)
    gla_s = np.concatenate([R[c]["o_gla_s"] for c in range(NCORES)], axis=0)[None]
    k_s = np.concatenate([R[c]["o_k_s"] for c in range(NCORES)], axis=0).reshape(1, 128, 128, 4, 128)
    v_s = np.concatenate([R[c]["o_v_s"] for c in range(NCORES)], axis=0).reshape(1, 128, 128, 4, 128)
    conv_s = np.concatenate([R[c]["o_convT"][:, 2:34].T.reshape(16, 2, F2) for c in range(NCORES)], axis=0)[None]
    outs = (y_prompt, y_sample, gla_p, k_p, v_p, conv_p, gla_s, k_s, v_s, conv_s)
    return tuple(np.ascontiguousarray(o, dtype=f32) for o in outs)
```

```python
import numpy as np
from contextlib import ExitStack
import concourse.bass as bass
import concourse.mybir as mybir
from concourse.bass_utils import run_bass_kernel_spmd

F32 = mybir.dt.float32
BF16 = mybir.dt.bfloat16
AF = mybir.ActivationFunctionType
ALU = mybir.AluOpType
AX = mybir.AxisListType

NCORES = 8
D = 4096
KC = 32
NPRE = 55
NFULL = 10
NT = 65
INW = 9232
F2 = 22016
DFF = 11008
NBLK = 172
OFF_GQ, OFF_GK, OFF_GV, OFF_GR, OFF_GA, OFF_SQ, OFF_SK, OFF_SV = 0, 1024, 2048, 4096, 6144, 6160, 8208, 8720
PW = 3088
NEG = -30000.0
EPS = 1e-6

COMPUTE = ("pe", "act", "dve")
QUEUES = ("sp", "pool", "aq")
ALLENG = COMPUTE + QUEUES


class Sched:
    def __init__(self, nc, stack, ndma=8):
        self.nc = nc
        self.sems = []
        self.eng_sem = {e: self._mk(stack, "s_" + e) for e in COMPUTE}
        self.eng_cnt = {e: 0 for e in COMPUTE}
        self.dma_pool = {q: [self._mk(stack, "d_%s%d" % (q, i)) for i in range(ndma)] for q in QUEUES}
        self.dma_n = {q: 0 for q in QUEUES}
        self.ops = {e: [] for e in ALLENG}
        self.known = {e: {} for e in ALLENG}
        self.lastw = {}
        self.reads = {}
        self.flip = 0
        self.seq = 0

    def _mk(self, stack, name):
        s = stack.enter_context(self.nc.semaphore(name))
        self.sems.append(s)
        return len(self.sems) - 1

    def add(self, eng, fn, reads=(), writes=()):
        need = {}

        def dep(tok, kind):
            teng, si, val = tok
            if teng == eng and eng in COMPUTE:
                if eng == "pe" or kind != "raw":
                    return
            need[si] = max(need.get(si, 0), val)

        for b in reads:
            w = self.lastw.get(b)
            if w is not None:
                dep(w, "raw")
        for b in writes:
            w = self.lastw.get(b)
            if w is not None:
                dep(w, "waw")
            for r in self.reads.get(b, ()):
                dep(r, "war")
        if eng in COMPUTE:
            self.eng_cnt[eng] += 1
            si = self.eng_sem[eng]
            tok = (eng, si, self.eng_cnt[eng])
            inc = (si, 1)
        else:
            n = self.dma_n[eng]
            pool = self.dma_pool[eng]
            P = len(pool)
            si = pool[n % P]
            val = 16 * (n // P + 1)
            if n >= P:
                need[si] = max(need.get(si, 0), 16 * (n // P))
            tok = (eng, si, val)
            inc = (si, 16)
            self.dma_n[eng] += 1
        kn = self.known[eng]
        waits = []
        for si, val in need.items():
            if kn.get(si, 0) >= val:
                continue
            kn[si] = val
            waits.append((si, val))
        self.seq += 1
        self.ops[eng].append((waits, fn, inc, self.seq))
        for b in writes:
            self.lastw[b] = tok
            self.reads[b] = []
        for b in reads:
            if b in writes:
                continue
            lst = self.reads.setdefault(b, [])
            if eng in COMPUTE:
                lst[:] = [t for t in lst if t[0] != eng]
            lst.append(tok)
        return tok

    def pe(self, fn, reads=(), writes=()):
        return self.add("pe", fn, reads, writes)

    def act(self, fn, reads=(), writes=()):
        return self.add("act", fn, reads, writes)

    def dve(self, fn, reads=(), writes=()):
        return self.add("dve", fn, reads, writes)

    def any(self, fn, reads=(), writes=()):
        self.flip ^= 1
        if self.flip:
            return self.add("act", lambda e: fn(e, True), reads, writes)
        return self.add("dve", lambda e: fn(e, False), reads, writes)

    def dma(self, q, out, in_, reads=(), writes=()):
        return self.add(q, lambda e: e.dma_start(out=out, in_=in_), reads, writes)

    def barrier(self):
        targets = []
        for e in COMPUTE:
            if self.eng_cnt[e] > 0:
                targets.append((self.eng_sem[e], self.eng_cnt[e]))
        for q in QUEUES:
            pool = self.dma_pool[q]
            P = len(pool)
            n = self.dma_n[q]
            for i, si in enumerate(pool):
                cnt = (n - i + P - 1) // P if n > i else 0
                if cnt > 0:
                    targets.append((si, 16 * cnt))
        for eng in ALLENG:
            kn = self.known[eng]
            waits = []
            for si, val in targets:
                if kn.get(si, 0) >= val:
                    continue
                kn[si] = val
                waits.append((si, val))
            if waits:
                self.seq += 1
                self.ops[eng].append((waits, None, None, self.seq))
        self.lastw = {}
        self.reads = {}

    def emit(self):
        nc = self.nc
        sems = self.sems
        ops = self.ops

        def replay(name, eng):
            lst = ops[name]
            if name == "act":
                lst = sorted(ops["act"] + ops["aq"], key=lambda o: o[3])
            for waits, fn, inc, _ in lst:
                for si, val in waits:
                    eng.wait_ge(sems[si], val)
                if fn is not None:
                    ins = fn(eng)
                    ins.then_inc(sems[inc[0]], inc[1])

        with nc.Block() as block:
            @block.tensor
            def _(e):
                replay("pe", e)

            @block.scalar
            def _(e):
                replay("act", e)

            @block.vector
            def _(e):
                replay("dve", e)

            @block.sync
            def _(e):
                replay("sp", e)

            @block.gpsimd
            def _(e):
                replay("pool", e)
        self.ops = {e: [] for e in ALLENG}

    def end_phase(self):
        self.barrier()
        self.emit()


def acopy(e, is_act, out, in_):
    if is_act:
        return e.activation(out=out, in_=in_, func=AF.Copy)
    return e.tensor_copy(out=out, in_=in_)


def build_program():
    nc = bass.Bass("TRN2", target_bir_lowering=False)

    def din(name, shape):
        return nc.dram_tensor(name, list(shape), F32, kind="ExternalInput").ap()

    def dout(name, shape):
        return nc.dram_tensor(name, list(shape), F32, kind="ExternalOutput").ap()

    def dscr(name, shape, dt=F32):
        return nc.dram_tensor(name, list(shape), dt).ap()

    xall = din("xall", [NT, 128, D])
    cT_in = din("cT", [128, KC, 256])
    st_gla = din("st_gla", [16, 4, 256, 512])
    st_k = din("st_k", [16, 128, 512])
    st_v = din("st_v", [16, 128, 512])
    st_kT = din("st_kT", [16, 4, 128, 128])
    st_convT = din("st_convT", [F2, 32])
    w_ada = din("w_ada", [D, 6 * D])
    bada_bc = din("bada_bc", [128, 6 * D])
    gn_bc = din("gn_bc", [128, 2, D])
    w_in = din("w_in", [D, INW])
    w17 = din("w17", [17, 1024])
    ggla_bc = din("ggla_bc", [128, 512])
    sink_bc = din("sink_bc", [128, 16])
    sink_s = din("sink_s", [16, 4])
    w_o = din("w_o", [D, D])
    w_up = din("w_up", [D, F2])
    w_convT = din("w_convT", [F2, 4])
    w_down = din("w_down", [DFF, D])
    gf_bc = din("gf_bc", [128, D])
    ident_in = din("ident", [128, 128])
    cmat = din("cmat", [128, 5, 128])
    ti4 = din("ti4", [128, 2, 4, 128])
    negcol = din("negcol", [128, 1])
    bsel16 = din("bsel16", [128, 16])
    bmask = din("bmask", [128, 16])
    pmask = din("pmask", [128, NT])
    hmask = din("hmask", [128, 1])
    bias_p = din("bias_p", [128, 2, 16, 256])
    bias_sb = din("bias_sb", [16, 4, 128])
    bias_sn = din("bias_sn", [16, 4, 16, 64])

    y_p = dout("y_p", [8, 128, D])
    y_s = dout("y_s", [64, D])
    o_gla_p = dout("o_gla_p", [4, 256, 512])
    o_k_p = dout("o_k_p", [128, 512])
    o_v_p = dout("o_v_p", [128, 512])
    o_convT = dout("o_convT", [F2, 34])
    o_gla_s = dout("o_gla_s", [16, 4, 256, 512])
    o_k_s = dout("o_k_s", [16, 128, 512])
    o_v_s = dout("o_v_s", [16, 128, 512])

    MOD = dscr("MOD", [2, 128, 6 * D])
    HT = dscr("HT", [NT, 128, KC * 128], BF16)
    PP = dscr("PP", [NPRE, 128, PW])
    PH2 = dscr("PH2", [128, 1024])
    PF = dscr("PF", [NFULL, 128, INW])
    MIX = dscr("MIX", [NFULL, 128, D], BF16)
    X1 = dscr("X1", [NFULL, 128, D])
    H2T = dscr("H2T", [NFULL, 128, KC * 128], BF16)
    ACTT = dscr("ACTT", [9, 128, 86, 128], BF16)
    X2 = dscr("X2", [9, 128, D])
    SCB = dscr("SCB", [128, KC * 256], BF16)

    with ExitStack() as top:
        S = Sched(nc, top)

        with ExitStack() as ph:
            sb = lambda n, s, dt=F32: ph.enter_context(nc.sbuf_tensor(n, list(s), dt))
            ps = lambda n, s, dt=F32: ph.enter_context(nc.psum_tensor(n, list(s), dt))
            cTs = sb("cTs", [128, KC, 256])
            scb = sb("scb", [128, KC, 256], BF16)
            wb = [sb("wada%d" % i, [128, KC, 512], BF16) for i in range(2)]
            bb = [sb("bada%d" % i, [128, 512]) for i in range(2)]
            mo = [sb("mo%d" % i, [128, 2, 512]) for i in range(2)]
            pm = [ps("pm%d" % i, [128, 2, 512]) for i in range(2)]
            S.dma("sp", cTs[:], cT_in, writes=["cTs"])
            S.act(lambda e: e.activation(out=scb[:], in_=cTs[:], func=AF.Silu), reads=["cTs"], writes=["scb"])
            def ldw0(n):
                S.dma("pool", wb[n % 2][:], w_ada[:, n * 512:(n + 1) * 512].rearrange("(k p) n -> p k n", p=128), writes=[("w", n % 2)])
            S.dma("pool", SCB, scb[:].rearrange("p k m -> p (k m)"), reads=["scb"], writes=["SCB"])
            ldw0(0)
            for n in range(16):
                i = n % 2
                cs = slice(n * 512, (n + 1) * 512)
                if n + 1 < 16:
                    ldw0(n + 1)
                S.dma("sp", bb[i][:], bada_bc[:, cs], writes=[("b", i)])

                def mm(e, i=i):
                    for g in range(2):
                        for k in range(KC):
                            ins = e.matmul(pm[i][:, g, :], lhsT=scb[:, k, g * 128:(g + 1) * 128], rhs=wb[i][:, k, :],
                                           start=(k == 0), stop=(k == KC - 1))
                    return ins
                S.pe(mm, reads=["scb", ("w", i)], writes=[("pm", i)])
                for g in range(2):
                    S.dve(lambda e, i=i, g=g: e.tensor_tensor(out=mo[i][:, g, :], in0=pm[i][:, g, :], in1=bb[i][:], op=ALU.add),
                          reads=[("pm", i), ("b", i)], writes=[("mo", i, g)])
                S.dma("pool", MOD[:, :, cs].rearrange("g p n -> p g n"), mo[i][:],
                      reads=[("mo", i, 0), ("mo", i, 1)], writes=[("MOD", n)])
            S.end_phase()

        def norm_phase(tag, src_tiles, mod_sel, off_sh, off_sc, gidx, dst):
            with ExitStack() as ph:
                sb = lambda n, s, dt=F32: ph.enter_context(nc.sbuf_tensor(tag + n, list(s), dt))
                ps = lambda n, s, dt=F32: ph.enter_context(nc.psum_tensor(tag + "ps" + n, list(s), dt))
                G = [sb("G%d" % g, [128, D]) for g in range(2)]
                SH = [sb("SH%d" % g, [128, D]) for g in range(2)]
                gn = sb("gn", [128, D])
                idf = sb("idf", [128, 128])
                idb = sb("idb", [128, 128], BF16)
                epsb = sb("eps", [128, 1])
                ssq = sb("ssq", [128, 2 * len(src_tiles)])
                xt = [sb("xt%d" % i, [128, D]) for i in range(3)]
                junk = sb("junk", [128, D], BF16)
                hb = [sb("hb%d" % i, [128, D], BF16) for i in range(2)]
                hT = [sb("hT%d" % i, [128, KC, 128], BF16) for i in range(2)]
                ptr = [ps("ptr%d" % i, [128, 16, 128], BF16) for i in range(3)]
                S.dma("sp", gn[:], gn_bc[:, gidx, :], writes=["gn"])
                S.dma("sp", idf[:], ident_in, writes=["idf"])
                S.dve(lambda e: e.tensor_copy(out=idb[:], in_=idf[:]), reads=["idf"], writes=["idb"])
                S.dve(lambda e: e.memset(epsb[:], EPS), writes=["eps"])
                S.dve(lambda e: e.memset(ssq[:], 0.0), writes=["ssq"])
                for g in range(2):
                    S.dma("sp", SH[g][:], MOD[g, :, off_sh:off_sh + D], writes=[("SH", g)])
                    S.dma("sp", G[g][:], MOD[g, :, off_sc:off_sc + D], writes=[("G", g)])
                    S.dve(lambda e, g=g: e.scalar_tensor_tensor(out=G[g][:], in0=G[g][:], scalar=1.0, in1=gn[:],
                                                                op0=ALU.add, op1=ALU.mult),
                          reads=[("G", g), "gn"], writes=[("G", g)])
                nptr = 0
                for i, src in enumerate(src_tiles):
                    g = mod_sel[i]
                    a = i % 3
                    b2 = i % 2
                    S.dma("sp", xt[a][:], src, writes=[("xt", a)])
                    S.act(lambda e, a=a, i=i: e.activation(out=junk[:], in_=xt[a][:], func=AF.Square,
                                                           accum_out=ssq[:, 2 * i:2 * i + 1]),
                          reads=[("xt", a), "ssq"], writes=["junk", ("ssq", i)])
                    S.act(lambda e, i=i: e.activation(out=ssq[:, 2 * i + 1:2 * i + 2], in_=ssq[:, 2 * i:2 * i + 1],
                                                      func=AF.Sqrt, bias=epsb[:, 0:1], scale=1.0 / D),
                          reads=[("ssq", i), "eps"], writes=[("ssq2", i)])
                    S.dve(lambda e, i=i: e.reciprocal(out=ssq[:, 2 * i + 1:2 * i + 2], in_=ssq[:, 2 * i + 1:2 * i + 2]),
                          reads=[("ssq2", i)], writes=[("ssq2", i)])
                    S.dve(lambda e, a=a, b2=b2, i=i, g=g: e.scalar_tensor_tensor(
                        out=xt[a][:], in0=xt[a][:], scalar=ssq[:, 2 * i + 1:2 * i + 2], in1=G[g][:],
                        op0=ALU.mult, op1=ALU.mult),
                        reads=[("xt", a), ("ssq2", i), ("G", g)], writes=[("xt", a)])
                    S.dve(lambda e, a=a, b2=b2, g=g: e.tensor_tensor(out=hb[b2][:], in0=xt[a][:], in1=SH[g][:], op=ALU.add),
                          reads=[("xt", a), ("SH", g)], writes=[("hb", b2)])
                    for half in range(2):
                        pj = nptr % 3
                        nptr += 1

                        def tr(e, b2=b2, half=half, pj=pj):
                            for k in range(16):
                                kk = half * 16 + k
                                ins = e.transpose(out=ptr[pj][:, k, :], in_=hb[b2][:, kk * 128:(kk + 1) * 128], identity=idb[:])
                            return ins
                        S.pe(tr, reads=[("hb", b2), "idb"], writes=[("ptr", pj)])
                        S.any(lambda e, ia, b2=b2, half=half, pj=pj: acopy(e, ia, hT[b2][:, half * 16:(half + 1) * 16, :], ptr[pj][:]),
                              reads=[("ptr", pj)], writes=[("hT", b2, half)])
                    S.dma("pool", dst[i], hT[b2][:].rearrange("p k t -> p (k t)"),
                          reads=[("hT", b2, 0), ("hT", b2, 1)], writes=[("dst", i)])
                S.end_phase()

        norm_phase("nA", [xall[i] for i in range(NT)], [1 if i == NT - 1 else 0 for i in range(NT)],
                   0, D, 0, [HT[i] for i in range(NT)])

        ALLT = list(range(NT))
        FULLT = list(range(NPRE, NT))

        def pdest(t, c0, n):
            if t >= NPRE:
                return PF[t - NPRE][:, c0:c0 + n]
            if c0 >= OFF_SK:
                return PH2[:, c0 - OFF_SK:c0 - OFF_SK + n]
            if c0 == OFF_GA:
                return PP[t][:, 3072:3088]
            return PP[t][:, c0 - OFF_GK:c0 - OFF_GK + n]

        blocks = []
        for j in range(2):
            blocks.append((OFF_GQ + j * 512, 512, FULLT))
        for j in range(2):
            blocks.append((OFF_GK + j * 512, 512, ALLT))
        for j in range(4):
            blocks.append((OFF_GV + j * 512, 512, ALLT))
        for j in range(4):
            blocks.append((OFF_GR + j * 512, 512, FULLT))
        blocks.append((OFF_GA, 16, ALLT))
        for j in range(4):
            blocks.append((OFF_SQ + j * 512, 512, FULLT))
        blocks.append((OFF_SK, 512, [NPRE - 1] + FULLT))
        blocks.append((OFF_SV, 512, [NPRE - 1] + FULLT))

        with ExitStack() as ph:
            sb = lambda n, s, dt=F32: ph.enter_context(nc.sbuf_tensor("pB" + n, list(s), dt))
            ps = lambda n, s, dt=F32: ph.enter_context(nc.psum_tensor("psB" + n, list(s), dt))
            wb = [sb("w%d" % i, [128, KC, 512], BF16) for i in range(2)]
            hts = [sb("h%d" % i, [128, KC, 128], BF16) for i in range(4)]
            ob = [sb("o%d" % i, [128, 512]) for i in range(4)]
            pp = [ps("p%d" % i, [128, 512]) for i in range(4)]
            cnt = 0
            def ldwB(bi):
                c0, n, tl = blocks[bi]
                S.dma("pool", wb[bi % 2][:, :, 0:n], w_in[:, c0:c0 + n].rearrange("(k p) n -> p k n", p=128), writes=[("w", bi % 2)])
            ldwB(0)
            for bi, (c0, n, tl) in enumerate(blocks):
                wi = bi % 2
                if bi + 1 < len(blocks):
                    ldwB(bi + 1)
                for t in tl:
                    a = cnt % 4
                    cnt += 1
                    S.dma("sp", hts[a][:].rearrange("p k t -> p (k t)"), HT[t], writes=[("h", a)])

                    def mm(e, a=a, wi=wi, n=n):
                        for k in range(KC):
                            ins = e.matmul(pp[a][:, 0:n], lhsT=hts[a][:, k, :], rhs=wb[wi][:, k, 0:n],
                                           start=(k == 0), stop=(k == KC - 1))
                        return ins
                    S.pe(mm, reads=[("h", a), ("w", wi)], writes=[("p", a)])
                    S.any(lambda e, ia, a=a, n=n: acopy(e, ia, ob[a][:, 0:n], pp[a][:, 0:n]), reads=[("p", a)], writes=[("o", a)])
                    S.dma("pool", pdest(t, c0, n), ob[a][:, 0:n], reads=[("o", a)], writes=[("pd", t, c0)])
            S.end_phase()

        with ExitStack() as ph:
            sb = lambda n, s, dt=F32: ph.enter_context(nc.sbuf_tensor("pC" + n, list(s), dt))
            ps = lambda n, s, dt=F32: ph.enter_context(nc.psum_tensor("psC" + n, list(s), dt))
            idf = sb("idf", [128, 128])
            cm = sb("cm", [128, 5, 128])
            t4 = sb("t4", [128, 2, 4, 128])
            ncol = sb("ncol", [128, 1])
            bs16 = sb("bs16", [128, 16])
            bmk = sb("bmk", [128, 16])
            pmk = sb("pmk", [128, NT])
            w17s = sb("w17", [32, 1024])
            gg = sb("gg", [128, 512])
            one1 = sb("one1", [128, 1])
            epsb = sb("eps", [128, 1])
            St = sb("S", [128, 8, 512])
            Sb = sb("Sb", [128, 8, 512], BF16)
            kt = [sb("k%d" % i, [128, 1024]) for i in range(2)]
            vt = [sb("v%d" % i, [128, 2048]) for i in range(2)]
            gat = [sb("ga%d" % i, [128, 16]) for i in range(2)]
            qt = [sb("q%d" % i, [128, 1024]) for i in range(2)]
            rt = [sb("r%d" % i, [128, 2048]) for i in range(2)]
            ga17 = [sb("g17%d" % i, [32, 128]) for i in range(2)]
            e1 = sb("e1", [128, 1024])
            la2 = [sb("la%d" % i, [128, 1024]) for i in range(2)]
            er = sb("er", [128, 1024])
            kd2 = [sb("kd%d" % i, [128, 1024], BF16) for i in range(2)]
            vb2 = [sb("vb%d" % i, [128, 2048], BF16) for i in range(2)]
            dec2 = [sb("dec%d" % i, [128, 16]) for i in range(2)]
            eb = sb("eb", [128, 8, 128])
            enb = sb("enb", [128, 8, 128])
            qg = sb("qg", [128, 8, 128], BF16)
            kg = sb("kg", [128, 8, 128], BF16)
            qg0 = sb("qg0", [128, 8, 128], BF16)
            qg1 = sb("qg1", [128, 8, 128], BF16)
            AT = sb("AT", [128, 4, 128], BF16)
            sg = sb("sg", [128, 2048])
            ss = sb("ss", [128, 8])
            junk = sb("junk", [128, 512], BF16)
            tn = sb("tn", [128, 512])
            mixo = [sb("mx%d" % i, [128, 2048], BF16) for i in range(2)]
            qgm = sb("qgm", [128, 16, 8, 4], BF16)
            qgmf = sb("qgmf", [128, 8, 128], BF16)
            decs = sb("decs", [128, 8, 16])
            s0 = [sb("s0%d" % i, [128, 4, 512]) for i in range(2)]
            s0b = [sb("s0b%d" % i, [128, 4, 512], BF16) for i in range(2)]
            vm = [sb("vm%d" % i, [128, 1024], BF16) for i in range(2)]
            PA = ps("A", [128, 2, 512])
            PB = ps("B", [128, 8, 128])
            PAT = ps("AT", [128, 4, 128])
            PO = ps("O", [128, 2, 512])
            PDS = ps("DS", [128, 512])

            S.dma("sp", idf[:], ident_in, writes=["idf"])
            S.dma("sp", cm[:], cmat, writes=["cm"])
            S.dma("sp", t4[:], ti4, writes=["t4"])
            S.dma("sp", ncol[:], negcol, writes=["ncol"])
            S.dma("sp", bs16[:], bsel16, writes=["bs16"])
            S.dma("sp", bmk[:], bmask, writes=["bmk"])
            S.dma("sp", pmk[:], pmask, writes=["pmk"])
            S.dma("sp", w17s[0:17, :], w17, writes=["w17"])
            S.dma("sp", gg[:], ggla_bc, writes=["gg"])
            S.dve(lambda e: e.memset(one1[:], 1.0), writes=["one1"])
            S.dve(lambda e: e.memset(epsb[:], EPS), writes=["eps"])
            S.dve(lambda e: e.memset(St[:], 0.0), writes=["S"])
            S.dve(lambda e: e.memset(Sb[:], 0.0), writes=["Sb"])
            S.dve(lambda e: e.memset(qg0[:], 0.0), writes=["qg0"])
            S.dve(lambda e: e.memset(qg1[:], 0.0), writes=["qg1"])
            S.dve(lambda e: e.memset(qgmf[:], 0.0), writes=["qgmf"])
            for i in range(2):
                S.dve(lambda e, i=i: e.memset(ga17[i][:], 1.0), writes=[("g17", i)])

            def gla_tile(t):
                full = t >= NPRE
                samp = t == NT - 1
                f = t - NPRE
                a = t % 2
                la = la2[a]
                kd = kd2[a]
                vb = vb2[a]
                dec = dec2[a]
                ci = 2 if samp else 0
                if full:
                    src = PF[f]
                    S.dma("sp", kt[a][:], src[:, OFF_GK:OFF_GK + 1024], writes=[("k", a)])
                    S.dma("sp", vt[a][:], src[:, OFF_GV:OFF_GV + 2048], writes=[("v", a)])
                    S.dma("sp", gat[a][:], src[:, OFF_GA:OFF_GA + 16], writes=[("ga", a)])
                    S.dma("sp", qt[a][:], src[:, OFF_GQ:OFF_GQ + 1024], writes=[("q", a)])
                    S.dma("sp", rt[a][:], src[:, OFF_GR:OFF_GR + 2048], writes=[("r", a)])
                else:
                    S.dma("sp", kt[a][:], PP[t][:, 0:1024], writes=[("k", a)])
                    S.dma("sp", vt[a][:], PP[t][:, 1024:3072], writes=[("v", a)])
                    S.dma("sp", gat[a][:], PP[t][:, 3072:3088], writes=[("ga", a)])
                S.pe(lambda e, a=a: e.transpose(out=PA[0:16, 0, 0:128], in_=gat[a][:], identity=idf[:]),
                     reads=[("ga", a), "idf"], writes=["PA"])
                S.act(lambda e, a=a: e.activation(out=ga17[a][0:16, :], in_=PA[0:16, 0, 0:128], func=AF.Copy),
                      reads=["PA"], writes=[("g17", a)])

                def mmz(e, a=a):
                    for n in range(2):
                        ins = e.matmul(PA[:, n, :], lhsT=ga17[a][0:17, :], rhs=w17s[0:17, n * 512:(n + 1) * 512],
                                       start=True, stop=True)
                    return ins
                S.pe(mmz, reads=[("g17", a), "w17"], writes=["PA"])
                S.act(lambda e: e.activation(out=e1[:], in_=PA[:].rearrange("p a b -> p (a b)"), func=AF.Exp, scale=-1.0),
                      reads=["PA"], writes=["e1"])
                S.act(lambda e: e.activation(out=la[:], in_=e1[:], func=AF.Ln, bias=one1[:, 0:1], scale=1.0),
                      reads=["e1", "one1"], writes=[("la", a)])

                ui = (ci + 1) if full else 4

                def mmr(e, ui=ui):
                    for n in range(2):
                        ins = e.matmul(PA[:, n, :], lhsT=cm[:, ui, :], rhs=la[:, n * 512:(n + 1) * 512], start=True, stop=True)
                    return ins
                S.pe(mmr, reads=[("la", a), "cm"], writes=["PA"])
                S.act(lambda e: e.activation(out=er[:], in_=PA[:].rearrange("p a b -> p (a b)"), func=AF.Exp),
                      reads=["PA"], writes=["er"])
                S.dve(lambda e, a=a, t=t: e.scalar_tensor_tensor(out=kd[:], in0=kt[a][:], scalar=pmk[:, t:t + 1], in1=er[:],
                                                                 op0=ALU.mult, op1=ALU.mult),
                      reads=[("k", a), "pmk", "er"], writes=[("kd", a)])
                S.act(lambda e, a=a: e.activation(out=vb[:], in_=vt[a][:], func=AF.Copy), reads=[("v", a)], writes=[("vb", a)])
                if not full:
                    def mmd(e):
                        for c in range(8):
                            ins = e.matmul(PAT[:, 0, c:c + 1], lhsT=la[:, c * 128:(c + 1) * 128], rhs=ncol[:, 0:1], start=True, stop=True)
                        return ins
                    S.pe(mmd, reads=[("la", a), "ncol"], writes=["PAT"])
                    S.act(lambda e: e.activation(out=dec[:, 0:8], in_=PAT[:, 0, 0:8], func=AF.Exp), reads=["PAT"], writes=[("dec", a)])
                elif not samp:
                    def mmd(e):
                        for j in range(2):
                            for c in range(8):
                                ins = e.matmul(PAT[:, 0, j * 8 + c:j * 8 + c + 1],
                                               lhsT=la[64 * j:64 * j + 64, c * 128:(c + 1) * 128],
                                               rhs=ncol[64 * j:64 * j + 64, 0:1], start=True, stop=True)
                        return ins
                    S.pe(mmd, reads=[("la", a), "ncol"], writes=["PAT"])
                    S.act(lambda e: e.activation(out=dec[:], in_=PAT[:, 0, 0:16], func=AF.Exp), reads=["PAT"], writes=[("dec", a)])
                else:
                    def mmd(e):
                        for c in range(8):
                            ins = e.matmul(PAT[:, 0, c * 16:(c + 1) * 16], lhsT=la[0:64, c * 128:(c + 1) * 128],
                                           rhs=bs16[0:64, :], start=True, stop=True)
                        return ins
                    S.pe(mmd, reads=[("la", a), "bs16"], writes=["PAT"])
                    S.act(lambda e: e.activation(out=decs[:].rearrange("p c b -> p (c b)"), in_=PAT[:, 0, :], func=AF.Exp),
                          reads=["PAT"], writes=["decs"])

                if full:
                    def mmb(e, ci=ci):
                        for c in range(8):
                            ins = e.matmul(PA[:].rearrange("p a (c t) -> p (a c) t", t=128)[:, c, :],
                                           lhsT=la[:, c * 128:(c + 1) * 128], rhs=cm[:, ci, :], start=True, stop=True)
                        return ins
                    S.pe(mmb, reads=[("la", a), "cm"], writes=["PA"])
                    pa8 = PA[:].rearrange("p a (c t) -> p (a c) t", t=128)
                    S.act(lambda e, pa8=pa8: e.activation(out=eb[:], in_=pa8, func=AF.Exp), reads=["PA"], writes=["eb"])
                    S.act(lambda e, pa8=pa8: e.activation(out=enb[:], in_=pa8, func=AF.Exp, scale=-1.0), reads=["PA"], writes=["enb"])

                    def trq(e, a=a):
                        for c in range(8):
                            ins = e.transpose(out=PB[:, c, :], in_=qt[a][:, c * 128:(c + 1) * 128], identity=idf[:])
                        return ins
                    S.pe(trq, reads=[("q", a), "idf"], writes=["PB"])
                    S.dve(lambda e: e.scalar_tensor_tensor(out=qg[:], in0=PB[:], scalar=0.0625, in1=eb[:],
                                                           op0=ALU.mult, op1=ALU.mult),
                          reads=["PB", "eb"], writes=["qg"])

                    def trk(e, a=a):
                        for c in range(8):
                            ins = e.transpose(out=PB[:, c, :], in_=kt[a][:, c * 128:(c + 1) * 128], identity=idf[:])
                        return ins
                    S.pe(trk, reads=[("k", a), "idf"], writes=["PB"])
                    S.dve(lambda e: e.tensor_tensor(out=kg[:], in0=PB[:], in1=enb[:], op=ALU.mult),
                          reads=["PB", "enb"], writes=["kg"])
                    if not samp:
                        S.act(lambda e: e.activation(out=qg0[:, :, 0:64], in_=qg[:, :, 0:64], func=AF.Copy),
                              reads=["qg"], writes=["qg0"])
                        S.act(lambda e: e.activation(out=qg1[:, :, 64:128], in_=qg[:, :, 64:128], func=AF.Copy),
                              reads=["qg"], writes=["qg1"])

                    def mma(e):
                        for h in range(4):
                            for kc in range(2):
                                ins = e.matmul(PAT[:, h, :], lhsT=kg[:, h * 2 + kc, :], rhs=qg[:, h * 2 + kc, :],
                                               start=(kc == 0), stop=(kc == 1))
                        return ins
                    S.pe(mma, reads=["kg", "qg"], writes=["PAT"])
                    S.dve(lambda e, samp=samp: e.tensor_tensor(out=AT[:], in0=PAT[:], in1=t4[:, 1 if samp else 0, :, :], op=ALU.mult),
                          reads=["PAT", "t4"], writes=["AT"])
                    S.act(lambda e, a=a: e.activation(out=sg[:], in_=rt[a][:], func=AF.Silu), reads=[("r", a)], writes=["sg"])

                dsb = [(PDS[:], "PDS"), (PO[:, 0, :], ("PO", 0)), (PO[:, 1, :], ("PO", 1))]
                dsn = [0]

                def s_update(j, c, h, rot=False, cast=True):
                    if rot:
                        dap, dkey = dsb[dsn[0] % 3]
                        dsn[0] += 1
                    else:
                        dap, dkey = dsb[0]
                    if j is None:
                        p0, p1, dcol = 0, 128, c
                    else:
                        p0, p1, dcol = 64 * j, 64 * j + 64, j * 8 + c
                    S.pe(lambda e, c=c, h=h, dap=dap, p0=p0, p1=p1: e.matmul(dap, lhsT=kd[p0:p1, c * 128:(c + 1) * 128],
                                                                             rhs=vb[p0:p1, h * 512:(h + 1) * 512], start=True, stop=True),
                         reads=[("kd", a), ("vb", a)], writes=[dkey])
                    S.dve(lambda e, c=c, dap=dap, dcol=dcol: e.scalar_tensor_tensor(out=St[:, c, :], in0=St[:, c, :],
                                                                                    scalar=dec[:, dcol:dcol + 1], in1=dap,
                                                                                    op0=ALU.mult, op1=ALU.add),
                          reads=[dkey, ("dec", a), ("S", c)], writes=[("S", c)])
                    if cast:
                        S.act(lambda e, c=c: e.activation(out=Sb[:, c, :], in_=St[:, c, :], func=AF.Copy),
                              reads=[("S", c)], writes=[("Sb", c)])

                def finish_heads(hp, a=a, f=f):
                    m = f % 2
                    for hh in range(2):
                        h = hp * 2 + hh
                        S.act(lambda e, hh=hh, h=h: e.activation(out=junk[:], in_=PO[:, hh, :], func=AF.Square,
                                                                 accum_out=ss[:, h:h + 1]),
                              reads=[("PO", hh), "ss0"], writes=["junk", ("ss", h)])
                        S.act(lambda e, h=h: e.activation(out=ss[:, 4 + h:5 + h], in_=ss[:, h:h + 1], func=AF.Sqrt,
                                                          bias=epsb[:, 0:1], scale=1.0 / 512),
                              reads=[("ss", h), "eps"], writes=[("rs", h)])
                        S.dve(lambda e, h=h: e.reciprocal(out=ss[:, 4 + h:5 + h], in_=ss[:, 4 + h:5 + h]),
                              reads=[("rs", h)], writes=[("rs", h)])
                        S.dve(lambda e, hh=hh, h=h: e.scalar_tensor_tensor(out=tn[:], in0=PO[:, hh, :], scalar=ss[:, 4 + h:5 + h],
                                                                           in1=gg[:], op0=ALU.mult, op1=ALU.mult),
                              reads=[("PO", hh), ("rs", h), "gg"], writes=["tn"])
                        S.dve(lambda e, h=h, m=m: e.tensor_tensor(out=mixo[m][:, h * 512:(h + 1) * 512], in0=tn[:],
                                                                  in1=sg[:, h * 512:(h + 1) * 512], op=ALU.mult),
                              reads=["tn", "sg"], writes=[("mx", m, h)])

                if not samp:
                    if full:
                        S.dve(lambda e: e.memset(ss[:, 0:4], 0.0), writes=["ss0"] + [("ss", h) for h in range(4)])
                        for hp in range(2):
                            for hh in range(2):
                                h = hp * 2 + hh

                                def mmo(e, hh=hh, h=h):
                                    e.matmul(PO[:, hh, :], lhsT=AT[:, h, :], rhs=vb[:, h * 512:(h + 1) * 512], start=True, stop=False)
                                    for kc in range(2):
                                        ins = e.matmul(PO[:, hh, :], lhsT=qg0[:, h * 2 + kc, :], rhs=Sb[:, h * 2 + kc, :],
                                                       start=False, stop=False)
                                    return ins
                                S.pe(mmo, reads=["AT", ("vb", a), "qg0", ("Sb", h * 2), ("Sb", h * 2 + 1)], writes=[("PO", hh)])
                            for hh in range(2):
                                h = hp * 2 + hh
                                for kc in range(2):
                                    s_update(0, h * 2 + kc, h)
                            for hh in range(2):
                                h = hp * 2 + hh

                                def mmo2(e, hh=hh, h=h):
                                    for kc in range(2):
                                        ins = e.matmul(PO[:, hh, :], lhsT=qg1[:, h * 2 + kc, :], rhs=Sb[:, h * 2 + kc, :],
                                                       start=False, stop=(kc == 1))
                                    return ins
                                S.pe(mmo2, reads=["qg1", ("Sb", h * 2), ("Sb", h * 2 + 1)], writes=[("PO", hh)])
                            for hh in range(2):
                                h = hp * 2 + hh
                                for kc in range(2):
                                    s_update(1, h * 2 + kc, h)
                            finish_heads(hp)
                        S.dma("pool", MIX[f][:, 0:2048], mixo[f % 2][:], reads=[("mx", f % 2, h) for h in range(4)],
                              writes=[("MIXg", f)])
                    else:
                        for h in range(4):
                            for kc in range(2):
                                s_update(None, h * 2 + kc, h, rot=True, cast=(t == NPRE - 1))
                    if t == NT - 2:
                        S.dma("sp", o_gla_p.rearrange("h (kc p) v -> p h kc v", p=128),
                              St[:].rearrange("p (h kc) v -> p h kc v", kc=2),
                              reads=[("S", c) for c in range(8)], writes=["o_gla_p"])
                else:
                    S.dve(lambda e: e.tensor_copy(out=qgm[:], in_=qg[:, :, 0:64].rearrange("p c (b t) -> p b c t", t=4)),
                          reads=["qg"], writes=["qgm"])
                    S.dve(lambda e: e.memset(ss[:, 0:4], 0.0), writes=["ss0"] + [("ss", h) for h in range(4)])
                    it = 0
                    for hp in range(2):
                        for hh in range(2):
                            h = hp * 2 + hh
                            S.pe(lambda e, hh=hh, h=h: e.matmul(PO[:, hh, :], lhsT=AT[:, h, :], rhs=vb[:, h * 512:(h + 1) * 512],
                                                                start=True, stop=False),
                                 reads=["AT", ("vb", a)], writes=[("PO", hh)])
                        for b in range(16):
                            u = it % 2
                            it += 1
                            S.dma("sp", s0[u][:].rearrange("p (h kc) v -> p h kc v", kc=2),
                                  st_gla[b, hp * 2:hp * 2 + 2].rearrange("h (kc p) v -> p h kc v", p=128), writes=[("s0", u)])
                            S.act(lambda e, u=u: e.activation(out=s0b[u][:], in_=s0[u][:], func=AF.Copy),
                                  reads=[("s0", u)], writes=[("s0b", u)])
                            S.dve(lambda e, b=b: e.tensor_copy(out=qgmf[:, :, 4 * b:4 * b + 4], in_=qgm[:, b, :, :]),
                                  reads=["qgm"], writes=["qgmf"])
                            S.act(lambda e, u=u, b=b, hp=hp: e.mul(out=vm[u][:], in_=vb[:, hp * 1024:(hp + 1) * 1024], mul=bmk[:, b:b + 1]),
                                  reads=[("vb", a), "bmk"], writes=[("vm", u)])

                            def mmi(e, u=u, hp=hp, b=b):
                                for hh in range(2):
                                    for kc in range(2):
                                        last = (b == 15 and kc == 1)
                                        ins = e.matmul(PO[:, hh, :], lhsT=qgmf[:, (hp * 2 + hh) * 2 + kc, :], rhs=s0b[u][:, hh * 2 + kc, :],
                                                       start=False, stop=last)
                                return ins
                            S.pe(mmi, reads=["qgmf", ("s0b", u)], writes=[("PO", 0), ("PO", 1)])
                            if b > 0 or True:
                                S.dve(lambda e, b=b: e.memset(qgmf[:, :, 4 * b:4 * b + 4], 0.0), reads=[], writes=["qgmf"])
                            for hh in range(2):
                                for kc in range(2):
                                    c = (hp * 2 + hh) * 2 + kc
                                    S.pe(lambda e, u=u, hh=hh, c=c: e.matmul(PDS[:], lhsT=kd[0:64, c * 128:(c + 1) * 128],
                                                                            rhs=vm[u][0:64, hh * 512:(hh + 1) * 512], start=True, stop=True),
                                         reads=[("kd", a), ("vm", u)], writes=["PDS"])
                                    S.dve(lambda e, u=u, hh=hh, kc=kc, c=c, b=b: e.scalar_tensor_tensor(
                                        out=s0[u][:, hh * 2 + kc, :], in0=s0[u][:, hh * 2 + kc, :], scalar=decs[:, c, b:b + 1],
                                        in1=PDS[:], op0=ALU.mult, op1=ALU.add),
                                        reads=["PDS", "decs", ("s0", u)], writes=[("s0", u)])
                            S.dma("pool", o_gla_s[b, hp * 2:hp * 2 + 2].rearrange("h (kc p) v -> p h kc v", p=128),
                                  s0[u][:].rearrange("p (h kc) v -> p h kc v", kc=2),
                                  reads=[("s0", u)], writes=[("ogs", b, hp)])
                        finish_heads(hp)
                    S.dma("sp", MIX[f][:, 0:2048], mixo[f % 2][:], reads=[("mx", f % 2, h) for h in range(4)],
                          writes=[("MIXg", f)])
            for t in range(NT):
                gla_tile(t)
            S.end_phase()

        SCALE = 128.0 ** -0.5
        with ExitStack() as ph:
            sb = lambda n, s, dt=F32: ph.enter_context(nc.sbuf_tensor("pD" + n, list(s), dt))
            ps = lambda n, s, dt=F32: ph.enter_context(nc.psum_tensor("psD" + n, list(s), dt))
            idf = sb("idf", [128, 128])
            idb = sb("idb", [128, 128], BF16)
            bias = sb("bias", [128, 2, 16, 256])
            snk = sb("snk", [128, 16])
            snks = sb("snks", [16, 4])
            bsb = sb("bsb", [16, 4, 128])
            bsn = sb("bsn", [16, 4, 16, 64])
            qin = [sb("qin%d" % i, [128, 2048]) for i in range(2)]
            kin = [sb("kin%d" % i, [128, 512]) for i in range(2)]
            vin = [sb("vin%d" % i, [128, 512]) for i in range(2)]
            qT = sb("qT", [128, 16, 128], BF16)
            kT = [sb("kT%d" % i, [128, 4, 128], BF16) for i in range(2)]
            vv = [sb("vv%d" % i, [128, 512], BF16) for i in range(2)]
            ssb = [sb("ssb%d" % i, [128, 256]) for i in range(2)]
            pex = [sb("pex%d" % i, [128, 256], BF16) for i in range(2)]
            qb = [sb("qb%d" % i, [128, 2048], BF16) for i in range(2)]
            kbf = [sb("kbf%d" % i, [128, 512], BF16) for i in range(2)]
            st = sb("st", [128, 16, 8])
            pT = [sb("pT%d" % i, [128, 2, 128], BF16) for i in range(2)]
            osw = [sb("osw%d" % i, [128, 2048], BF16) for i in range(2)]
            kbT = [sb("kbT%d" % i, [128, 4, 128], BF16) for i in range(2)]
            vbf = [sb("vbf%d" % i, [128, 512], BF16) for i in range(2)]
            OS = sb("OS", [16, 16, 4, 128], BF16)
            qS = sb("qS", [128, 16, 4, 16], BF16)
            ssg = [sb("ssg%d" % i, [128, 1028]) for i in range(4)]
            pxg = [sb("pxg%d" % i, [128, 1024], BF16) for i in range(2)]
            pTg = [sb("pTg%d" % i, [128, 8, 128], BF16) for i in range(2)]
            stg = sb("stg", [128, 2, 16])
            gcnt = [0]
            scb2 = sb("scb2", [128, KC, 256], BF16)
            wa = [sb("wa%d" % i, [128, KC, 256], BF16) for i in range(2)]
            ba = [sb("ba%d" % i, [128, 256]) for i in range(2)]
            ma = [sb("ma%d" % i, [128, 2, 256]) for i in range(2)]
            PQ = ps("Q", [128, 16, 128], BF16)
            PK = ps("K", [128, 4, 128], BF16)
            PS_ = ps("S", [128, 4, 256])
            PT = ps("T", [128, 4, 2, 128], BF16)
            POo = ps("O", [128, 4, 128])
            PM = ps("M", [128, 2, 256])
            S.dma("sp", scb2[:].rearrange("p k m -> p (k m)"), SCB, writes=["scb2"])
            NB0 = 32
            NB1 = 96

            def ldwa(n):
                S.dma("pool", wa[n % 2][:], w_ada[:, n * 256:(n + 1) * 256].rearrange("(k p) n -> p k n", p=128), writes=[("wa", n % 2)])
            ldwa(NB0)
            p0state = [NB0, 0]

            def p0b_step():
                n = p0state[0]
                if n >= NB1:
                    return
                p0state[0] += 1
                i = n % 2
                cs = slice(n * 256, (n + 1) * 256)
                if n + 1 < NB1:
                    ldwa(n + 1)
                S.dma("sp", ba[i][:], bada_bc[:, cs], writes=[("ba", i)])

                def mm(e, i=i):
                    for g in range(2):
                        for k in range(KC):
                            ins = e.matmul(PM[:, g, :], lhsT=scb2[:, k, g * 128:(g + 1) * 128], rhs=wa[i][:, k, :],
                                           start=(k == 0), stop=(k == KC - 1))
                    return ins
                S.pe(mm, reads=["scb2", ("wa", i)], writes=["PM"])
                for g in range(2):
                    S.dve(lambda e, i=i, g=g: e.tensor_tensor(out=ma[i][:, g, :], in0=PM[:, g, :], in1=ba[i][:], op=ALU.add),
                          reads=["PM", ("ba", i)], writes=[("ma", i, g)])
                S.dma("pool", MOD[:, :, cs].rearrange("g p n -> p g n"), ma[i][:],
                      reads=[("ma", i, 0), ("ma", i, 1)], writes=[("MOD", n)])

            def p0b_tick(total_units=208):
                p0state[1] += 1
                want = NB0 + (p0state[1] * (NB1 - NB0) + total_units - 1) // total_units
                while p0state[0] < min(want, NB1):
                    p0b_step()
            S.dma("sp", idf[:], ident_in, writes=["idf"])
            S.dve(lambda e: e.tensor_copy(out=idb[:], in_=idf[:]), reads=["idf"], writes=["idb"])
            S.dma("sp", bias[:], bias_p, writes=["bias"])
            S.dma("sp", snk[:], sink_bc, writes=["snk"])
            S.dma("sp", snks[:], sink_s, writes=["snks"])
            S.dma("sp", bsb[:], bias_sb, writes=["bsb"])
            S.dma("sp", bsn[:], bias_sn, writes=["bsn"])
            S.dve(lambda e: e.memset(st[:], 0.0), writes=["st"])
            for kvh in range(4):
                S.dve(lambda e, kvh=kvh: e.tensor_copy(out=ssg[kvh][:, 1024:1028], in_=snk[:, kvh * 4:(kvh + 1) * 4]),
                      reads=["snk"], writes=[("ssgs", kvh)])

            def load_kv(slot, ksrc, vsrc):
                S.dma("sp", kin[slot][:], ksrc, writes=[("kin", slot)])
                S.dma("sp", vin[slot][:], vsrc, writes=[("vin", slot)])

                S.act(lambda e, slot=slot: e.activation(out=kbf[slot][:], in_=kin[slot][:], func=AF.Copy),
                      reads=[("kin", slot)], writes=[("kbf", slot)])

                def trk(e, slot=slot):
                    for c in range(4):
                        ins = e.transpose(out=PK[:, c, :], in_=kbf[slot][:, c * 128:(c + 1) * 128], identity=idb[:])
                    return ins
                S.pe(trk, reads=[("kbf", slot), "idb"], writes=["PK"])
                S.act(lambda e, slot=slot: e.activation(out=kT[slot][:], in_=PK[:], func=AF.Copy), reads=["PK"], writes=[("kT", slot)])
                S.act(lambda e, slot=slot: e.activation(out=vv[slot][:], in_=vin[slot][:], func=AF.Copy),
                      reads=[("vin", slot)], writes=[("vv", slot)])

            def load_q(slot, qsrc):
                S.dma("sp", qin[slot][:], qsrc, writes=[("qin", slot)])

                S.act(lambda e, slot=slot: e.activation(out=qb[slot][:], in_=qin[slot][:], func=AF.Copy),
                      reads=[("qin", slot)], writes=[("qb", slot)])

                def trq(e, slot=slot):
                    for c in range(16):
                        ins = e.transpose(out=PQ[:, c, :], in_=qb[slot][:, c * 128:(c + 1) * 128], identity=idb[:])
                    return ins
                S.pe(trq, reads=[("qb", slot), "idb"], writes=["PQ"])
                S.dve(lambda e: e.tensor_copy(out=qT[:], in_=PQ[:]), reads=["PQ"], writes=["qT"])

            def softmax_rows(np_, z, sinkcol, width, hkey):
                S.dve(lambda e: e.tensor_reduce(out=st[0:np_, hkey, 0:1], in_=ssb[z][0:np_, 0:width], axis=AX.X, op=ALU.max),
                      reads=[("ssb", z)], writes=[("st", hkey)])
                S.dve(lambda e: e.tensor_scalar(out=st[0:np_, hkey, 1:2], in0=st[0:np_, hkey, 0:1], scalar1=sinkcol, scalar2=-1.0,
                                                op0=ALU.max, op1=ALU.mult),
                      reads=[("st", hkey), "snk"], writes=[("st", hkey)])
                S.act(lambda e: e.activation(out=pex[z][0:np_, 0:width], in_=ssb[z][0:np_, 0:width], func=AF.Exp,
                                             bias=st[0:np_, hkey, 1:2], scale=1.0, accum_out=st[0:np_, hkey, 2:3]),
                      reads=[("ssb", z), ("st", hkey)], writes=[("pex", z), ("st", hkey)])
                S.act(lambda e: e.activation(out=st[0:np_, hkey, 3:4], in_=sinkcol, func=AF.Exp, bias=st[0:np_, hkey, 1:2], scale=1.0),
                      reads=[("st", hkey), "snk"], writes=[("st", hkey)])
                S.dve(lambda e: e.tensor_tensor(out=st[0:np_, hkey, 4:5], in0=st[0:np_, hkey, 2:3], in1=st[0:np_, hkey, 3:4], op=ALU.add),
                      reads=[("st", hkey)], writes=[("st", hkey)])
                S.dve(lambda e: e.reciprocal(out=st[0:np_, hkey, 5:6], in_=st[0:np_, hkey, 4:5]),
                      reads=[("st", hkey)], writes=[("st", hkey)])

            load_kv(1, PH2[:, 0:512], PH2[:, 512:1024])
            hc = 0
            for f in range(NFULL):
                cur = f % 2
                prv = 1 - cur
                src = PF[f]
                load_kv(cur, src[:, OFF_SK:OFF_SK + 512], src[:, OFF_SV:OFF_SV + 512])
                load_q(cur, src[:, OFF_SQ:OFF_SQ + 2048])
                if f < NFULL - 1:
                    bsel = 0 if f == 1 else 1
                    for kvh in range(4):
                        rot = gcnt[0] % 2
                        gcnt[0] += 1

                        def mms(e, kvh=kvh, prv=prv, cur=cur):
                            for hh in range(4):
                                h = kvh * 4 + hh
                                e.matmul(PS_[:, hh, 0:128], lhsT=qT[:, h, :], rhs=kT[prv][:, kvh, :], start=True, stop=True)
                                ins = e.matmul(PS_[:, hh, 128:256], lhsT=qT[:, h, :], rhs=kT[cur][:, kvh, :], start=True, stop=True)
                            return ins
                        S.pe(mms, reads=["qT", ("kT", prv), ("kT", cur)], writes=["PSg"])
                        sview = ssg[kvh][:, 0:1024].rearrange("p (h k) -> p h k", k=256)
                        S.dve(lambda e, kvh=kvh, bsel=bsel, sview=sview: e.scalar_tensor_tensor(
                            out=sview, in0=PS_[:], scalar=SCALE, in1=bias[:, bsel, kvh * 4:(kvh + 1) * 4, :], op0=ALU.mult, op1=ALU.add),
                            reads=["PSg", "bias"], writes=[("ssg", kvh)])
                        S.dve(lambda e, kvh=kvh, rot=rot: e.tensor_reduce(out=stg[:, rot, 0:1], in_=ssg[kvh][:, 0:1028], axis=AX.X, op=ALU.max),
                              reads=[("ssg", kvh), ("ssgs", kvh)], writes=[("stg", rot)])
                        S.dve(lambda e, rot=rot: e.tensor_scalar(out=stg[:, rot, 1:2], in0=stg[:, rot, 0:1], scalar1=-1.0, scalar2=None, op0=ALU.mult),
                              reads=[("stg", rot)], writes=[("stg", rot)])
                        S.act(lambda e, kvh=kvh, rot=rot: e.activation(out=pxg[rot][:], in_=ssg[kvh][:, 0:1024], func=AF.Exp,
                                                                      bias=stg[:, rot, 1:2], scale=1.0),
                              reads=[("ssg", kvh), ("stg", rot)], writes=[("pxg", rot)])
                        S.act(lambda e, kvh=kvh, rot=rot: e.activation(out=stg[:, rot, 6:10], in_=ssg[kvh][:, 1024:1028], func=AF.Exp,
                                                                      bias=stg[:, rot, 1:2], scale=1.0),
                              reads=[("ssgs", kvh), ("stg", rot)], writes=[("stg2", rot)])
                        S.dve(lambda e, rot=rot: e.tensor_reduce(out=stg[:, rot, 2:6], in_=pxg[rot][:].rearrange("p (h k) -> p h k", k=256),
                                                                 axis=AX.X, op=ALU.add),
                              reads=[("pxg", rot)], writes=[("stg3", rot)])
                        S.dve(lambda e, rot=rot: e.tensor_tensor(out=stg[:, rot, 10:14], in0=stg[:, rot, 2:6], in1=stg[:, rot, 6:10], op=ALU.add),
                              reads=[("stg2", rot), ("stg3", rot)], writes=[("stg4", rot)])
                        S.dve(lambda e, rot=rot: e.reciprocal(out=stg[:, rot, 10:14], in_=stg[:, rot, 10:14]),
                              reads=[("stg4", rot)], writes=[("stg4", rot)])
                        PT8 = PT[:].rearrange("p z j t -> p (z j) t")

                        def trp(e, rot=rot, PT8=PT8):
                            for c in range(8):
                                ins = e.transpose(out=PT8[:, c, :], in_=pxg[rot][:, c * 128:(c + 1) * 128], identity=idb[:])
                            return ins
                        S.pe(trp, reads=[("pxg", rot), "idb"], writes=["PT8"])
                        S.any(lambda e, ia, rot=rot, PT8=PT8: acopy(e, ia, pTg[rot][:], PT8), reads=["PT8"], writes=[("pTg", rot)])

                        def mmv(e, rot=rot, kvh=kvh, prv=prv, cur=cur):
                            for hh in range(4):
                                e.matmul(POo[:, hh, :], lhsT=pTg[rot][:, hh * 2, :], rhs=vv[prv][:, kvh * 128:(kvh + 1) * 128], start=True, stop=False)
                                ins = e.matmul(POo[:, hh, :], lhsT=pTg[rot][:, hh * 2 + 1, :], rhs=vv[cur][:, kvh * 128:(kvh + 1) * 128],
                                               start=False, stop=True)
                            return ins
                        S.pe(mmv, reads=[("pTg", rot), ("vv", prv), ("vv", cur)], writes=["POg"])
                        for hh in range(4):
                            h = kvh * 4 + hh
                            S.act(lambda e, hh=hh, h=h, cur=cur, rot=rot: e.mul(out=osw[cur][:, h * 128:(h + 1) * 128], in_=POo[:, hh, :],
                                                                               mul=stg[:, rot, 10 + hh:11 + hh]),
                                  reads=["POg", ("stg4", rot)], writes=[("osw", cur, h)])
                        p0b_tick(100)
                    S.dma("pool", MIX[f][:, 2048:4096], osw[cur][:], reads=[("osw", cur, h) for h in range(16)], writes=[("MIXs", f)])
                    if f == NFULL - 2:
                        S.dma("sp", o_k_p, src[:, OFF_SK:OFF_SK + 512], writes=["okp"])
                        S.dma("sp", o_v_p, src[:, OFF_SV:OFF_SV + 512], writes=["ovp"])
                else:
                    S.barrier()
                    S.dma("sp", o_k_s[:, 0:124, :], st_k[:, 4:128, :], writes=["oks0"])
                    S.dma("sp", o_v_s[:, 0:124, :], st_v[:, 4:128, :], writes=["ovs0"])
                    S.dma("sp", o_k_s[:, 124:128, :], src[0:64, OFF_SK:OFF_SK + 512].rearrange("(b t) c -> b t c", t=4), writes=["oks1"])
                    S.dma("sp", o_v_s[:, 124:128, :], src[0:64, OFF_SV:OFF_SV + 512].rearrange("(b t) c -> b t c", t=4), writes=["ovs1"])
                    for kvh in range(4):
                        S.dve(lambda e, kvh=kvh: e.tensor_copy(out=qS[:, :, kvh, :].rearrange("p b (g t) -> p b g t", t=4),
                                                               in_=qT[:, kvh * 4:(kvh + 1) * 4, 0:64].rearrange("p g (b t) -> p b g t", t=4)),
                              reads=["qT"], writes=["qS"])
                    for b in range(16):
                        u = b % 2
                        S.dma("pool", kbT[u][:], st_kT[b].rearrange("h d k -> d h k"), writes=[("kbT", u)])
                        S.dma("pool", vbf[u][:], st_v[b], writes=[("vbf", u)])
                        for kvh in range(4):
                            z = hc % 2
                            o4 = hc % 4
                            hc += 1
                            hk = kvh * 4
                            qsl = qS[:, b, kvh, :]

                            def mms(e, z=z, kvh=kvh, u=u, qsl=qsl, cur=cur):
                                e.matmul(PS_[0:16, z, 0:128], lhsT=qsl, rhs=kbT[u][:, kvh, :], start=True, stop=True)
                                return e.matmul(PS_[0:16, z, 128:192], lhsT=qsl, rhs=kT[cur][:, kvh, 0:64], start=True, stop=True)
                            S.pe(mms, reads=["qS", ("kbT", u), ("kT", cur)], writes=[("PS", z)])
                            S.dve(lambda e, z=z, kvh=kvh: e.scalar_tensor_tensor(out=ssb[z][0:16, 0:128], in0=PS_[0:16, z, 0:128], scalar=SCALE,
                                                                                 in1=bsb[:, kvh, :], op0=ALU.mult, op1=ALU.add),
                                  reads=[("PS", z), "bsb"], writes=[("ssb", z)])
                            S.dve(lambda e, z=z, kvh=kvh, b=b: e.scalar_tensor_tensor(out=ssb[z][0:16, 128:192], in0=PS_[0:16, z, 128:192],
                                                                                      scalar=SCALE, in1=bsn[:, kvh, b, :],
                                                                                      op0=ALU.mult, op1=ALU.add),
                                  reads=[("PS", z), "bsn", ("ssb", z)], writes=[("ssb", z)])
                            S.dve(lambda e, hk=hk: e.memset(st[0:16, hk, 2:3], 0.0), writes=[("st", hk)])
                            softmax_rows(16, z, snks[:, kvh:kvh + 1], 192, hk)

                            def trp(e, z=z):
                                e.transpose(out=PT[:, z, 0, 0:16], in_=pex[z][0:16, 0:128], identity=idb[0:16, 0:16])
                                return e.transpose(out=PT[0:64, z, 1, 0:16], in_=pex[z][0:16, 128:192], identity=idb[0:16, 0:16])
                            S.pe(trp, reads=[("pex", z), "idb"], writes=[("PT", z)])
                            S.any(lambda e, ia, z=z: acopy(e, ia, pT[z][:, 0, 0:16], PT[:, z, 0, 0:16]), reads=[("PT", z)], writes=[("pT", z, 0)])
                            S.any(lambda e, ia, z=z: acopy(e, ia, pT[z][0:64, 1, 0:16], PT[0:64, z, 1, 0:16]), reads=[("PT", z)], writes=[("pT", z, 1)])

                            def mmv(e, z=z, kvh=kvh, o4=o4, u=u, cur=cur):
                                e.matmul(POo[0:16, o4, :], lhsT=pT[z][:, 0, 0:16], rhs=vbf[u][:, kvh * 128:(kvh + 1) * 128], start=True, stop=False)
                                return e.matmul(POo[0:16, o4, :], lhsT=pT[z][0:64, 1, 0:16], rhs=vv[cur][0:64, kvh * 128:(kvh + 1) * 128],
                                                start=False, stop=True)
                            S.pe(mmv, reads=[("pT", z, 0), ("pT", z, 1), ("vbf", u), ("vv", cur)], writes=[("PO", o4)])
                            S.act(lambda e, o4=o4, hk=hk, b=b, kvh=kvh: e.mul(out=OS[:, b, kvh, :], in_=POo[0:16, o4, :], mul=st[0:16, hk, 5:6]),
                                  reads=[("PO", o4), ("st", hk)], writes=["OS"])
                            p0b_tick(100)
                    for g in range(4):
                        for kvh in range(4):
                            S.dma("sp", MIX[f][0:64, 2048:4096].rearrange("(b t) (kvh g d) -> g kvh t b d", t=4, g=4, d=128)[g, kvh],
                                  OS[4 * g:4 * g + 4, :, kvh, :], reads=["OS"], writes=[("MIXs", f, g, kvh)])
            while p0state[0] < NB1:
                p0b_step()
            S.end_phase()

        with ExitStack() as ph:
            sb = lambda n, s, dt=F32: ph.enter_context(nc.sbuf_tensor("pE" + n, list(s), dt))
            ps = lambda n, s, dt=F32: ph.enter_context(nc.psum_tensor("psE" + n, list(s), dt))
            idf = sb("idf", [128, 128])
            idb = sb("idb", [128, 128], BF16)
            mT = sb("mT", [128, NFULL, KC, 128], BF16)
            mx = [sb("mx%d" % i, [128, D], BF16) for i in range(2)]
            GT = [sb("GT%d" % g, [128, D]) for g in range(2)]
            wb = [sb("w%d" % i, [128, KC, 256], BF16) for i in range(2)]
            xb = [sb("xb%d" % i, [128, 256]) for i in range(3)]
            tb = [sb("tb%d" % i, [128, 256]) for i in range(3)]
            ptr = [ps("ptr%d" % i, [128, 16, 128], BF16) for i in range(2)]
            pp = [ps("p%d" % i, [128, 512]) for i in range(3)]
            S.dma("sp", idf[:], ident_in, writes=["idf"])
            S.dve(lambda e: e.tensor_copy(out=idb[:], in_=idf[:]), reads=["idf"], writes=["idb"])
            for g in range(2):
                S.dma("sp", GT[g][:], MOD[g, :, 2 * D:3 * D], writes=[("GT", g)])
            npt = 0
            for f in range(NFULL):
                a = f % 2
                S.dma("sp", mx[a][:], MIX[f], writes=[("mx", a)])
                for half in range(2):
                    pj = npt % 2
                    npt += 1

                    def tr(e, a=a, half=half, pj=pj):
                        for k in range(16):
                            kk = half * 16 + k
                            ins = e.transpose(out=ptr[pj][:, k, :], in_=mx[a][:, kk * 128:(kk + 1) * 128], identity=idb[:])
                        return ins
                    S.pe(tr, reads=[("mx", a), "idb"], writes=[("ptr", pj)])
                    S.any(lambda e, ia, f=f, half=half, pj=pj: acopy(e, ia, mT[:, f, half * 16:(half + 1) * 16, :], ptr[pj][:]),
                          reads=[("ptr", pj)], writes=[("mT", f, half)])
            cnt = 0
            def ldwE(n):
                S.dma("pool", wb[n % 2][:], w_o[:, n * 256:(n + 1) * 256].rearrange("(k p) n -> p k n", p=128), writes=[("w", n % 2)])
            ldwE(0)
            for n in range(16):
                wi = n % 2
                cs = slice(n * 256, (n + 1) * 256)
                if n + 1 < 16:
                    ldwE(n + 1)
                for f in range(NFULL):
                    a = cnt % 3
                    cnt += 1
                    g = 1 if f == NFULL - 1 else 0
                    S.dma("sp", xb[a][:], xall[NPRE + f][:, cs], writes=[("xb", a)])

                    def mm(e, a=a, wi=wi, f=f):
                        for k in range(KC):
                            ins = e.matmul(pp[a][:, 0:256], lhsT=mT[:, f, k, :], rhs=wb[wi][:, k, :], start=(k == 0), stop=(k == KC - 1))
                        return ins
                    S.pe(mm, reads=[("mT", f, 0), ("mT", f, 1), ("w", wi)], writes=[("p", a)])
                    S.dve(lambda e, a=a, g=g, cs=cs: e.tensor_tensor(out=tb[a][:], in0=pp[a][:, 0:256], in1=GT[g][:, cs], op=ALU.mult),
                          reads=[("p", a), ("GT", g)], writes=[("tb", a)])
                    S.dve(lambda e, a=a: e.tensor_tensor(out=tb[a][:], in0=tb[a][:], in1=xb[a][:], op=ALU.add),
                          reads=[("tb", a), ("xb", a)], writes=[("tb", a)])
                    S.dma("pool", X1[f][:, cs], tb[a][:], reads=[("tb", a)], writes=[("X1", f, n)])
            S.end_phase()

        norm_phase("nE", [X1[f] for f in range(NFULL)], [1 if f == NFULL - 1 else 0 for f in range(NFULL)],
                   3 * D, 4 * D, 1, [H2T[f] for f in range(NFULL)])

        NTF = 1090
        TG = [(0, 512), (512, 512), (1024, 66)]
        with ExitStack() as ph:
            sb = lambda n, s, dt=F32: ph.enter_context(nc.sbuf_tensor("pF" + n, list(s), dt))
            ps = lambda n, s, dt=F32: ph.enter_context(nc.psum_tensor("psF" + n, list(s), dt))
            h2 = sb("h2", [128, KC, NTF], BF16)
            hm = sb("hm", [128, 1])
            wc = sb("wc", [128, NBLK, 4])
            cst = sb("cst", [128, NBLK, 32])
            wg = [sb("wg%d" % i, [128, KC, 128], BF16) for i in range(2)]
            wv = [sb("wv%d" % i, [128, KC, 128], BF16) for i in range(2)]
            U = [sb("U%d" % i, [128, 3, 512]) for i in range(2)]
            US = [sb("US%d" % i, [128, 16, 6]) for i in range(2)]
            acc = [sb("acc%d" % i, [128, 1088]) for i in range(2)]
            sgt = sb("sgt", [128, 1088])
            ao = [sb("ao%d" % i, [128, 1088], BF16) for i in range(2)]
            cn = [sb("cn%d" % i, [128, 34]) for i in range(2)]
            PU = [ps("U%d" % i, [128, 3, 512]) for i in range(2)]
            S.dma("sp", hm[:], hmask, writes=["hm"])
            S.dma("sp", wc[:], w_convT.rearrange("(b p) j -> p b j", p=128), writes=["wc"])
            S.dma("sp", cst[:], st_convT.rearrange("(b p) j -> p b j", p=128), writes=["cst"])
            for f in range(1, NFULL):
                ntok = 64 if f == NFULL - 1 else 128
                S.dma("sp", h2[:, :, 2 + (f - 1) * 128:2 + (f - 1) * 128 + ntok],
                      H2T[f].rearrange("p (k t) -> p k t", t=128)[:, :, 0:ntok], writes=[("h2", f)])
            S.dma("sp", h2[:, :, 0:2], H2T[0].rearrange("p (k t) -> p k t", t=128)[:, :, 126:128], writes=[("h2", 0)])
            S.dve(lambda e: e.tensor_scalar(out=h2[:, :, 0:2], in0=h2[:, :, 0:2], scalar1=hm[:, 0:1], scalar2=None, op0=ALU.mult),
                  reads=[("h2", 0), "hm"], writes=[("h2", 0)])
            h2keys = [("h2", f) for f in range(NFULL)]
            def ldwF(gi):
                S.dma("pool", wg[gi % 2][:], w_up[:, gi * 128:(gi + 1) * 128].rearrange("(k p) n -> p k n", p=128), writes=[("wg", gi % 2)])
                S.dma("pool", wv[gi % 2][:], w_up[:, DFF + gi * 128:DFF + (gi + 1) * 128].rearrange("(k p) n -> p k n", p=128),
                      writes=[("wv", gi % 2)])
            ldwF(0)
            for gi in range(86):
                wi = gi % 2
                if gi + 1 < 86:
                    ldwF(gi + 1)
                for sub in range(1):
                    blk = gi
                    ai = blk % 2
                    for which in range(2):
                        wt = wg if which == 0 else wv
                        bidx = blk if which == 0 else 86 + blk

                        def mm(e, wt=wt, wi=wi, sub=sub, which=which):
                            for tg, (t0, n) in enumerate(TG):
                                for k in range(KC):
                                    ins = e.matmul(PU[which][:, tg, 0:n], lhsT=wt[wi][:, k, sub * 128:(sub + 1) * 128],
                                                   rhs=h2[:, k, t0:t0 + n], start=(k == 0), stop=(k == KC - 1))
                            return ins
                        S.pe(mm, reads=h2keys + [("wg" if which == 0 else "wv", wi)], writes=[("PU", which)])
                        S.act(lambda e, which=which: e.activation(out=U[which][:], in_=PU[which][:], func=AF.Copy),
                              reads=[("PU", which)], writes=[("U", which)])
                        Uf = U[which][:].rearrange("p a b -> p (a b)")
                        S.dve(lambda e, which=which, Uf=Uf, bidx=bidx: e.tensor_scalar(out=acc[which][:, 0:1024], in0=Uf[:, 2:1026],
                                                                                      scalar1=wc[:, bidx, 2:3], scalar2=wc[:, bidx, 3:4],
                                                                                      op0=ALU.mult, op1=ALU.add),
                              reads=[("U", which), "wc"], writes=[("acc", which)])
                        S.dve(lambda e, which=which, Uf=Uf, bidx=bidx: e.scalar_tensor_tensor(out=acc[which][:, 0:1024], in0=Uf[:, 1:1025],
                                                                                             scalar=wc[:, bidx, 1:2], in1=acc[which][:, 0:1024],
                                                                                             op0=ALU.mult, op1=ALU.add),
                              reads=[("U", which), "wc", ("acc", which)], writes=[("acc", which)])
                        S.dve(lambda e, which=which, Uf=Uf, bidx=bidx: e.scalar_tensor_tensor(out=acc[which][:, 0:1024], in0=Uf[:, 0:1024],
                                                                                             scalar=wc[:, bidx, 0:1], in1=acc[which][:, 0:1024],
                                                                                             op0=ALU.mult, op1=ALU.add),
                              reads=[("U", which), "wc", ("acc", which)], writes=[("acc", which)])
                        S.act(lambda e, which=which, bidx=bidx: e.activation(out=US[which][:, :, 0:2],
                                                                            in_=cst[:, bidx, :].rearrange("p (b j) -> p b j", j=2), func=AF.Copy),
                              reads=["cst"], writes=[("US", which, 0)])
                        S.act(lambda e, which=which, Uf=Uf: e.activation(out=US[which][:, :, 2:6],
                                                                        in_=Uf[:, 1026:1090].rearrange("p (b t) -> p b t", t=4), func=AF.Copy),
                              reads=[("U", which)], writes=[("US", which, 1)])
                        accs = acc[which][:, 1024:1088].rearrange("p (b t) -> p b t", t=4)
                        S.dve(lambda e, which=which, bidx=bidx, accs=accs: e.tensor_scalar(out=accs, in0=US[which][:, :, 2:6],
                                                                                          scalar1=wc[:, bidx, 2:3], scalar2=wc[:, bidx, 3:4],
                                                                                          op0=ALU.mult, op1=ALU.add),
                              reads=[("US", which, 0), ("US", which, 1), "wc", ("acc", which)], writes=[("acc", which)])
                        S.dve(lambda e, which=which, bidx=bidx, accs=accs: e.scalar_tensor_tensor(out=accs, in0=US[which][:, :, 1:5],
                                                                                                 scalar=wc[:, bidx, 1:2], in1=accs,
                                                                                                 op0=ALU.mult, op1=ALU.add),
                              reads=[("US", which, 0), ("US", which, 1), "wc", ("acc", which)], writes=[("acc", which)])
                        S.dve(lambda e, which=which, bidx=bidx, accs=accs: e.scalar_tensor_tensor(out=accs, in0=US[which][:, :, 0:4],
                                                                                                 scalar=wc[:, bidx, 0:1], in1=accs,
                                                                                                 op0=ALU.mult, op1=ALU.add),
                              reads=[("US", which, 0), ("US", which, 1), "wc", ("acc", which)], writes=[("acc", which)])
                        ci = (blk * 2 + which) % 2
                        S.act(lambda e, ci=ci, Uf=Uf: e.activation(out=cn[ci][:, 0:2], in_=Uf[:, 1024:1026], func=AF.Copy),
                              reads=[("U", which)], writes=[("cn", ci, 0)])
                        S.act(lambda e, ci=ci, which=which: e.activation(out=cn[ci][:, 2:34].rearrange("p (b j) -> p b j", j=2),
                                                                        in_=US[which][:, :, 4:6], func=AF.Copy),
                              reads=[("US", which, 1)], writes=[("cn", ci, 1)])
                        S.dma("sp", o_convT[bidx * 128:(bidx + 1) * 128, :], cn[ci][:], reads=[("cn", ci, 0), ("cn", ci, 1)],
                              writes=[("oc", bidx)])
                    S.act(lambda e: e.activation(out=sgt[:], in_=acc[0][:], func=AF.Silu), reads=[("acc", 0)], writes=["sgt"])
                    S.dve(lambda e, ai=ai: e.tensor_tensor(out=ao[ai][:], in0=sgt[:], in1=acc[1][:], op=ALU.mult),
                          reads=["sgt", ("acc", 1)], writes=[("ao", ai)])
                    S.dma("sp", ACTT[0:8, :, blk, :].rearrange("t p c -> p t c"), ao[ai][:, 0:1024].rearrange("p (t c) -> p t c", c=128),
                          reads=[("ao", ai)], writes=[("ACTT", blk, 0)])
                    S.dma("sp", ACTT[8, :, blk, 0:64], ao[ai][:, 1024:1088], reads=[("ao", ai)], writes=[("ACTT", blk, 1)])
            S.end_phase()

        with ExitStack() as ph:
            sb = lambda n, s, dt=F32: ph.enter_context(nc.sbuf_tensor("pG" + n, list(s), dt))
            ps = lambda n, s, dt=F32: ph.enter_context(nc.psum_tensor("psG" + n, list(s), dt))
            wd = [sb("wd%d" % i, [128, 86, 256], BF16) for i in range(2)]
            at = [sb("at%d" % i, [128, 86, 128], BF16) for i in range(2)]
            GT = [sb("GT%d" % g, [128, D]) for g in range(2)]
            xb = [sb("xb%d" % i, [128, 256]) for i in range(3)]
            tb = [sb("tb%d" % i, [128, 256]) for i in range(3)]
            pp = [ps("p%d" % i, [128, 512]) for i in range(3)]
            for g in range(2):
                S.dma("sp", GT[g][:], MOD[g, :, 5 * D:6 * D], writes=[("GT", g)])
            cnt = 0
            def ldwG(n):
                S.dma("pool", wd[n % 2][:], w_down[:, n * 256:(n + 1) * 256].rearrange("(k p) n -> p k n", p=128), writes=[("wd", n % 2)])
            def g_loads(idx):
                n_, tt_ = idx // 9, idx % 9
                S.dma("sp", at[idx % 2][:, 0:43, :], ACTT[tt_][:, 0:43, :], writes=[("at", idx % 2, 0)])
                S.dma("aq", at[idx % 2][:, 43:86, :], ACTT[tt_][:, 43:86, :], writes=[("at", idx % 2, 1)])
                S.dma("sp", xb[idx % 3][:], X1[tt_ + 1][:, n_ * 256:(n_ + 1) * 256], writes=[("xb", idx % 3)])
            ldwG(0)
            g_loads(0)
            for n in range(16):
                wi = n % 2
                cs = slice(n * 256, (n + 1) * 256)
                if n + 1 < 16:
                    ldwG(n + 1)
                for tt in range(9):
                    a = cnt % 3
                    a2 = cnt % 2
                    cnt += 1
                    g = 1 if tt == 8 else 0
                    m = 64 if tt == 8 else 128
                    if cnt < 16 * 9:
                        g_loads(cnt)

                    def mm(e, a=a, a2=a2, wi=wi, m=m):
                        for k in range(86):
                            ins = e.matmul(pp[a][0:m, 0:256], lhsT=at[a2][:, k, 0:m], rhs=wd[wi][:, k, :], start=(k == 0), stop=(k == 85))
                        return ins
                    S.pe(mm, reads=[("at", a2, 0), ("at", a2, 1), ("wd", wi)], writes=[("p", a)])
                    S.dve(lambda e, a=a, g=g, cs=cs, m=m: e.tensor_tensor(out=tb[a][0:m, :], in0=pp[a][0:m, 0:256], in1=GT[g][0:m, cs], op=ALU.mult),
                          reads=[("p", a), ("GT", g)], writes=[("tb", a)])
                    S.dve(lambda e, a=a, m=m: e.tensor_tensor(out=tb[a][0:m, :], in0=tb[a][0:m, :], in1=xb[a][0:m, :], op=ALU.add),
                          reads=[("tb", a), ("xb", a)], writes=[("tb", a)])
                    S.dma("pool", X2[tt][0:m, cs], tb[a][0:m, :], reads=[("tb", a)], writes=[("X2", tt, n)])
            S.end_phase()

        with ExitStack() as ph:
            sb = lambda n, s, dt=F32: ph.enter_context(nc.sbuf_tensor("pH" + n, list(s), dt))
            gf = sb("gf", [128, D])
            epsb = sb("eps", [128, 1])
            ssq = sb("ssq", [128, 18])
            xt = [sb("xt%d" % i, [128, D]) for i in range(3)]
            yo = [sb("yo%d" % i, [128, D]) for i in range(2)]
            junk = sb("junk", [128, D], BF16)
            S.dma("sp", gf[:], gf_bc, writes=["gf"])
            S.dve(lambda e: e.memset(epsb[:], EPS), writes=["eps"])
            S.dve(lambda e: e.memset(ssq[:], 0.0), writes=["ssq"])
            for tt in range(9):
                a = tt % 3
                b2 = tt % 2
                m = 64 if tt == 8 else 128
                S.dma("sp", xt[a][0:m, :], X2[tt][0:m, :], writes=[("xt", a)])
                S.act(lambda e, a=a, tt=tt, m=m: e.activation(out=junk[0:m, :], in_=xt[a][0:m, :], func=AF.Square,
                                                             accum_out=ssq[0:m, 2 * tt:2 * tt + 1]),
                      reads=[("xt", a), "ssq"], writes=["junk", ("ssq", tt)])
                S.act(lambda e, tt=tt, m=m: e.activation(out=ssq[0:m, 2 * tt + 1:2 * tt + 2], in_=ssq[0:m, 2 * tt:2 * tt + 1],
                                                        func=AF.Sqrt, bias=epsb[0:m, 0:1], scale=1.0 / D),
                      reads=[("ssq", tt), "eps"], writes=[("ssq2", tt)])
                S.dve(lambda e, tt=tt, m=m: e.reciprocal(out=ssq[0:m, 2 * tt + 1:2 * tt + 2], in_=ssq[0:m, 2 * tt + 1:2 * tt + 2]),
                      reads=[("ssq2", tt)], writes=[("ssq2", tt)])
                S.dve(lambda e, a=a, b2=b2, tt=tt, m=m: e.scalar_tensor_tensor(out=yo[b2][0:m, :], in0=xt[a][0:m, :],
                                                                              scalar=ssq[0:m, 2 * tt + 1:2 * tt + 2], in1=gf[0:m, :],
                                                                              op0=ALU.mult, op1=ALU.mult),
                      reads=[("xt", a), ("ssq2", tt), "gf"], writes=[("yo", b2)])
                dst = y_s if tt == 8 else y_p[tt]
                S.dma("pool", dst, yo[b2][0:m, :], reads=[("yo", b2)], writes=[("y", tt)])
            S.end_phase()
    return nc


def _constants():
    p = np.arange(128)
    same64 = (p[:, None] // 64) == (p[None, :] // 64)
    TIp = (same64 & (p[:, None] <= p[None, :])).astype(np.float32)
    Up = (same64 & (p[:, None] > p[None, :])).astype(np.float32)
    val = (p[:, None] < 64) & (p[None, :] < 64)
    same4 = ((p[:, None] // 4) == (p[None, :] // 4)) & val
    TIs = (same4 & (p[:, None] <= p[None, :])).astype(np.float32)
    Us = (same4 & (p[:, None] > p[None, :])).astype(np.float32)
    U128 = (p[:, None] > p[None, :]).astype(np.float32)
    cmat = np.stack([-TIp / 16.0, -Up / 16.0, -TIs / 16.0, -Us / 16.0, -U128 / 16.0], axis=1).astype(np.float32)
    ti4 = np.stack([np.repeat(TIp[:, None, :], 4, axis=1), np.repeat(TIs[:, None, :], 4, axis=1)], axis=1).astype(np.float32)
    negcol = np.full((128, 1), -1.0 / 16.0, np.float32)
    bm = ((p[:, None] // 4) == np.arange(16)[None, :]) & (p[:, None] < 64)
    bmask = bm.astype(np.float32)
    bsel16 = (-bmask / 16.0).astype(np.float32)
    slopes = np.exp2(-8.0 * np.arange(1, 17, dtype=np.float32) / 16.0).astype(np.float32)
    i = np.arange(128)[:, None]
    j = np.arange(256)[None, :]
    d = (i + 128 - j).astype(np.float32)
    valid = (d >= 0) & (d <= 128)
    bias_gen = np.where(valid[:, None, :], -slopes[None, :, None] * d[:, None, :], NEG).astype(np.float32)
    bias_first = bias_gen.copy()
    bias_first[:, :, 0:128] = NEG
    r = np.arange(16)
    g_ = r // 4
    t_ = r % 4
    jb = np.arange(128)
    bias_sb = np.zeros((16, 4, 128), np.float32)
    bias_sn = np.full((16, 4, 16, 64), NEG, np.float32)
    for kvh in range(4):
        sl = slopes[kvh * 4 + g_]
        dd = (t_[:, None] + 128 - jb[None, :]).astype(np.float32)
        ok = (dd >= 0) & (dd <= 128)
        bias_sb[:, kvh, :] = np.where(ok, -sl[:, None] * dd, NEG)
        for b in range(16):
            for tp in range(4):
                dn = (t_ - tp).astype(np.float32)
                okn = dn >= 0
                bias_sn[:, kvh, b, 4 * b + tp] = np.where(okn, -sl * dn, NEG)
    return dict(ident=np.eye(128, dtype=np.float32), cmat=cmat, ti4=ti4, negcol=negcol, bsel16=bsel16, bmask=bmask,
                bias_gen=bias_gen, bias_first=bias_first, bias_sb=bias_sb, bias_sn=bias_sn)


_NC_CACHE = {}


def kernel(x_prompt, x_sample, c_prompt, c_sample, state_gla, state_swa_k, state_swa_v,
           state_ffn_conv, w_ada, b_ada, g_norm, w_in, w_a_up, b_a, g_gla, swa_sinks,
           w_o, w_up, w_conv, b_conv, w_down, g_final):
    f32 = np.float32
    A = lambda a: np.ascontiguousarray(np.asarray(a, dtype=f32))
    x_prompt, x_sample, c_prompt, c_sample = A(x_prompt), A(x_sample), A(c_prompt), A(c_sample)
    state_gla, state_swa_k, state_swa_v, state_ffn_conv = A(state_gla), A(state_swa_k), A(state_swa_v), A(state_ffn_conv)
    w_ada, b_ada, g_norm, w_in, w_a_up, b_a = A(w_ada)[0], A(b_ada)[0], A(g_norm)[0], A(w_in)[0], A(w_a_up)[0], A(b_a)[0]
    g_gla, swa_sinks, w_o, w_up, w_conv, b_conv, w_down, g_final = (A(g_gla)[0], A(swa_sinks)[0], A(w_o)[0], A(w_up)[0],
                                                                    A(w_conv)[0], A(b_conv)[0], A(w_down)[0], A(g_final))
    C = _constants()
    rep = lambda v, n=128: np.ascontiguousarray(np.broadcast_to(v[None], (n,) + v.shape))
    shared = dict(
        w_ada=w_ada, bada_bc=rep(b_ada), gn_bc=rep(g_norm), w_in=w_in,
        w17=np.ascontiguousarray(np.concatenate([w_a_up, b_a[None, :]], axis=0)),
        ggla_bc=rep(g_gla), sink_bc=rep(swa_sinks),
        sink_s=np.ascontiguousarray(swa_sinks.reshape(4, 4)[:, np.arange(16) // 4].T),
        w_o=w_o, w_up=w_up,
        w_convT=np.ascontiguousarray(np.concatenate([w_conv, b_conv[None, :]], axis=0).T),
        w_down=w_down, gf_bc=rep(g_final),
        ident=C["ident"], cmat=C["cmat"], ti4=C["ti4"], negcol=C["negcol"], bsel16=C["bsel16"], bmask=C["bmask"],
        bias_sb=C["bias_sb"], bias_sn=C["bias_sn"],
    )
    xp = x_prompt[0]
    xpad = np.concatenate([np.zeros((7168, D), f32), xp], axis=0)
    in_maps = []
    for c in range(NCORES):
        start = 1024 * c
        xa = np.zeros((NT, 128, D), f32)
        xa[0:64] = xpad[start:start + 8192].reshape(64, 128, D)
        xa[64, 0:64] = x_sample[16 * c:16 * c + 16].reshape(64, D)
        pm = np.zeros((128, NT), f32)
        for t in range(64):
            g0 = start - 7168 + t * 128
            pm[:, t] = 1.0 if g0 >= 0 else 0.0
        pm[0:64, 64] = 1.0
        cT = np.zeros((128, KC, 256), f32)
        cT[:, :, 0:128] = c_prompt[0].reshape(KC, 128).T[:, :, None]
        cs = np.repeat(c_sample[16 * c:16 * c + 16], 4, axis=0)
        cT[:, :, 128:192] = cs.reshape(64, KC, 128).transpose(2, 1, 0)
        bias_p = np.stack([C["bias_first"] if c == 0 else C["bias_gen"], C["bias_gen"]], axis=1)
        sk = state_swa_k[0, 16 * c:16 * c + 16]
        sv = state_swa_v[0, 16 * c:16 * c + 16]
        m = dict(shared)
        m.update(
            xall=xa, cT=cT, st_gla=np.ascontiguousarray(state_gla[0, 16 * c:16 * c + 16]),
            st_k=np.ascontiguousarray(sk.reshape(16, 128, 512)), st_v=np.ascontiguousarray(sv.reshape(16, 128, 512)),
            st_kT=np.ascontiguousarray(sk.transpose(0, 2, 3, 1)),
            st_convT=np.ascontiguousarray(state_ffn_conv[0, 16 * c:16 * c + 16].reshape(32, F2).T),
            pmask=pm, hmask=np.full((128, 1), 0.0 if c == 0 else 1.0, f32), bias_p=np.ascontiguousarray(bias_p),
        )
        in_maps.append(m)
    if "nc" not in _NC_CACHE:
        _NC_CACHE["nc"] = build_program()
    nc = _NC_CACHE["nc"]
    res = run_bass_kernel_spmd(nc, in_maps, core_ids=list(range(NCORES)))
    R = res.results
    y_prompt = np.concatenate([R[c]["y_p"].reshape(1024, D) for c in range(NCORES)], axis=0)[None]
    y_sample = np.concatenate([R[c]["y_s"] for c in range(NCORES)], axis=0).reshape(128, 4, D)
    gla_p = R[7]["o_gla_p"].reshape(1, 1, 4, 256, 512)
    k_p = R[7]["o_k_p"].reshape(1, 1, 128, 4, 128)
    v_p = R[7]["o_v_p"].reshape(1, 1, 128, 4, 128)
    conv_p = np.ascontiguousarray(R[7]["o_convT"][:, 0:2].T).reshape(1, 1, 2, F2)
    gla_s = np.concatenate([R[c]["o_gla_s"] for c in range(NCORES)], axis=0)[None]
    k_s = np.concatenate([R[c]["o_k_s"] for c in range(NCORES)], axis=0).reshape(1, 128, 128, 4, 128)
    v_s = np.concatenate([R[c]["o_v_s"] for c in range(NCORES)], axis=0).reshape(1, 128, 128, 4, 128)
    conv_s = np.concatenate([R[c]["o_convT"][:, 2:34].T.reshape(16, 2, F2) for c in range(NCORES)], axis=0)[None]
    outs = (y_prompt, y_sample, gla_p, k_p, v_p, conv_p, gla_s, k_s, v_s, conv_s)
    return tuple(np.ascontiguousarray(o, dtype=f32) for o in outs)
```
